# Optimizing a Trainium2 kernel written in Bass

```python
import math
import jax, jax.numpy as jnp
from jax import lax
import numpy as np

D_MODEL = 2048
BATCH = 2
SEQ = 4096
DEPTH = 1
DEC_BATCH = 128
DEC_SEQ = 8
PAST_LEN = 8192
PAGE_SIZE = 128

N_META = 16
MIX_WIDTH = D_MODEL
ATT_WIDTH = MIX_WIDTH // 2
HEAD_DIM = 64
N_HEADS = ATT_WIDTH // HEAD_DIM
N_KV_HEADS = 4
GQA_GROUP = N_HEADS // N_KV_HEADS
KV_WIDTH = N_KV_HEADS * HEAD_DIM
WINDOW = 128
ATT_BLOCK = WINDOW
N_BUCKETS = 32
MAX_DISTANCE = 128
GLA_WIDTH = MIX_WIDTH - ATT_WIDTH
GLA_HEADS = 4
GLA_DV = GLA_WIDTH // GLA_HEADS
GLA_DK = GLA_DV // 2
GLA_KEY_WIDTH = GLA_HEADS * GLA_DK
GLA_RANK = 16
GLA_NORMALIZER = 16.0
GLA_CHUNK = 64
FRONT_PAD = ATT_BLOCK - N_META
EPS = 1e-6

IN_SIZES = (ATT_WIDTH, KV_WIDTH, KV_WIDTH, ATT_WIDTH,
            GLA_KEY_WIDTH, GLA_KEY_WIDTH, GLA_WIDTH, GLA_WIDTH, GLA_RANK)
IN_WIDTH = sum(IN_SIZES)
SPLIT_POINTS = tuple(int(c) for c in np.cumsum(IN_SIZES)[:-1])

kernel_name = 'hymba_swa_sink_gla_decode_step'


def rmsnorm(x, w):
    xf = x.astype(jnp.float32)
    var = jnp.mean(xf * xf, axis=-1, keepdims=True)
    return (xf * lax.rsqrt(var + EPS) * w.astype(jnp.float32)).astype(x.dtype)


def t5_bucket(dist):
    max_exact = N_BUCKETS // 2
    d = jnp.maximum(dist, 0)
    ratio = jnp.maximum(d, max_exact).astype(jnp.float32) / max_exact
    large = max_exact + (jnp.log(ratio) / math.log(MAX_DISTANCE / max_exact)
                         * (N_BUCKETS - max_exact)).astype(jnp.int32)
    large = jnp.minimum(large, N_BUCKETS - 1)
    return jnp.where(d < max_exact, d, large)


def sink_softmax(s, mask, sink):
    s = jnp.where(mask, s, -jnp.inf)
    m = jnp.maximum(jnp.max(s, axis=-1, keepdims=True), sink)
    e = jnp.exp(s - m)
    return e / (jnp.sum(e, axis=-1, keepdims=True) + jnp.exp(sink - m))


def project(h, w_in, w_a2, b_a):
    q_a, k_a, v_a, gate_a, q_g, k_g, v_g, gate_g, z_g = jnp.split(h @ w_in, SPLIT_POINTS, axis=-1)
    lead = h.shape[:-1]
    q_a = q_a.reshape(*lead, N_KV_HEADS, GQA_GROUP, HEAD_DIM)
    k_a = k_a.reshape(*lead, N_KV_HEADS, HEAD_DIM)
    v_a = v_a.reshape(*lead, N_KV_HEADS, HEAD_DIM)
    q_g = q_g.reshape(*lead, GLA_HEADS, GLA_DK) * (GLA_DK ** -0.5)
    k_g = k_g.reshape(*lead, GLA_HEADS, GLA_DK)
    v_g = v_g.reshape(*lead, GLA_HEADS, GLA_DV)
    log_f = jax.nn.log_sigmoid((z_g @ w_a2 + b_a).astype(jnp.float32)) / GLA_NORMALIZER
    log_f = log_f.reshape(*lead, GLA_HEADS, GLA_DK)
    return q_a, k_a, v_a, gate_a, q_g, k_g, v_g, gate_g, log_f


def merge(att_o, gate_a, gla_o, gate_g, gla_norm, w_out):
    lead = att_o.shape[:-1]
    gla_o = rmsnorm(gla_o, gla_norm).reshape(*lead, GLA_WIDTH).astype(gate_g.dtype)
    mixed = jnp.concatenate([att_o * jax.nn.silu(gate_a), gla_o * jax.nn.silu(gate_g)], axis=-1)
    return mixed @ w_out


def swa_prompt(q, k, v, sinks, rel_bias):
    B, Lp = q.shape[:2]
    nb = Lp // ATT_BLOCK
    qb = q.reshape(B, nb, ATT_BLOCK, N_KV_HEADS, GQA_GROUP, HEAD_DIM)
    kb = k.reshape(B, nb, ATT_BLOCK, N_KV_HEADS, HEAD_DIM)
    vb = v.reshape(B, nb, ATT_BLOCK, N_KV_HEADS, HEAD_DIM)

    def with_prev(t):
        prev = jnp.concatenate([jnp.zeros_like(t[:, :1]), t[:, :-1]], axis=1)
        return jnp.concatenate([prev, t], axis=2)

    kk, vv = with_prev(kb), with_prev(vb)
    qpos = (jnp.arange(Lp) - FRONT_PAD).reshape(nb, ATT_BLOCK)
    kpos = (jnp.arange(nb)[:, None] - 1) * ATT_BLOCK + jnp.arange(2 * ATT_BLOCK)[None, :] - FRONT_PAD
    dist = qpos[:, :, None] - kpos[:, None, :]
    mask = (dist >= 0) & (dist < WINDOW) & (kpos[:, None, :] >= 0)
    bias = rel_bias[t5_bucket(dist)].astype(jnp.float32)
    bias = bias.reshape(nb, ATT_BLOCK, 2 * ATT_BLOCK, N_KV_HEADS, GQA_GROUP).transpose(0, 3, 4, 1, 2)
    s = jnp.einsum('bnqhgd,bnshd->bnhgqs', qb, kk).astype(jnp.float32) * (HEAD_DIM ** -0.5) + bias
    p = sink_softmax(s, mask[:, None, None], sinks.astype(jnp.float32).reshape(N_KV_HEADS, GQA_GROUP, 1, 1))
    o = jnp.einsum('bnhgqs,bnshd->bnqhgd', p.astype(v.dtype), vv)
    return o.reshape(B, Lp, ATT_WIDTH)


def swa_sample(q, k, v, k_buf, v_buf, sinks, rel_bias):
    DB, T = q.shape[:2]
    W = k_buf.shape[1]
    kk = jnp.concatenate([k_buf, k], axis=1)
    vv = jnp.concatenate([v_buf, v], axis=1)
    qpos = PAST_LEN + jnp.arange(T)
    kpos = PAST_LEN - W + jnp.arange(W + T)
    dist = qpos[:, None] - kpos[None, :]
    mask = (dist >= 0) & (dist < WINDOW)
    bias = rel_bias[t5_bucket(dist)].astype(jnp.float32)
    bias = bias.reshape(T, W + T, N_KV_HEADS, GQA_GROUP).transpose(2, 3, 0, 1)
    s = jnp.einsum('bqhgd,bshd->bhgqs', q, kk).astype(jnp.float32) * (HEAD_DIM ** -0.5) + bias
    p = sink_softmax(s, mask, sinks.astype(jnp.float32).reshape(N_KV_HEADS, GQA_GROUP, 1, 1))
    o = jnp.einsum('bhgqs,bshd->bqhgd', p.astype(v.dtype), vv)
    return o.reshape(DB, T, ATT_WIDTH), kk[:, -W:], vv[:, -W:]


def gla_chunk(q, k, v, g, s0):
    C = q.shape[2]
    b = jnp.cumsum(g, axis=2)
    causal = jnp.tril(jnp.ones((C, C), dtype=bool))
    diff = b[:, :, :, None, :] - b[:, :, None, :, :]
    decay = jnp.exp(jnp.where(causal[:, :, None], diff, -jnp.inf))
    scores = jnp.einsum('bhtk,bhtsk,bhsk->bhts', q, decay, k)
    o = (jnp.einsum('bhts,bhsv->bhtv', scores, v)
         + jnp.einsum('bhtk,bhkv->bhtv', q * jnp.exp(b), s0))
    b_last = b[:, :, -1:, :]
    s_new = (jnp.exp(b_last[:, :, 0, :, None]) * s0
             + jnp.einsum('bhsk,bhsv->bhkv', k * jnp.exp(b_last - b), v))
    return o, s_new


def gla_prompt(q, k, v, log_f):
    B, Lp = q.shape[:2]
    nc = Lp // GLA_CHUNK
    valid = (jnp.arange(Lp) >= FRONT_PAD)[None, :, None, None]
    k = jnp.where(valid, k, 0)
    log_f = jnp.where(valid, log_f, 0.0)

    def to_chunks(t):
        return t.astype(jnp.float32).reshape(B, nc, GLA_CHUNK, GLA_HEADS, -1).transpose(1, 0, 3, 2, 4)

    def step(s, inp):
        qc, kc, vc, gc = inp
        o, s = gla_chunk(qc, kc, vc, gc, s)
        return s, o

    s0 = jnp.zeros((B, GLA_HEADS, GLA_DK, GLA_DV), jnp.float32)
    s_fin, o = lax.scan(step, s0, (to_chunks(q), to_chunks(k), to_chunks(v), to_chunks(log_f)))
    o = o.transpose(1, 0, 3, 2, 4).reshape(B, Lp, GLA_HEADS, GLA_DV)
    return o, s_fin


def setup_inputs(seed: int = 0) -> dict:
    key = jax.random.key(seed)
    ks = jax.random.split(key, 16)
    w_cache = min(WINDOW, PAST_LEN)
    nrm = jax.random.normal
    return {
        'x_prompt': nrm(ks[0], (BATCH, SEQ, D_MODEL), jnp.float32),
        'x_sample': nrm(ks[1], (DEC_BATCH, DEC_SEQ, D_MODEL), jnp.float32),
        'cache_k_win': nrm(ks[2], (DEPTH, DEC_BATCH, w_cache, N_KV_HEADS, HEAD_DIM), jnp.float32),
        'cache_v_win': nrm(ks[3], (DEPTH, DEC_BATCH, w_cache, N_KV_HEADS, HEAD_DIM), jnp.float32),
        'state_gla': 0.1 * nrm(ks[4], (DEPTH, DEC_BATCH, GLA_HEADS, GLA_DK, GLA_DV), jnp.float32),
        'meta_tokens': nrm(ks[5], (N_META, D_MODEL), jnp.float32),
        'rel_bias': 0.5 * nrm(ks[6], (N_BUCKETS, N_HEADS), jnp.float32),
        'norm_pre': 1.0 + 0.1 * nrm(ks[7], (DEPTH, D_MODEL), jnp.float32),
        'norm_post': 1.0 + 0.1 * nrm(ks[8], (DEPTH, D_MODEL), jnp.float32),
        'w_in': nrm(ks[9], (DEPTH, D_MODEL, IN_WIDTH), jnp.float32) * D_MODEL ** -0.5,
        'w_a2': nrm(ks[10], (DEPTH, GLA_RANK, GLA_KEY_WIDTH), jnp.float32) * GLA_RANK ** -0.5,
        'b_a': 0.1 * nrm(ks[11], (DEPTH, GLA_KEY_WIDTH), jnp.float32),
        'attn_sinks': nrm(ks[12], (DEPTH, N_HEADS), jnp.float32),
        'gla_norm': 1.0 + 0.1 * nrm(ks[13], (DEPTH, GLA_DV), jnp.float32),
        'w_out': nrm(ks[14], (DEPTH, MIX_WIDTH, D_MODEL), jnp.float32) * MIX_WIDTH ** -0.5,
    }


def reference(x_prompt, x_sample, cache_k_win, cache_v_win, state_gla, meta_tokens, rel_bias,
              norm_pre, norm_post, w_in, w_a2, b_a, attn_sinks, gla_norm, w_out):
    B = x_prompt.shape[0]
    meta = jnp.broadcast_to(meta_tokens[None].astype(x_prompt.dtype), (B, N_META, D_MODEL))
    xp = jnp.concatenate([meta, x_prompt], axis=1)
    xs = x_sample
    kwp, vwp, sgp, kws, vws, sgs = [], [], [], [], [], []
    for l in range(DEPTH):
        h = rmsnorm(xp, norm_pre[l])
        h = jnp.pad(h, ((0, 0), (FRONT_PAD, 0), (0, 0)))
        q_a, k_a, v_a, gate_a, q_g, k_g, v_g, gate_g, log_f = project(h, w_in[l], w_a2[l], b_a[l])
        att_o = swa_prompt(q_a, k_a, v_a, attn_sinks[l], rel_bias)
        gla_o, s_fin = gla_prompt(q_g, k_g, v_g, log_f)
        o = merge(att_o, gate_a, gla_o, gate_g, gla_norm[l], w_out[l])[:, FRONT_PAD:]
        xp = xp + rmsnorm(o, norm_post[l])
        kwp.append(k_a[:, -WINDOW:])
        vwp.append(v_a[:, -WINDOW:])
        sgp.append(s_fin)
        h = rmsnorm(xs, norm_pre[l])
        q_a, k_a, v_a, gate_a, q_g, k_g, v_g, gate_g, log_f = project(h, w_in[l], w_a2[l], b_a[l])
        att_o, k_new, v_new = swa_sample(q_a, k_a, v_a, cache_k_win[l], cache_v_win[l], attn_sinks[l], rel_bias)
        to_bhtd = lambda t: t.astype(jnp.float32).transpose(0, 2, 1, 3)
        gla_o, s_new = gla_chunk(to_bhtd(q_g), to_bhtd(k_g), to_bhtd(v_g), to_bhtd(log_f),
                                 state_gla[l].astype(jnp.float32))
        gla_o = gla_o.transpose(0, 2, 1, 3)
        o = merge(att_o, gate_a, gla_o, gate_g, gla_norm[l], w_out[l])
        xs = xs + rmsnorm(o, norm_post[l])
        kws.append(k_new)
        vws.append(v_new)
        sgs.append(s_new)
    y_prompt = xp[:, N_META:]
    y_sample = xs
    return (y_prompt, y_sample, jnp.stack(kwp), jnp.stack(vwp), jnp.stack(sgp),
            jnp.stack(kws), jnp.stack(vws), jnp.stack(sgs))
```

```python
import math
from contextlib import ExitStack
import numpy as np
import concourse.bass as bass
import concourse.mybir as mybir
from concourse.bass_utils import run_bass_kernel_spmd

F32 = mybir.dt.float32
BF16 = mybir.dt.bfloat16
AF = mybir.ActivationFunctionType
ALU = mybir.AluOpType

D = 2048
INW = 5648
Q_OFF, K_OFF, V_OFF, GA_OFF, QG_OFF, KG_OFF, VG_OFF, GG_OFF, Z_OFF = 0, 1024, 1280, 1536, 2560, 3072, 3584, 4608, 5632
NSLOT = 34
NPRE = 25
NMB = 10
NEG = -30000.0
EPS = 1e-6

C_ID, C_U64, C_U8, C_CI64, C_CI8, C_CMK64, C_CMK8, C_OH, C_NV, C_HM = (
    0, 128, 256, 384, 386, 402, 658, 2706, 3089, 3123)
NCST = 3124
C_US, C_ONE, NCB = 2752, 2880, 2888


class Prog:
    ENG = ("pe", "act", "dve", "pool", "sp")

    def __init__(self, nc, stack):
        self.nc = nc
        self.stack = stack
        self.sems = {e: stack.enter_context(nc.semaphore("s_" + e)) for e in self.ENG}
        self.cnt = {e: 0 for e in self.ENG}
        self.dsems = []
        self._reset()

    def _reset(self):
        self.q = {e: [] for e in self.ENG}
        self.regs = {}
        self.children = {}

    def dma_sem(self, name):
        d = {"h": self.stack.enter_context(self.nc.semaphore(name)), "v": 0}
        self.dsems.append(d)
        return d

    def _reg(self, k):
        r = self.regs.get(k)
        if r is None:
            r = self.regs[k] = {"w": {}, "r": {}}
            for i in range(1, len(k)):
                self.children.setdefault(k[:i], set()).add(k)
        return r

    def _conf(self, k):
        out = []
        for i in range(1, len(k) + 1):
            r = self.regs.get(k[:i])
            if r is not None:
                out.append(r)
        for c in self.children.get(k, ()):
            out.append(self.regs[c])
        return out

    @staticmethod
    def _merge(dst, src):
        for sk, (h, v) in src.items():
            if sk not in dst or dst[sk][1] < v:
                dst[sk] = (h, v)

    def _deps(self, reads, writes):
        need = {}
        for k in reads:
            for c in self._conf(k):
                self._merge(need, c["w"])
        for k in writes:
            for c in self._conf(k):
                self._merge(need, c["w"])
                self._merge(need, c["r"])
        return need

    def _record(self, tok, reads, writes):
        sk, h, v = tok
        for k in reads:
            self._merge(self._reg(k)["r"], {sk: (h, v)})
        for k in writes:
            r = self._reg(k)
            for c in self._conf(k):
                if c is not r:
                    self._merge(c["w"], {sk: (h, v)})
            r["w"] = {sk: (h, v)}
            r["r"] = {}

    @staticmethod
    def _norm(k):
        k = tuple(k) if isinstance(k, (tuple, list)) else (k,)
        return k[:2] if k[0] == "ps" else k

    def op(self, eng, fn, reads=(), writes=(), sig=True):
        reads = [self._norm(k) for k in reads]
        writes = [self._norm(k) for k in writes]
        writes = writes + [k for k in reads if k[0] == "ps" and k not in writes]
        reads = [k for k in reads if k[0] != "ps"]
        need = self._deps(reads, writes)
        if sig:
            self.cnt[eng] += 1
            self._record((eng, self.sems[eng], self.cnt[eng]), reads, writes)
        self.q[eng].append((need, fn, sig, None))

    def dma(self, eng, fn, sem, reads=(), writes=()):
        reads = [self._norm(k) for k in reads]
        writes = [self._norm(k) for k in writes]
        need = self._deps(reads, writes)
        if sem["v"] > 0:
            self._merge(need, {id(sem): (sem["h"], sem["v"])})
        sem["v"] += 16
        self._record((id(sem), sem["h"], sem["v"]), reads, writes)
        self.q[eng].append((need, fn, False, sem))

    def flush(self):
        need = {id(d): (d["h"], d["v"]) for d in self.dsems if d["v"] > 0}
        for e in self.ENG:
            self.q[e].append((dict(need), lambda h: h.nop(), False, None))
        nc = self.nc
        with nc.Block() as block:
            handles = {"pe": block.tensor, "act": block.scalar, "dve": block.vector,
                       "pool": block.gpsimd, "sp": block.sync}
            for e in self.ENG:
                def body(h, lst=self.q[e], e=e):
                    waited = {}
                    for need_, fn, sig, dsem in lst:
                        for sk, (sh, v) in need_.items():
                            if waited.get(sk, 0) >= v:
                                continue
                            waited[sk] = v
                            h.wait_ge(sh, v)
                        ins = fn(h)
                        if sig:
                            ins.then_inc(self.sems[e], 1)
                        if dsem is not None:
                            ins.then_inc(dsem["h"], 16)
                handles[e](body)
        self._reset()


def t5_bucket_np(dist):
    d = np.maximum(dist, 0)
    ratio = np.maximum(d, 16).astype(np.float32) / np.float32(16)
    large = 16 + (np.log(ratio) / np.float32(math.log(128 / 16)) * 16).astype(np.int32)
    large = np.minimum(large, 31)
    return np.where(d < 16, d, large)


def make_consts():
    c = np.zeros((128, NCST), np.float32)
    i = np.arange(128)
    c[:, C_ID:C_ID + 128] = np.eye(128)
    for off, C in ((C_U64, 64), (C_U8, 8)):
        c[:, off:off + 128] = ((i[:, None] <= i[None, :]) & (i[:, None] // C == i[None, :] // C))
    c[:, C_CI64:C_CI64 + 2] = (i[:, None] // 64 == np.arange(2)[None, :])
    c[:, C_CI8:C_CI8 + 16] = (i[:, None] // 8 == np.arange(16)[None, :])
    c[:, C_CMK64:C_CMK64 + 256] = np.tile((np.arange(2)[:, None] == i[None, :] // 64).reshape(1, 256), (128, 1))
    c[:, C_CMK8:C_CMK8 + 2048] = np.tile((np.arange(16)[:, None] == i[None, :] // 8).reshape(1, 2048), (128, 1))
    m = np.arange(383)
    dist = m - 127
    valid = (dist >= 0) & (dist < 128)
    bk = t5_bucket_np(dist)
    oh = np.zeros((33, 383), np.float32)
    oh[bk[valid], m[valid]] = 1.0
    oh[32, ~valid] = NEG
    c[:33, C_OH:C_OH + 383] = oh
    return c


_STOP = None
_SKIP = set()


def build():
    nc = bass.Bass("TRN2", target_bir_lowering=False)
    dt_in = lambda n, s: nc.dram_tensor(n, s, F32, kind="ExternalInput").ap()
    dt_out = lambda n, s: nc.dram_tensor(n, s, F32, kind="ExternalOutput").ap()
    xp = dt_in("xp", [NSLOT * 128, D])
    w_in = dt_in("w_in", [D, INW])
    w_out = dt_in("w_out", [D, D])
    ck = dt_in("ck", [16, 128, 256])
    cv = dt_in("cv", [16, 128, 256])
    sg = dt_in("sg", [16, 4, 128, 256])
    cstS_d = dt_in("cstS", [128, 163])
    cstB_d = dt_in("cstB", [128, NCB])
    ohm_d = dt_in("ohm", [33, 383])
    relb = dt_in("relb", [32, 16])
    npre = dt_in("npre", [1, D])
    npost = dt_in("npost", [1, D])
    wa2 = dt_in("wa2", [16, 512])
    ba = dt_in("ba", [1, 512])
    sinks = dt_in("sinks", [1, 16])
    gnorm = dt_in("gnorm", [1, 256])
    y_main = dt_out("y_main", [8 * 128, D])
    y_samp = dt_out("y_samp", [128, D])
    kwin = dt_out("kwin", [128, 256])
    vwin = dt_out("vwin", [128, 256])
    glast = dt_out("glast", [4, 128, 256])
    ks_out = dt_out("ks_out", [16, 128, 256])
    vs_out = dt_out("vs_out", [16, 128, 256])
    gs_out = dt_out("gs_out", [16, 4, 128, 256])
    fscr = nc.dram_tensor("fscr", [16, 128, 383], F32, kind="Internal").ap()

    def bcast_rows(ap2d, n):
        return bass.AP(ap2d.tensor, 0, [[0, 128], [1, n]])

    with ExitStack() as top:
        P = Prog(nc, top)
        sb = lambda st, name, shape, dt: st.enter_context(nc.sbuf_tensor(name, shape, dt, align_bytes=64))
        cstS = sb(top, "cstS_sb", [128, 163], F32)
        cstb = sb(top, "cstb", [128, NCB], BF16)
        esink = sb(top, "esink", [128, 16], F32)
        epst = sb(top, "epst", [128, 1], F32)
        wa2a = sb(top, "wa2a", [17, 512], BF16)
        Sst = sb(top, "Sst", [128, 4, 256], F32)
        Sbf = sb(top, "Sbf", [128, 4, 256], BF16)
        rstd_all = sb(top, "rstd_all", [128, NSLOT], F32)
        small = sb(top, "small", [128, 64], F32)
        zT = sb(top, "zT", [17, NMB * 128], BF16)
        xT_main = sb(top, "xT_main", [128, 16, NMB * 128], BF16)
        mixed = sb(top, "mixed", [128, NMB, D], BF16)
        kout = sb(top, "kout", [128, 2, 256], F32)
        vout = sb(top, "vout", [128, 2, 256], F32)
        pb = [top.enter_context(nc.psum_tensor("pb%d" % i, [128, 512], F32)) for i in range(8)]
        pbb = [t[:].bitcast(BF16) for t in pb]
        ident_b = cstb[:, C_ID:C_ID + 128]
        ident_f = cstS[:, 0:128]
        NVC = lambda slot: cstS[:, 128 + slot:129 + slot]
        HMC = cstS[:, 162:163]
        s_ld = [P.dma_sem("s_ld%d" % i) for i in range(2)]
        s_c = P.dma_sem("s_c")
        s_cp = P.dma_sem("s_cp")
        s_w = [P.dma_sem("s_w%d" % i) for i in range(2)]
        s_o = P.dma_sem("s_o")
        s_mf = [P.dma_sem("s_mf%d" % i) for i in range(4)]
        s_mb = [P.dma_sem("s_mb%d" % i) for i in range(4)]
        s_mo = [P.dma_sem("s_mo%d" % i) for i in range(4)]

        P.dma("sp", lambda e: e.dma_start(out=cstS[:], in_=cstS_d[:, :]), s_c, writes=["cstf"])
        P.dma("pool", lambda e: e.dma_start(out=cstb[:], in_=cstB_d[:, :]), s_w[0], writes=["cstb"])
        if "bcast" not in _SKIP:
            P.dma("sp", lambda e: e.dma_start(out=esink[:], in_=bcast_rows(sinks, 16)), s_c, writes=["esink"])
        if "wa2" not in _SKIP:
            P.dma("pool", lambda e: e.dma_start(out=wa2a[0:16, :], in_=wa2[:, :]), s_cp, writes=["wa2a"])
            P.dma("pool", lambda e: e.dma_start(out=wa2a[16:17, :], in_=ba[:, :]), s_cp, writes=["wa2a"])
        P.op("pool", lambda e: e.memset(epst[:], EPS), writes=["epst"])
        P.op("pool", lambda e: e.memset(Sst[:], 0.0), writes=["S"])
        P.op("pool", lambda e: e.memset(Sbf[:], 0.0), writes=["Sbf"])
        P.op("pool", lambda e: e.memset(zT[:], 1.0), writes=["zT"])
        P.op("act", lambda e: e.activation(out=esink[:], in_=esink[:], func=AF.Exp), reads=["esink"], writes=["esink"])
        with ExitStack() as st:
            rba = sb(st, "rba", [33, 16], F32)
            fes = sb(st, "fes", [16, 383], F32)
            ohm = sb(st, "ohm_sb", [33, 383], F32)
            P.dma("sp", lambda e: e.dma_start(out=ohm[:], in_=ohm_d[:, :]), s_c, writes=["ohm"])
            P.op("pool", lambda e: e.memset(rba[:], 1.0), writes=["rba"])
            P.dma("sp", lambda e: e.dma_start(out=rba[0:32, :], in_=relb[:, :]), s_c, reads=[], writes=["rba"])
            if "f32mm" not in _SKIP:
                P.op("pe", lambda e: e.matmul(pb[0][0:16, 0:383], lhsT=rba[0:33, :], rhs=ohm[:, :],
                                             start=True, stop=True), reads=["rba", "ohm"], writes=[("ps", 0)])
                P.op("dve", lambda e: e.tensor_copy(out=fes[:], in_=pb[0][0:16, 0:383]), reads=[("ps", 0)], writes=["fes"])
            if "fscr" not in _SKIP:
                P.dma("sp", lambda e: e.dma_start(out=fscr[:, :, :], in_=fes[:].unsqueeze(1).broadcast_to([16, 128, 383])),
                      s_c, reads=["fes"], writes=["fscr"])
            P.flush()
            if _STOP == 0:
                return nc

        cnt = {"x": 0, "ev": 0}

        def xprep(slot, dst, dkey, par=0):
            i = cnt["x"] % 2
            cnt["x"] += 1
            P.dma("sp", lambda e: e.dma_start(out=xs[i][:], in_=xp[slot * 128:(slot + 1) * 128, :]), s_ld[i], writes=[("xs", i)])
            P.op("act", lambda e: e.activation(out=junk[:], in_=xs[i][:], func=AF.Square, accum_out=small[:, 2 * par:2 * par + 1]),
                 reads=[("xs", i)], writes=["junk", ("small", 2 * par)])
            P.op("act", lambda e: e.activation(out=small[:, 2 * par + 1:2 * par + 2], in_=small[:, 2 * par:2 * par + 1], func=AF.Ln, bias=epst[:, 0:1], scale=1.0 / D),
                 reads=[("small", 2 * par), "epst"], writes=[("small", 2 * par + 1)])
            P.op("act", lambda e: e.activation(out=rstd_all[:, slot:slot + 1], in_=small[:, 2 * par + 1:2 * par + 2], func=AF.Exp, scale=-0.5),
                 reads=[("small", 2 * par + 1)], writes=[("rstd", slot)])
            P.op("dve", lambda e: e.scalar_tensor_tensor(out=xb[par][:], in0=xs[i][:], scalar=rstd_all[:, slot:slot + 1], in1=npre_b[:],
                                                         op0=ALU.mult, op1=ALU.mult),
                 reads=[("xs", i), ("rstd", slot), "npre_b"], writes=[("xb", par)])
            for half in range(2):
                for j in range(8):
                    kc = half * 8 + j
                    P.op("pe", lambda e, kc=kc, j=j: e.transpose(out=pbb[2][:, j * 128:(j + 1) * 128],
                                                                in_=xb[par][:, kc * 128:(kc + 1) * 128], identity=ident_b),
                         reads=[("xb", par), "cstb"], writes=[("ps", 2)], sig=(j == 7))
                src = pbb[2][:, :].rearrange("p (a b) -> p a b", b=128)
                if half == 0:
                    P.op("act", lambda e, half=half, src=src: e.activation(out=dst[:, half * 8:(half + 1) * 8, :], in_=src, func=AF.Copy),
                         reads=[("ps", 2)], writes=[dkey])
                else:
                    P.op("dve", lambda e, half=half, src=src: e.tensor_copy(out=dst[:, half * 8:(half + 1) * 8, :], in_=src),
                         reads=[("ps", 2)], writes=[dkey])

        def wload(dst, dkey, segs, sem):
            off = 0
            for (c0, n) in segs:
                P.dma("pool", lambda e, c0=c0, n=n, off=off: e.dma_start(
                    out=dst[:, :, off:off + n], in_=w_in[:, c0:c0 + n].rearrange("(kc p) n -> p kc n", p=128)),
                    sem, writes=[dkey])
                off += n

        def mm16(out_ap, okey, lhs_fn, rhs_fn, rkeys):
            for kc in range(16):
                P.op("pe", lambda e, kc=kc: e.matmul(out_ap, lhsT=lhs_fn(kc), rhs=rhs_fn(kc), start=(kc == 0), stop=(kc == 15)),
                     reads=rkeys, writes=[okey], sig=(kc == 15))

        def evac(fn_act, fn_dve, reads, writes):
            cnt["ev"] += 1
            if cnt["ev"] % 2:
                P.op("act", fn_act, reads=reads, writes=writes)
            else:
                P.op("dve", fn_dve, reads=reads, writes=writes)

        def copy_evac(dst, src, reads, writes, scale=None):
            if scale is None:
                evac(lambda e: e.activation(out=dst, in_=src, func=AF.Copy), lambda e: e.tensor_copy(out=dst, in_=src), reads, writes)
            else:
                evac(lambda e: e.mul(dst, src, scale),
                     lambda e: e.tensor_scalar(out=dst, in0=src, scalar1=scale, scalar2=None, op0=ALU.mult), reads, writes)

        pcnt = {"b": 0}

        def nbank():
            pcnt["b"] ^= 1
            return pcnt["b"]

        def gla_block(T, heads, zT_ap, zkey, ktm, ktm_key, vtm, vtm_key, nvcol, C,
                      qT, kT, fkeys, sgg, sgg_key, mixed_dst, mixed_key, sample=False, stage="all", par=0):
            nch = 128 // C
            Uo, CIo, CMKo = (C_U64, C_CI64, C_CMK64) if C == 64 else (C_U8, C_CI8, C_CMK8)
            U = cstb[:, Uo:Uo + 128]
            CI = cstb[:, CIo:CIo + nch]
            CMK = cstb[:, CMKo:CMKo + nch * 128].rearrange("p (c t) -> p c t", t=128)
            h0 = heads[0]
            kpx, qpx, atm, el = T["kpx"][par], T["qpx"][par], T["atm"][par], T["el"][par]
            if stage in ("all", 1):
                P.op("pe", lambda e: e.matmul(pb[3][:, 0:256], lhsT=zT_ap, rhs=wa2a[0:17, h0 * 128:h0 * 128 + 256], start=True, stop=True),
                     reads=[zkey, "wa2a"], writes=[("ps", 3)])
                P.op("act", lambda e: e.activation(out=T["lf"][:], in_=pb[3][:, 0:256], func=AF.Exp, scale=-1.0),
                     reads=[("ps", 3)], writes=["lf"])
                P.op("act", lambda e: e.activation(out=T["lf"][:], in_=T["lf"][:], func=AF.Ln, bias=1.0), reads=["lf"], writes=["lf"])
                P.op("dve", lambda e: e.tensor_scalar(out=T["lfb"][:], in0=T["lf"][:], scalar1=nvcol, scalar2=None, op0=ALU.mult),
                     reads=["lf", "cstf"], writes=["lfb"])
                for hh in range(2):
                    P.op("pe", lambda e, hh=hh: e.matmul(pb[4][:, 256 + hh * 16:256 + hh * 16 + nch], lhsT=T["lfb"][:, hh * 128:(hh + 1) * 128],
                                                        rhs=CI, start=True, stop=True),
                         reads=["lfb", "cstb"], writes=[("ps", 4, "bl")], sig=(hh == 1))
                P.op("act", lambda e: e.activation(out=el[:], in_=pb[4][:, 256:288], func=AF.Exp), reads=[("ps", 4, "bl")], writes=[("el", par)])
                for hh in range(2):
                    P.op("pe", lambda e, hh=hh: e.matmul(pb[4][:, hh * 128:(hh + 1) * 128], lhsT=T["lfb"][:, hh * 128:(hh + 1) * 128],
                                                        rhs=U, start=True, stop=True),
                         reads=["lfb", "cstb"], writes=[("ps", 4, "bf")], sig=(hh == 1))
                P.op("act", lambda e: e.activation(out=T["efm"][:], in_=pb[4][:, 0:256], func=AF.Exp), reads=[("ps", 4, "bf")], writes=["efm"])
                P.op("act", lambda e: e.activation(out=T["eifm"][:], in_=pb[4][:, 0:256], func=AF.Exp, scale=-1.0),
                     reads=[("ps", 4, "bf")], writes=["eifm"])
                for hh in range(2):
                    P.op("dve", lambda e, hh=hh: e.scalar_tensor_tensor(out=T["qpT"][:, hh, :], in0=qT[hh], scalar=128.0 ** -0.5,
                                                                       in1=T["efm"][:, hh * 128:(hh + 1) * 128], op0=ALU.mult, op1=ALU.mult),
                         reads=list(fkeys) + ["efm"], writes=[("qpT", hh)])
                    P.op("pool", lambda e, hh=hh: e.tensor_tensor(out=T["kpT"][:, hh, :], in0=kT[hh], in1=T["eifm"][:, hh * 128:(hh + 1) * 128],
                                                                 op=ALU.mult),
                         reads=list(fkeys) + ["eifm"], writes=[("kpT", hh)])
                for hh in range(2):
                    P.op("pe", lambda e, hh=hh: e.transpose(out=pbb[3][:, hh * 128:(hh + 1) * 128], in_=T["kpT"][:, hh, :], identity=ident_b),
                         reads=[("kpT", hh), "cstb"], writes=[("ps", 3)], sig=(hh == 1))
                P.op("act", lambda e: e.activation(out=T["kp"][:], in_=pbb[3][:, 0:256], func=AF.Copy), reads=[("ps", 3)], writes=["kp"])
                for hh in range(2):
                    P.op("pool", lambda e, hh=hh: e.tensor_tensor(
                        out=kpx[:, hh, 0:nch, :], in0=T["kp"][:, hh * 128:(hh + 1) * 128].unsqueeze(1).broadcast_to([128, nch, 128]),
                        in1=CI.unsqueeze(2).broadcast_to([128, nch, 128]), op=ALU.mult),
                        reads=["kp", "cstb"], writes=[("kpx", par, hh)])
                    abk = 5 if hh == 0 else 2
                    P.op("pe", lambda e, hh=hh, abk=abk: e.matmul(pb[abk][:, 0:128], lhsT=T["kpT"][:, hh, :], rhs=T["qpT"][:, hh, :],
                                                                 start=True, stop=True),
                         reads=[("kpT", hh), ("qpT", hh)], writes=[("ps", abk)])
                    P.op("dve", lambda e, hh=hh, abk=abk: e.tensor_tensor(out=atm[:, hh, :], in0=pb[abk][:, 0:128], in1=U, op=ALU.mult),
                         reads=[("ps", abk), "cstb"], writes=[("atm", par, hh)])
                    P.op("dve", lambda e, hh=hh: e.tensor_tensor(
                        out=qpx[:, hh, 0:nch, :], in0=T["qpT"][:, hh, :].unsqueeze(1).broadcast_to([128, nch, 128]), in1=CMK, op=ALU.mult),
                        reads=[("qpT", hh), "cstb"], writes=[("qpx", par, hh)])
            if stage not in ("all", 2):
                return
            for hh in range(2):
                h = heads[hh]
                obk = 6 if hh == 0 else 0
                tbk = 7 if hh == 0 else 1
                ob = pb[obk][:, 0:256]
                P.op("pe", lambda e, hh=hh, ob=ob: e.matmul(ob, lhsT=atm[:, hh, :], rhs=vtm[:, hh * 256:(hh + 1) * 256], start=True, stop=False),
                     reads=[("atm", par, hh), vtm_key], writes=[("ps", obk)], sig=False)

                def s_load(c, h=h, hh=hh):
                    i = (c + hh) % 4
                    s0f, s0b = T["s0f"][i], T["s0b"][i]
                    P.dma("sp", lambda e, c=c, h=h, s0f=s0f: e.dma_start(out=s0f[:], in_=sg[c, h, :, :]), s_mf[i], writes=[("s0f", i)])
                    P.dma("pool", lambda e, c=c, h=h, s0b=s0b: e.dma_start(out=s0b[:], in_=sg[c, h, :, :]), s_mb[i], writes=[("s0b", i)])
                for c in range(nch):
                    if sample:
                        if c == 0:
                            for c2 in range(3):
                                s_load(c2)
                        if c + 3 < nch:
                            s_load(c + 3)
                        i = (c + hh) % 4
                        s0f, s0b = T["s0f"][i], T["s0b"][i]
                        Sf, Sb, skf, skb = s0f[:], s0b[:], ("s0f", i), ("s0b", i)
                    else:
                        Sf, Sb, skf, skb = Sst[:, h, :], Sbf[:, h, :], ("S", h), ("Sbf", h)
                    P.op("pe", lambda e, hh=hh, c=c, Sb=Sb, ob=ob: e.matmul(ob, lhsT=qpx[:, hh, c, :], rhs=Sb, start=False, stop=(c == nch - 1)),
                         reads=[("qpx", par, hh), skb], writes=[("ps", obk)])
                    P.op("pe", lambda e, hh=hh, c=c, tbk=tbk: e.matmul(pb[tbk][:, 0:256], lhsT=kpx[:, hh, c, :], rhs=vtm[:, hh * 256:(hh + 1) * 256],
                                                                      start=True, stop=True),
                         reads=[("kpx", par, hh), vtm_key], writes=[("ps", tbk)])
                    ecol = el[:, hh * 16 + c:hh * 16 + c + 1]
                    P.op("act", lambda e, Sf=Sf, ecol=ecol: e.activation(out=T["se"][:], in_=Sf, func=AF.Copy, scale=ecol),
                         reads=[skf, ("el", par)], writes=["se"])
                    P.op("dve", lambda e, ecol=ecol, Sf=Sf, tbk=tbk: e.scalar_tensor_tensor(out=Sf, in0=pb[tbk][:, 0:256], scalar=ecol, in1=T["se"][:],
                                                                                          op0=ALU.mult, op1=ALU.add),
                         reads=[("ps", tbk), ("el", par), "se"], writes=[skf])
                    if sample:
                        P.dma("sp", lambda e, c=c, h=h, Sf=Sf: e.dma_start(out=gs_out[c, h, :, :], in_=Sf), s_mo[i], reads=[skf])
                    else:
                        P.op("act", lambda e, Sf=Sf, Sb=Sb: e.activation(out=Sb, in_=Sf, func=AF.Copy), reads=[skf], writes=[skb])
                ssq = small[:, 8 + hh:9 + hh]
                P.op("act", lambda e, ob=ob, ssq=ssq: e.activation(out=junk[:, 0:256], in_=ob, func=AF.Square, accum_out=ssq),
                     reads=[("ps", obk)], writes=["junk", ("small", 8 + hh)])
                P.op("act", lambda e, ssq=ssq: e.activation(out=ssq, in_=ssq, func=AF.Ln, bias=epst[:, 0:1], scale=1.0 / 256),
                     reads=[("small", 8 + hh), "epst"], writes=[("small", 8 + hh)])
                P.op("act", lambda e, ssq=ssq: e.activation(out=ssq, in_=ssq, func=AF.Exp, scale=-0.5),
                     reads=[("small", 8 + hh)], writes=[("small", 8 + hh)])
                P.op("dve", lambda e, hh=hh, ob=ob, ssq=ssq: e.scalar_tensor_tensor(
                    out=mixed_dst[:, hh * 256:(hh + 1) * 256], in0=ob, scalar=ssq, in1=sgg[:, hh * 256:(hh + 1) * 256], op0=ALU.mult, op1=ALU.mult),
                    reads=[("ps", obk), ("small", 8 + hh), sgg_key], writes=[mixed_key])

        def gla_tiles(st):
            sbq = lambda name, shape, dt: st.enter_context(nc.sbuf_tensor(name, shape, dt, align_bytes=64))
            T = {}
            T["lf"] = sbq("g_lf", [128, 256], F32)
            T["lfb"] = sbq("g_lfb", [128, 256], BF16)
            T["einv"] = sbq("g_einv", [128, 256], F32)
            T["kp"] = sbq("g_kp", [128, 256], BF16)
            el0 = sbq("g_el0", [128, 32], F32)
            kpx0 = sbq("g_kpx0", [128, 2, 16, 128], BF16)
            T["se"] = sbq("g_se", [128, 256], F32)
            T["efm"] = sbq("g_efm", [128, 256], F32)
            T["eifm"] = sbq("g_eifm", [128, 256], F32)
            T["qpT"] = sbq("g_qpT", [128, 2, 128], BF16)
            T["kpT"] = sbq("g_kpT", [128, 2, 128], BF16)
            atm0 = sbq("g_atm0", [128, 2, 128], BF16)
            qpx0 = sbq("g_qpx0", [128, 2, 16, 128], BF16)
            T["s0f"] = [sbq("g_s0f%d" % i, [128, 256], F32) for i in range(4)]
            T["s0b"] = [sbq("g_s0b%d" % i, [128, 256], BF16) for i in range(4)]
            T["el"] = [el0, sbq("g_el1", [128, 32], F32)]
            T["kpx"] = [kpx0, sbq("g_kpx1", [128, 2, 2, 128], BF16)]
            T["qpx"] = [qpx0, sbq("g_qpx1", [128, 2, 2, 128], BF16)]
            T["atm"] = [atm0, sbq("g_atm1", [128, 2, 128], BF16)]
            return T

        st12 = ExitStack()
        npre_b = sb(st12, "npre_b", [128, D], F32)
        xs = [sb(st12, "xs%d" % i, [128, D], F32) for i in range(2)]
        xb = [sb(st12, "xb%d" % i, [128, D], BF16) for i in range(2)]
        junk = sb(st12, "junk", [128, D], BF16)
        P.dma("sp", lambda e: e.dma_start(out=npre_b[:], in_=bcast_rows(npre, D)), s_c, writes=["npre_b"])
        with ExitStack() as st:
            wst = sb(st, "wst", [128, 16, 1552], BF16)
            xTb = [sb(st, "xTb%d" % i, [128, 16, 128], BF16) for i in range(2)]
            vtm = [sb(st, "p_vtm%d" % i, [128, 1024], BF16) for i in range(2)]
            lf = [sb(st, "p_lf%d" % i, [128, 512], F32) for i in range(2)]
            lfb = [sb(st, "p_lfb%d" % i, [128, 512], BF16) for i in range(2)]
            zTp = [sb(st, "p_zT%d" % i, [17, 128], BF16) for i in range(2)]
            ew = sb(st, "p_ew", [128, 512], F32)
            kpp = sb(st, "p_kpp", [128, 512], BF16)
            el = sb(st, "p_el", [128, 8], F32)
            US = cstb[:, C_US:C_US + 128]
            ONE = cstb[:, C_ONE:C_ONE + 2]
            for i in range(2):
                P.op("pool", lambda e, i=i: e.memset(zTp[i][:], 1.0), writes=[("zTp", i)])
            wload(wst, "wst", [(KG_OFF, 512), (VG_OFF, 512), (VG_OFF + 512, 512), (Z_OFF, 16)], s_w[0])
            NPS = 0 if 'only5' in _SKIP else NPRE

            def xt_of(slot):
                if slot == NPRE - 1:
                    return xT_main[:, :, 0:128], ("xT", 0)
                return xTb[slot % 2][:], ("xTb", slot % 2)

            def stAB(slot):
                par = slot % 2
                xt, xkey = xt_of(slot)
                xprep(slot, xt, xkey, par)
                mm16(pb[par][:, :], ("ps", par), lambda kc: xt[:, kc, :], lambda kc: wst[:, kc, 0:512], [xkey, "wst"])
                mm16(pb[4][0:16, 0:128], ("ps", 4, "z"), lambda kc: wst[:, kc, 1536:1552], lambda kc: xt[:, kc, :], [xkey, "wst"])
                copy_evac(zTp[par][0:16, :], pb[4][0:16, 0:128], [("ps", 4, "z")], [("zTp", par)])
                P.op("pe", lambda e: e.matmul(pb[3][:, :], lhsT=zTp[par][0:17, :], rhs=wa2a[0:17, :], start=True, stop=True),
                     reads=[("zTp", par), "wa2a"], writes=[("ps", 3)])

            def stC(slot):
                par = slot % 2
                for hf in range(2):
                    P.op("act", lambda e, hf=hf: e.activation(out=lf[par][:, hf * 256:(hf + 1) * 256], in_=pb[3][:, hf * 256:(hf + 1) * 256], func=AF.Exp, scale=-1.0),
                         reads=[("ps", 3)], writes=[("lf", par, hf)])
                P.op("act", lambda e: e.activation(out=lf[par][:], in_=lf[par][:], func=AF.Ln, bias=1.0), reads=[("lf", par)], writes=[("lf", par)])
                P.op("dve", lambda e: e.tensor_scalar(out=lfb[par][:], in0=lf[par][:], scalar1=NVC(slot), scalar2=None, op0=ALU.mult),
                     reads=[("lf", par)], writes=[("lfb", par)])

            def stD(slot):
                par = slot % 2
                xt, xkey = xt_of(slot)
                for pi in range(2):
                    mm16(pb[6][:, :], ("ps", 6), lambda kc: xt[:, kc, :], lambda kc, pi=pi: wst[:, kc, 512 + pi * 512:1024 + pi * 512], [xkey, "wst"])
                    copy_evac(vtm[par][:, pi * 512:(pi + 1) * 512], pb[6][:, :], [("ps", 6)], [("vtm", par, pi)])

            def stE(slot):
                par = slot % 2
                P.op("pe", lambda e: e.matmul(pb[3][:, :], lhsT=US, rhs=lfb[par][:], start=True, stop=True),
                     reads=[("lfb", par), "cstb"], writes=[("ps", 3)])
                for h in range(4):
                    P.op("pe", lambda e, h=h: e.matmul(pb[4][:, 200 + 2 * h:202 + 2 * h], lhsT=lfb[par][:, h * 128:(h + 1) * 128], rhs=ONE, start=True, stop=True),
                         reads=[("lfb", par), "cstb"], writes=[("ps", 4, "bl")], sig=(h == 3))
                for hf in range(2):
                    P.op("act", lambda e, hf=hf: e.activation(out=ew[:, hf * 256:(hf + 1) * 256], in_=pb[3][:, hf * 256:(hf + 1) * 256], func=AF.Exp),
                         reads=[("ps", 3)], writes=[("ew", hf)])
                P.op("act", lambda e: e.activation(out=el[:], in_=pb[4][:, 200:208], func=AF.Exp), reads=[("ps", 4, "bl")], writes=["el"])
                P.op("dve", lambda e: e.tensor_tensor(out=kpp[:], in0=pb[par][:, :], in1=ew[:], op=ALU.mult),
                     reads=[("ps", par), "ew"], writes=["kpp"])

            def stF(slot):
                par = slot % 2
                for half in range(2):
                    bnk = 5 if half == 0 else 7
                    for hh in range(2):
                        h = half * 2 + hh
                        P.op("pe", lambda e, h=h, hh=hh, bnk=bnk: e.matmul(pb[bnk][:, hh * 256:(hh + 1) * 256], lhsT=kpp[:, h * 128:(h + 1) * 128],
                                                                         rhs=vtm[par][:, h * 256:(h + 1) * 256], start=True, stop=True),
                             reads=["kpp", ("vtm", par)], writes=[("ps", bnk, hh)])
                    for hh in range(2):
                        h = half * 2 + hh
                        P.op("dve", lambda e, h=h, hh=hh, bnk=bnk: e.scalar_tensor_tensor(
                            out=Sst[:, h, :], in0=Sst[:, h, :], scalar=el[:, 2 * h:2 * h + 1], in1=pb[bnk][:, hh * 256:(hh + 1) * 256], op0=ALU.mult, op1=ALU.add),
                            reads=[("S", h), "el", ("ps", bnk, hh)], writes=[("S", h)])

            if NPS:
                stAB(0); stC(0); stD(0)
            for slot in range(NPS):
                if slot + 1 < NPS:
                    stAB(slot + 1); stC(slot + 1)
                if "pxE" not in _SKIP:
                    stE(slot)
                if slot + 1 < NPS:
                    stD(slot + 1)
                if "pxE" not in _SKIP and "pxF" not in _SKIP:
                    stF(slot)
            P.op("act", lambda e: e.activation(out=Sbf[:], in_=Sst[:], func=AF.Copy), reads=["S"], writes=["Sbf"])
            P.flush()
        if _STOP == 1:
            st12.close()
            return nc

        for mb in range(1, 1 if 'only5' in _SKIP else NMB):
            xprep(NPRE - 1 + mb, xT_main[:, :, mb * 128:(mb + 1) * 128], ("xT", mb), mb % 2)
        P.flush()
        st12.close()
        if _STOP == 2:
            return nc

        TOKG = [(128, 512), (640, 512), (1152, 128)]

        def fproj(wsl, wkey, lhs_fn, M, tok_groups, consumer):
            for (t0, n) in tok_groups:
                b = nbank()
                mm16(pb[b][0:M, 0:n], ("ps", b), lhs_fn, lambda kc, t0=t0, n=n: xT_main[:, kc, t0:t0 + n], ["xT", wkey])
                consumer(pb[b][0:M, 0:n], ("ps", b), t0, n)

        def tproj(wsl, wkey, c0, ncols, blocks, consumer):
            for mb in blocks:
                b = nbank()
                mm16(pb[b][:, 0:ncols], ("ps", b), lambda kc, mb=mb: xT_main[:, kc, mb * 128:(mb + 1) * 128],
                     lambda kc: wsl[:, kc, c0:c0 + ncols], [("xT", mb), wkey])
                consumer(pb[b][:, 0:ncols], ("ps", b), mb)

        cnt["ev"] = 100
        with ExitStack() as st:
            wsl = [sb(st, "wsl%d" % i, [128, 16, 512], BF16) for i in range(2)]
            QT = sb(st, "QT", [128, 4, NMB * 128], BF16)
            KT = sb(st, "KT", [128, NMB * 128], BF16)
            Va = sb(st, "Va", [128, NMB, 2, 80], BF16)
            sga = sb(st, "sga", [128, NMB, 512], BF16)
            KcT = sb(st, "KcT", [128, 16, 128], BF16)
            Vc = sb(st, "Vc", [128, 16, 2, 80], BF16)
            kcs = sb(st, "kcs", [128, 16, 128], BF16)
            stt = [sb(st, "stt%d" % i, [128, 512], F32) for i in range(2)]
            ptt = [sb(st, "ptt%d" % i, [128, 512], BF16) for i in range(4)]
            oTs = [sb(st, "oTs%d" % i, [65, 512], F32) for i in range(2)]
            zcol = sb(st, "zcol", [128, 1], F32)
            BTp = sb(st, "BTp", [128, 16, 128], F32)
            BTc = sb(st, "BTc", [128, 16, 128], F32)
            BTs = sb(st, "BTs", [128, 16, 16, 8], F32)
            P.op("pool", lambda e: e.memset(BTs[:], NEG), writes=["BTs"])
            if "p3bt" not in _SKIP:
                P.dma("sp", lambda e: e.dma_start(out=BTc[:], in_=bass.AP(fscr.tensor, 127, [[382, 128], [128 * 383, 16], [1, 128]])),
                      s_c, writes=["BTc"])
                P.dma("sp", lambda e: e.dma_start(out=BTp[:], in_=bass.AP(fscr.tensor, 255, [[382, 128], [128 * 383, 16], [1, 128]])),
                      s_c, writes=["BTp"])
                for j in range(16):
                    P.dma("sp", lambda e, j=j: e.dma_start(
                        out=BTs[8 * j:8 * j + 8, j, :, :],
                        in_=bass.AP(fscr.tensor, 127, [[382, 8], [128 * 383, 16], [1, 8]])), s_c, writes=["BTs"])
            P.op("pool", lambda e: e.memset(Va[:], 1.0), writes=["Va"])
            P.op("pool", lambda e: e.memset(Vc[:], 1.0), writes=["Vc"])
            P.op("pool", lambda e: e.memset(zcol[:], 0.0), writes=["zcol"])
            wi = 0
            for p in range(0 if 'only5' in _SKIP else 2):
                w, wk, sem = wsl[wi % 2], ("wsl", wi % 2), s_w[wi % 2]; wi += 1
                wload(w, wk, [(Q_OFF + (2 * p + kvl_) * 256 + g_ * 64, 64) for g_ in range(4) for kvl_ in range(2)], sem)
                for g in range(4):
                    if "p3q" in _SKIP:
                        continue
                    lhs = lambda kc, w=w, g=g: w[:, kc, g * 128:(g + 1) * 128]
                    fproj(w, wk, lhs, 128, TOKG,
                          lambda ps, pk, t0, n, g=g: copy_evac(QT[:, g, t0:t0 + n], ps, [pk], [("QT", g)], scale=0.125))
                w, wk, sem = wsl[wi % 2], ("wsl", wi % 2), s_w[wi % 2]; wi += 1
                wload(w, wk, [(K_OFF + p * 128, 128), (V_OFF + p * 128, 128)], sem)
                if "p3k" not in _SKIP:
                    fproj(w, wk, lambda kc, w=w: w[:, kc, 0:128], 128, [(0, 512), (512, 512), (1024, 256)],
                          lambda ps, pk, t0, n: copy_evac(KT[:, t0:t0 + n], ps, [pk], ["KT"]))

                def kv_cons(ps, pk, mb, p=p):
                    if "p3va" not in _SKIP:
                        for a_ in range(2):
                            copy_evac(Va[:, mb, a_, 0:64], ps[:, 128 + a_ * 64:128 + (a_ + 1) * 64], [pk], [("Va", mb, a_)])
                    if mb >= 8 and "p3ko" not in _SKIP:
                        P.op("dve", lambda e: e.tensor_copy(out=kout[:, mb - 8, p * 128:(p + 1) * 128], in_=ps[:, 0:128]), reads=[pk], writes=["kout"])
                        P.op("dve", lambda e: e.tensor_copy(out=vout[:, mb - 8, p * 128:(p + 1) * 128], in_=ps[:, 128:256]), reads=[pk], writes=["vout"])
                if "p3kv" not in _SKIP:
                    tproj(w, wk, 0, 256, range(NMB), kv_cons)
                w, wk, sem = wsl[wi % 2], ("wsl", wi % 2), s_w[wi % 2]; wi += 1
                wload(w, wk, [(GA_OFF + p * 512, 512)], sem)
                if "p3g" not in _SKIP:
                    tproj(w, wk, 0, 512, range(1, NMB),
                          lambda ps, pk, mb: P.op("act", lambda e: e.activation(out=sga[:, mb, :], in_=ps, func=AF.Silu), reads=[pk], writes=[("sga", mb)]))
                if "p3cache" in _SKIP:
                    continue
                P.dma("pool", lambda e, p=p: e.dma_start(out=kcs[:], in_=ck[:, :, p * 128:(p + 1) * 128].rearrange("j s c -> s j c")), s_cp, writes=["kcs"])
                for a_ in range(2):
                    P.dma("pool", lambda e, p=p, a_=a_: e.dma_start(
                        out=Vc[:, :, a_, 0:64], in_=cv[:, :, p * 128 + a_ * 64:p * 128 + (a_ + 1) * 64].rearrange("j s d -> s j d")),
                        s_cp, writes=["Vc"])
                for jj in range(2):
                    for j8 in range(8):
                        j = jj * 8 + j8
                        P.op("pe", lambda e, j=j, j8=j8: e.transpose(out=pbb[2][:, j8 * 128:(j8 + 1) * 128], in_=kcs[:, j, :], identity=ident_b),
                             reads=["kcs", "cstb"], writes=[("ps", 2)], sig=(j8 == 7))
                    copy_evac(KcT[:, jj * 8:(jj + 1) * 8, :], pbb[2][:, :].rearrange("p (a b) -> p a b", b=128), [("ps", 2)], ["KcT"])

                def unit_s1(kvl, kprev, kprev_key, kcur, qfn, nq, bprev, bcur, hmcol, vprev, vprev_key, vcur, ocol0, first, last, ui):
                    pr = slice(kvl * 64, kvl * 64 + 64)
                    for kb, (kt, kk, bias) in enumerate(((kprev, kprev_key, bprev), (kcur, "KT", bcur))):
                        sbk = 3 + kb
                        spsum = pb[sbk][:, 0:4 * nq].rearrange("p (g q) -> p g q", g=4)
                        for g in range(4):
                            P.op("pe", lambda e, g=g, kt=kt, spsum=spsum: e.matmul(spsum[:, g, :], lhsT=kt[pr, :], rhs=qfn(g)[pr, :], start=True, stop=True),
                                 reads=[kk, ("QT", g)], writes=[("ps", sbk)], sig=(g == 3))
                        stv = stt[kb][:, 0:4 * nq].rearrange("p (g q) -> p g q", g=4)
                        P.op("dve", lambda e, spsum=spsum, bias=bias, stv=stv, kb=kb: e.scalar_tensor_tensor(
                            out=stv, in0=spsum, scalar=(hmcol if kb == 0 else zcol[:, 0:1]), in1=bias, op0=ALU.add, op1=ALU.add),
                            reads=[("ps", sbk), "BTp", "BTc", "BTs", "cstf", "zcol"], writes=[("stt", kb)])
                        pi = (ui % 2) * 2 + kb
                        P.op("act", lambda e, pi=pi, kb=kb: e.activation(out=ptt[pi][:, 0:4 * nq], in_=stt[kb][:, 0:4 * nq], func=AF.Exp),
                             reads=[("stt", kb)], writes=[("ptt", pi)])

                def unit_s2(kvl, kprev, kprev_key, kcur, qfn, nq, bprev, bcur, hmcol, vprev, vprev_key, vcur, ocol0, first, last, ui):
                    for kb, (vv, vk) in enumerate(((vprev, vprev_key), (vcur, "Va"))):
                        pi = (ui % 2) * 2 + kb
                        oap = pb[5][0:65, ocol0:ocol0 + 4 * nq]
                        P.op("pe", lambda e, vv=vv, pi=pi, oap=oap, kb=kb: e.matmul(
                            oap, lhsT=vv, rhs=ptt[pi][:, 0:4 * nq], start=(kb == 0), stop=(kb == 1)),
                            reads=[vk, ("ptt", pi)], writes=[("ps", 5)])

                def fin_a(kvl, mb, fi):
                    ot = oTs[fi % 2]
                    if mb == 9:
                        P.op("act", lambda e: e.activation(out=ot[:, :].rearrange("p (g j t) -> p g j t", g=4, j=16),
                                                           in_=pb[5][0:65, :].rearrange("p (j g t) -> p g j t", j=16, g=4), func=AF.Copy),
                             reads=[("ps", 5)], writes=[("oTs", fi % 2)])
                    else:
                        P.op("act", lambda e: e.activation(out=ot[:], in_=pb[5][0:65, :], func=AF.Copy), reads=[("ps", 5)], writes=[("oTs", fi % 2)])

                def fin_b(kvl, mb, fi, p=p):
                    ot = oTs[fi % 2]
                    for g in range(4):
                        P.op("pe", lambda e, g=g: e.transpose(out=pb[6][:, g * 65:(g + 1) * 65], in_=ot[0:65, g * 128:(g + 1) * 128],
                                                             identity=ident_f[0:65, 0:65]),
                             reads=[("oTs", fi % 2), "cstf"], writes=[("ps", 6)], sig=(g == 3))
                    for g in range(4):
                        h = (2 * p + kvl) * 4 + g
                        rc = small[:, 16 + g:17 + g]
                        P.op("dve", lambda e, g=g, h=h, rc=rc: e.tensor_tensor(out=rc, in0=pb[6][:, g * 65 + 64:g * 65 + 65], in1=esink[:, h:h + 1], op=ALU.add),
                             reads=[("ps", 6), "esink"], writes=[("small", 16 + g)])
                        P.op("dve", lambda e, rc=rc: e.reciprocal(out=rc, in_=rc), reads=[("small", 16 + g)], writes=[("small", 16 + g)])
                        P.op("dve", lambda e, g=g, h=h, rc=rc: e.scalar_tensor_tensor(
                            out=mixed[:, mb, h * 64:(h + 1) * 64], in0=pb[6][:, g * 65:g * 65 + 64], scalar=rc,
                            in1=sga[:, mb, kvl * 256 + g * 64:kvl * 256 + (g + 1) * 64], op0=ALU.mult, op1=ALU.mult),
                            reads=[("ps", 6), ("small", 16 + g), ("sga", mb)], writes=[("mixed", mb)])

                def samp_s1(kvl, kv, ui):
                    pr = slice(kvl * 64, kvl * 64 + 64)
                    for kb in range(2):
                        sbk = 3 + kb
                        for j in range(16):
                            kt = KcT[:, j, :] if kb == 0 else KT[:, 9 * 128:10 * 128]
                            for g in range(4):
                                P.op("pe", lambda e, g=g, j=j, kt=kt, sbk=sbk: e.matmul(
                                    pb[sbk][:, j * 32 + g * 8:j * 32 + g * 8 + 8], lhsT=kt[pr, :],
                                    rhs=QT[pr, g, 9 * 128 + 8 * j:9 * 128 + 8 * j + 8], start=True, stop=True),
                                    reads=["KcT" if kb == 0 else "KT", ("QT", g)], writes=[("ps", sbk)], sig=(j == 15 and g == 3))
                        for j in range(16):
                            bias = BTp[:, kv * 4:kv * 4 + 4, 0:8] if kb == 0 else BTs[:, j, kv * 4:kv * 4 + 4, :]
                            P.op("dve", lambda e, j=j, bias=bias, sbk=sbk, kb=kb: e.scalar_tensor_tensor(
                                out=stt[kb][:, j * 32:(j + 1) * 32].rearrange("p (g q) -> p g q", g=4),
                                in0=pb[sbk][:, j * 32:(j + 1) * 32].rearrange("p (g q) -> p g q", g=4),
                                scalar=zcol[:, 0:1], in1=bias, op0=ALU.add, op1=ALU.add),
                                reads=[("ps", sbk), "BTp", "BTc", "BTs", "zcol"], writes=[("stt", kb)])
                        pi = (ui % 2) * 2 + kb
                        P.op("act", lambda e, pi=pi, kb=kb: e.activation(out=ptt[pi][:, :], in_=stt[kb][:, :], func=AF.Exp),
                             reads=[("stt", kb)], writes=[("ptt", pi)])

                def samp_s2(kvl, kv, ui):
                    for j in range(16):
                        for kb in range(2):
                            pi = (ui % 2) * 2 + kb
                            vv = Vc[:, j, kvl, 0:65] if kb == 0 else Va[:, 9, kvl, 0:65]
                            P.op("pe", lambda e, vv=vv, pi=pi, j=j, kb=kb: e.matmul(
                                pb[5][0:65, j * 32:(j + 1) * 32], lhsT=vv, rhs=ptt[pi][:, j * 32:(j + 1) * 32], start=(kb == 0), stop=(kb == 1)),
                                reads=["Vc" if kb == 0 else "Va", ("ptt", pi)], writes=[("ps", 5)], sig=(j == 15 and kb == 1))

                units = []
                ui = 0
                for kvl in range(2):
                    if "p3units" in _SKIP:
                        continue
                    kv = 2 * p + kvl
                    for mb in range(1, 9):
                        ua = (kvl, KT[:, (mb - 1) * 128:mb * 128], "KT", KT[:, mb * 128:(mb + 1) * 128],
                              (lambda g, mb=mb: QT[:, g, mb * 128:(mb + 1) * 128]), 128,
                              BTp[:, kv * 4:kv * 4 + 4, :], BTc[:, kv * 4:kv * 4 + 4, :],
                              (HMC if mb == 1 else zcol[:, 0:1]),
                              Va[:, mb - 1, kvl, 0:65], "Va", Va[:, mb, kvl, 0:65], 0, True, True, ui)
                        units.append((lambda ua=ua: unit_s1(*ua), lambda ua=ua: unit_s2(*ua), (kvl, mb)))
                        ui += 1
                    units.append((lambda kvl=kvl, kv=kv, ui=ui: samp_s1(kvl, kv, ui), lambda kvl=kvl, kv=kv, ui=ui: samp_s2(kvl, kv, ui), (kvl, 9)))
                    ui += 1
                if units:
                    units[0][0]()
                for i_, (f1, f2, fin) in enumerate(units):
                    if i_ + 1 < len(units):
                        units[i_ + 1][0]()
                    f2()
                    fin_a(fin[0], fin[1], i_)
                    if i_ >= 1:
                        fin_b(units[i_ - 1][2][0], units[i_ - 1][2][1], i_ - 1)
                if units:
                    fin_b(units[-1][2][0], units[-1][2][1], len(units) - 1)
            if "p3out" in _SKIP:
                P.flush()
                return nc
            P.dma("sp", lambda e: e.dma_start(out=kwin[:, :], in_=kout[:, 0, :]), s_o, reads=["kout"])
            P.dma("sp", lambda e: e.dma_start(out=vwin[:, :], in_=vout[:, 0, :]), s_o, reads=["vout"])
            P.dma("sp", lambda e: e.dma_start(out=ks_out[:, 0:120, :], in_=ck[:, 8:128, :]), s_o)
            P.dma("sp", lambda e: e.dma_start(out=vs_out[:, 0:120, :], in_=cv[:, 8:128, :]), s_o)
            for j in range(16):
                P.dma("sp", lambda e, j=j: e.dma_start(out=ks_out[j, 120:128, :], in_=kout[8 * j:8 * j + 8, 1, :]), s_o, reads=["kout"])
                P.dma("sp", lambda e, j=j: e.dma_start(out=vs_out[j, 120:128, :], in_=vout[8 * j:8 * j + 8, 1, :]), s_o, reads=["vout"])
            P.flush()
            if _STOP == 3:
                return nc

        cnt["ev"] = 174
        with ExitStack() as st:
            wsl = [sb(st, "wslg%d" % i, [128, 16, 512], BF16) for i in range(2)]
            qgT = sb(st, "qgT", [128, 2, NMB * 128], BF16)
            kgT = sb(st, "kgT", [128, 2, NMB * 128], BF16)
            vgt = sb(st, "vgt", [128, NMB, 512], BF16)
            sgg = sb(st, "sgg", [128, NMB, 512], BF16)
            sgtmp = sb(st, "sgtmp", [128, 512], F32)
            gn_b = sb(st, "gn_b", [128, 512], F32)
            junk = sb(st, "junk4", [128, 256], BF16)
            P.dma("sp", lambda e: e.dma_start(out=gn_b[:, 0:256], in_=bcast_rows(gnorm, 256)), s_c, writes=["gn_b"])
            P.dma("sp", lambda e: e.dma_start(out=gn_b[:, 256:512], in_=bcast_rows(gnorm, 256)), s_c, writes=["gn_b"])
            T = gla_tiles(st)
            wz = sb(st, "wz", [128, 16, 16], BF16)
            wload(wz, "wz", [(Z_OFF, 16)], s_w[0])
            fproj(wz, "wz", lambda kc: wz[:, kc, :], 16, TOKG,
                  lambda ps, pk, t0, n: copy_evac(zT[0:16, t0:t0 + n], ps, [pk], ["zT"]))
            wi = 0
            for hp in range(0 if 'only5' in _SKIP else 2):
                w, wk, sem = wsl[wi % 2], ("wslg", wi % 2), s_w[wi % 2]; wi += 1
                wload(w, wk, [(QG_OFF + hp * 256, 256), (KG_OFF + hp * 256, 256)], sem)
                for hh in range(2):
                    fproj(w, wk, lambda kc, w=w, hh=hh: w[:, kc, hh * 128:(hh + 1) * 128], 128, TOKG,
                          lambda ps, pk, t0, n, hh=hh: copy_evac(qgT[:, hh, t0:t0 + n], ps, [pk], ["qgT"]))
                    fproj(w, wk, lambda kc, w=w, hh=hh: w[:, kc, 256 + hh * 128:256 + (hh + 1) * 128], 128, TOKG,
                          lambda ps, pk, t0, n, hh=hh: copy_evac(kgT[:, hh, t0:t0 + n], ps, [pk], ["kgT"]))
                w, wk, sem = wsl[wi % 2], ("wslg", wi % 2), s_w[wi % 2]; wi += 1
                wload(w, wk, [(VG_OFF + hp * 512, 512)], sem)
                tproj(w, wk, 0, 512, range(1, NMB), lambda ps, pk, mb: copy_evac(vgt[:, mb, :], ps, [pk], [("vgt", mb)]))
                w, wk, sem = wsl[wi % 2], ("wslg", wi % 2), s_w[wi % 2]; wi += 1
                wload(w, wk, [(GG_OFF + hp * 512, 512)], sem)

                def gg_cons(ps, pk, mb):
                    P.op("act", lambda e: e.activation(out=sgtmp[:], in_=ps, func=AF.Silu), reads=[pk], writes=["sgtmp"])
                    P.op("dve", lambda e: e.tensor_tensor(out=sgg[:, mb, :], in0=sgtmp[:], in1=gn_b[:], op=ALU.mult),
                         reads=["sgtmp", "gn_b"], writes=[("sgg", mb)])
                tproj(w, wk, 0, 512, range(1, NMB), gg_cons)
                def gb(mb, stage, par):
                    samp = (mb == 9)
                    gla_block(T, (2 * hp, 2 * hp + 1), zT[0:17, mb * 128:(mb + 1) * 128], "zT", None, None,
                              vgt[:, mb, :], ("vgt", mb), NVC(NPRE - 1 + mb), 8 if samp else 64,
                              [qgT[:, hh, mb * 128:(mb + 1) * 128] for hh in range(2)],
                              [kgT[:, hh, mb * 128:(mb + 1) * 128] for hh in range(2)], ("qgT", "kgT"),
                              sgg[:, mb, :], ("sgg", mb),
                              mixed[:, mb, 1024 + hp * 512:1024 + (hp + 1) * 512], ("mixed", mb), sample=samp, stage=stage, par=par)
                if "gseq" in _SKIP:
                    for mb in range(1, 9):
                        gb(mb, 1, mb % 2)
                        gb(mb, 2, mb % 2)
                else:
                    gb(1, 1, 1)
                    for mb in range(1, 9):
                        if mb + 1 < 9:
                            gb(mb + 1, 1, (mb + 1) % 2)
                        gb(mb, 2, mb % 2)
                gb(9, "all", 0)
            for h in range(4):
                P.dma("sp", lambda e, h=h: e.dma_start(out=glast[h, :, :], in_=Sst[:, h, :]), s_o, reads=[("S", h)])
            P.flush()
            if _STOP == 4:
                return nc

        cnt["ev"] = 237
        with ExitStack() as st:
            wo = sb(st, "wo", [128, 16, D], BF16)
            npost_b = sb(st, "npost_b", [128, D], F32)
            mT = sb(st, "mT", [128, 16, 128], BF16)
            osb = sb(st, "osb", [128, D], F32)
            xs = [sb(st, "xs5_%d" % i, [128, D], F32) for i in range(2)]
            junk = sb(st, "junk5", [128, D], BF16)
            for i in range(4):
                P.dma("pool", lambda e, i=i: e.dma_start(out=wo[:, :, i * 512:(i + 1) * 512],
                                                       in_=w_out[:, i * 512:(i + 1) * 512].rearrange("(kc p) n -> p kc n", p=128)),
                      s_w[i % 2], writes=[("wo", i)])
            P.dma("sp", lambda e: e.dma_start(out=npost_b[:], in_=bcast_rows(npost, D)), s_c, writes=["npost_b"])
            for mb in range(1, NMB):
                slot = NPRE - 1 + mb
                i = cnt["x"] % 2
                cnt["x"] += 1
                P.dma("sp", lambda e, i=i, slot=slot: e.dma_start(out=xs[i][:], in_=xp[slot * 128:(slot + 1) * 128, :]), s_ld[i], writes=[("xs", i)])
                for half in range(2):
                    for j in range(8):
                        kc = half * 8 + j
                        P.op("pe", lambda e, kc=kc, j=j, mb=mb: e.transpose(out=pbb[2][:, j * 128:(j + 1) * 128],
                                                                         in_=mixed[:, mb, kc * 128:(kc + 1) * 128], identity=ident_b),
                             reads=[("mixed", mb), "cstb"], writes=[("ps", 2)], sig=(j == 7))
                    copy_evac(mT[:, half * 8:(half + 1) * 8, :], pbb[2][:, :].rearrange("p (a b) -> p a b", b=128), [("ps", 2)], ["mT"])
                for n in range(4):
                    b = 4 + n
                    mm16(pb[b][:, :], ("ps", b), lambda kc: mT[:, kc, :], lambda kc, n=n: wo[:, kc, n * 512:(n + 1) * 512], ["mT", ("wo", n)])
                    P.op("dve", lambda e, n=n, b=b: e.tensor_copy(out=osb[:, n * 512:(n + 1) * 512], in_=pb[b][:, :]), reads=[("ps", b)], writes=[("osb", n)])
                P.op("act", lambda e: e.activation(out=junk[:], in_=osb[:], func=AF.Square, accum_out=small[:, 30:31]),
                     reads=["osb"], writes=["junk", ("small", 30)])
                P.op("act", lambda e: e.activation(out=small[:, 31:32], in_=small[:, 30:31], func=AF.Ln, bias=epst[:, 0:1], scale=1.0 / D),
                     reads=[("small", 30), "epst"], writes=[("small", 31)])
                P.op("act", lambda e: e.activation(out=small[:, 32:33], in_=small[:, 31:32], func=AF.Exp, scale=-0.5),
                     reads=[("small", 31)], writes=[("small", 32)])
                P.op("dve", lambda e: e.scalar_tensor_tensor(out=osb[:], in0=osb[:], scalar=small[:, 32:33], in1=npost_b[:], op0=ALU.mult, op1=ALU.mult),
                     reads=["osb", ("small", 32), "npost_b"], writes=["osb"])
                P.op("dve" if "p5pool" in _SKIP else "pool", lambda e, i=i: e.tensor_tensor(out=osb[:], in0=osb[:], in1=xs[i][:], op=ALU.add), reads=["osb", ("xs", i)], writes=["osb"])
                if "p5out" in _SKIP:
                    continue
                if mb < 9:
                    P.dma("sp", lambda e, mb=mb: e.dma_start(out=y_main[(mb - 1) * 128:mb * 128, :], in_=osb[:]), s_o, reads=["osb"])
                else:
                    P.dma("sp", lambda e: e.dma_start(out=y_samp[:, :], in_=osb[:]), s_o, reads=["osb"])
            P.flush()
            if _STOP == 5:
                return nc
    return nc


_CACHE = {}


def kernel(x_prompt, x_sample, cache_k_win, cache_v_win, state_gla, meta_tokens, rel_bias,
           norm_pre, norm_post, w_in, w_a2, b_a, attn_sinks, gla_norm, w_out):
    f = lambda a: np.ascontiguousarray(np.asarray(a, dtype=np.float32))
    x_prompt, x_sample = f(x_prompt), f(x_sample)
    ckw, cvw, sgl = f(cache_k_win)[0], f(cache_v_win)[0], f(state_gla)[0]
    meta = f(meta_tokens)
    cbase = make_consts()
    ii = np.arange(128)
    cB = np.zeros((128, NCB), np.float32)
    cB[:, 0:C_OH] = cbase[:, 0:C_OH]
    cB[:, C_US:C_US + 128] = (ii[:, None] > ii[None, :])
    cB[:, C_ONE:C_ONE + 2] = 1.0
    ohm_h = np.ascontiguousarray(cbase[0:33, C_OH:C_OH + 383])
    if "nc" not in _CACHE:
        _CACHE["nc"] = build()
    nc = _CACHE["nc"]
    w_in0, w_out0 = f(w_in)[0], f(w_out)[0]
    in_maps = []
    for c in range(8):
        b, q = c // 4, c % 4
        xpad = np.zeros((33 * 128, D), np.float32)
        xpad[112:128] = meta
        xpad[128:] = x_prompt[b]
        nblk = 8 * q + 9
        xc = np.zeros((NSLOT * 128, D), np.float32)
        xc[(33 - nblk) * 128:33 * 128] = xpad[:nblk * 128]
        xc[33 * 128:] = x_sample[16 * c:16 * c + 16].reshape(128, D)
        valid = np.zeros((NSLOT * 128,), np.float32)
        vpad = np.ones((33 * 128,), np.float32)
        vpad[:112] = 0
        valid[(33 - nblk) * 128:33 * 128] = vpad[:nblk * 128]
        valid[33 * 128:] = 1
        cs = np.zeros((128, 163), np.float32)
        cs[:, 0:128] = cbase[:, C_ID:C_ID + 128]
        cs[:, 128:128 + NSLOT] = (-valid / 16.0).reshape(NSLOT, 128).T
        if q == 0:
            cs[:112, 162] = NEG
        in_maps.append({
            "xp": xc, "w_in": w_in0, "w_out": w_out0,
            "ck": np.ascontiguousarray(ckw[16 * c:16 * c + 16].reshape(16, 128, 256)),
            "cv": np.ascontiguousarray(cvw[16 * c:16 * c + 16].reshape(16, 128, 256)),
            "sg": np.ascontiguousarray(sgl[16 * c:16 * c + 16]),
            "cstS": cs, "cstB": cB, "ohm": ohm_h, "relb": f(rel_bias), "npre": f(norm_pre), "npost": f(norm_post),
            "wa2": f(w_a2)[0], "ba": f(b_a), "sinks": f(attn_sinks), "gnorm": f(gla_norm),
        })
    res = run_bass_kernel_spmd(nc, in_maps, core_ids=list(range(8)))
    R = res.results
    y_prompt = np.zeros((2, 4096, D), np.float32)
    y_sample = np.zeros((128, 8, D), np.float32)
    kwp = np.zeros((1, 2, 128, 4, 64), np.float32)
    vwp = np.zeros((1, 2, 128, 4, 64), np.float32)
    sgp = np.zeros((1, 2, 4, 128, 256), np.float32)
    kws = np.zeros((1, 128, 128, 4, 64), np.float32)
    vws = np.zeros((1, 128, 128, 4, 64), np.float32)
    sgs = np.zeros((1, 128, 4, 128, 256), np.float32)
    for c in range(8):
        b, q = c // 4, c % 4
        r = R[c]
        y_prompt[b, q * 1024:(q + 1) * 1024] = r["y_main"]
        y_sample[16 * c:16 * c + 16] = r["y_samp"].reshape(16, 8, D)
        kws[0, 16 * c:16 * c + 16] = r["ks_out"].reshape(16, 128, 4, 64)
        vws[0, 16 * c:16 * c + 16] = r["vs_out"].reshape(16, 128, 4, 64)
        sgs[0, 16 * c:16 * c + 16] = r["gs_out"]
        if q == 3:
            kwp[0, b] = r["kwin"].reshape(128, 4, 64)
            vwp[0, b] = r["vwin"].reshape(128, 4, 64)
            sgp[0, b] = r["glast"]
    return (y_prompt, y_sample, kwp, vwp, sgp, kws, vws, sgs)
```

```python
import math
from contextlib import ExitStack
import numpy as np
import concourse.bass as bass
import concourse.mybir as mybir
from concourse.bass_utils import run_bass_kernel_spmd

F32 = mybir.dt.float32
BF16 = mybir.dt.bfloat16
AF = mybir.ActivationFunctionType
ALU = mybir.AluOpType

D = 2048
INW = 5648
Q_OFF, K_OFF, V_OFF, GA_OFF, QG_OFF, KG_OFF, VG_OFF, GG_OFF, Z_OFF = 0, 1024, 1280, 1536, 2560, 3072, 3584, 4608, 5632
NSLOT = 34
NPRE = 25
NMB = 10
NEG = -30000.0
EPS = 1e-6

C_ID, C_U64, C_U8, C_CI64, C_CI8, C_CMK64, C_CMK8, C_OH, C_NV, C_HM = (
    0, 128, 256, 384, 386, 402, 658, 2706, 3089, 3123)
NCST = 3124
C_US, C_ONE, NCB = 2752, 2880, 2888


class Prog:
    ENG = ("pe", "act", "dve", "pool", "sp")

    def __init__(self, nc, stack):
        self.nc = nc
        self.stack = stack
        self.sems = {e: stack.enter_context(nc.semaphore("s_" + e)) for e in self.ENG}
        self.cnt = {e: 0 for e in self.ENG}
        self.dsems = []
        self._reset()

    def _reset(self):
        self.q = {e: [] for e in self.ENG}
        self.regs = {}
        self.children = {}

    def dma_sem(self, name):
        d = {"h": self.stack.enter_context(self.nc.semaphore(name)), "v": 0}
        self.dsems.append(d)
        return d

    def _reg(self, k):
        r = self.regs.get(k)
        if r is None:
            r = self.regs[k] = {"w": {}, "r": {}}
            for i in range(1, len(k)):
                self.children.setdefault(k[:i], set()).add(k)
        return r

    def _conf(self, k):
        out = []
        for i in range(1, len(k) + 1):
            r = self.regs.get(k[:i])
            if r is not None:
                out.append(r)
        for c in self.children.get(k, ()):
            out.append(self.regs[c])
        return out

    @staticmethod
    def _merge(dst, src):
        for sk, (h, v) in src.items():
            if sk not in dst or dst[sk][1] < v:
                dst[sk] = (h, v)

    def _deps(self, reads, writes):
        need = {}
        for k in reads:
            for c in self._conf(k):
                self._merge(need, c["w"])
        for k in writes:
            for c in self._conf(k):
                self._merge(need, c["w"])
                self._merge(need, c["r"])
        return need

    def _record(self, tok, reads, writes):
        sk, h, v = tok
        for k in reads:
            self._merge(self._reg(k)["r"], {sk: (h, v)})
        for k in writes:
            r = self._reg(k)
            for c in self._conf(k):
                if c is not r:
                    self._merge(c["w"], {sk: (h, v)})
            r["w"] = {sk: (h, v)}
            r["r"] = {}

    @staticmethod
    def _norm(k):
        k = tuple(k) if isinstance(k, (tuple, list)) else (k,)
        return k[:2] if k[0] == "ps" else k

    def op(self, eng, fn, reads=(), writes=(), sig=True):
        reads = [self._norm(k) for k in reads]
        writes = [self._norm(k) for k in writes]
        writes = writes + [k for k in reads if k[0] == "ps" and k not in writes]
        reads = [k for k in reads if k[0] != "ps"]
        need = self._deps(reads, writes)
        if sig:
            self.cnt[eng] += 1
            self._record((eng, self.sems[eng], self.cnt[eng]), reads, writes)
        self.q[eng].append((need, fn, sig, None))

    def dma(self, eng, fn, sem, reads=(), writes=()):
        reads = [self._norm(k) for k in reads]
        writes = [self._norm(k) for k in writes]
        need = self._deps(reads, writes)
        if sem["v"] > 0:
            self._merge(need, {id(sem): (sem["h"], sem["v"])})
        sem["v"] += 16
        self._record((id(sem), sem["h"], sem["v"]), reads, writes)
        self.q[eng].append((need, fn, False, sem))

    def flush(self):
        need = {id(d): (d["h"], d["v"]) for d in self.dsems if d["v"] > 0}
        for e in self.ENG:
            self.q[e].append((dict(need), lambda h: h.nop(), False, None))
        nc = self.nc
        with nc.Block() as block:
            handles = {"pe": block.tensor, "act": block.scalar, "dve": block.vector,
                       "pool": block.gpsimd, "sp": block.sync}
            for e in self.ENG:
                def body(h, lst=self.q[e], e=e):
                    waited = {}
                    for need_, fn, sig, dsem in lst:
                        for sk, (sh, v) in need_.items():
                            if waited.get(sk, 0) >= v:
                                continue
                            waited[sk] = v
                            h.wait_ge(sh, v)
                        ins = fn(h)
                        if sig:
                            ins.then_inc(self.sems[e], 1)
                        if dsem is not None:
                            ins.then_inc(dsem["h"], 16)
                handles[e](body)
        self._reset()


def t5_bucket_np(dist):
    d = np.maximum(dist, 0)
    ratio = np.maximum(d, 16).astype(np.float32) / np.float32(16)
    large = 16 + (np.log(ratio) / np.float32(math.log(128 / 16)) * 16).astype(np.int32)
    large = np.minimum(large, 31)
    return np.where(d < 16, d, large)


def make_consts():
    c = np.zeros((128, NCST), np.float32)
    i = np.arange(128)
    c[:, C_ID:C_ID + 128] = np.eye(128)
    for off, C in ((C_U64, 64), (C_U8, 8)):
        c[:, off:off + 128] = ((i[:, None] <= i[None, :]) & (i[:, None] // C == i[None, :] // C))
    c[:, C_CI64:C_CI64 + 2] = (i[:, None] // 64 == np.arange(2)[None, :])
    c[:, C_CI8:C_CI8 + 16] = (i[:, None] // 8 == np.arange(16)[None, :])
    c[:, C_CMK64:C_CMK64 + 256] = np.tile((np.arange(2)[:, None] == i[None, :] // 64).reshape(1, 256), (128, 1))
    c[:, C_CMK8:C_CMK8 + 2048] = np.tile((np.arange(16)[:, None] == i[None, :] // 8).reshape(1, 2048), (128, 1))
    m = np.arange(383)
    dist = m - 127
    valid = (dist >= 0) & (dist < 128)
    bk = t5_bucket_np(dist)
    oh = np.zeros((33, 383), np.float32)
    oh[bk[valid], m[valid]] = 1.0
    oh[32, ~valid] = NEG
    c[:33, C_OH:C_OH + 383] = oh
    return c


_STOP = None
_SKIP = set()


def build():
    nc = bass.Bass("TRN2", target_bir_lowering=False)
    dt_in = lambda n, s: nc.dram_tensor(n, s, F32, kind="ExternalInput").ap()
    dt_out = lambda n, s: nc.dram_tensor(n, s, F32, kind="ExternalOutput").ap()
    xp = dt_in("xp", [NSLOT * 128, D])
    w_in = dt_in("w_in", [D, INW])
    w_out = dt_in("w_out", [D, D])
    ck = dt_in("ck", [16, 128, 256])
    cv = dt_in("cv", [16, 128, 256])
    sg = dt_in("sg", [16, 4, 128, 256])
    cstS_d = dt_in("cstS", [128, 163])
    cstB_d = dt_in("cstB", [128, NCB])
    ohm_d = dt_in("ohm", [33, 383])
    relb = dt_in("relb", [32, 16])
    npre = dt_in("npre", [1, D])
    npost = dt_in("npost", [1, D])
    wa2 = dt_in("wa2", [16, 512])
    ba = dt_in("ba", [1, 512])
    sinks = dt_in("sinks", [1, 16])
    gnorm = dt_in("gnorm", [1, 256])
    y_main = dt_out("y_main", [8 * 128, D])
    y_samp = dt_out("y_samp", [128, D])
    kwin = dt_out("kwin", [128, 256])
    vwin = dt_out("vwin", [128, 256])
    glast = dt_out("glast", [4, 128, 256])
    ks_out = dt_out("ks_out", [16, 128, 256])
    vs_out = dt_out("vs_out", [16, 128, 256])
    gs_out = dt_out("gs_out", [16, 4, 128, 256])
    fscr = nc.dram_tensor("fscr", [16, 128, 383], F32, kind="Internal").ap()

    def bcast_rows(ap2d, n):
        return bass.AP(ap2d.tensor, 0, [[0, 128], [1, n]])

    with ExitStack() as top:
        P = Prog(nc, top)
        sb = lambda st, name, shape, dt: st.enter_context(nc.sbuf_tensor(name, shape, dt, align_bytes=64))
        cstS = sb(top, "cstS_sb", [128, 163], F32)
        cstb = sb(top, "cstb", [128, NCB], BF16)
        esink = sb(top, "esink", [128, 16], F32)
        epst = sb(top, "epst", [128, 1], F32)
        wa2a = sb(top, "wa2a", [17, 512], BF16)
        Sst = sb(top, "Sst", [128, 4, 256], F32)
        Sbf = sb(top, "Sbf", [128, 4, 256], BF16)
        rstd_all = sb(top, "rstd_all", [128, NSLOT], F32)
        small = sb(top, "small", [128, 64], F32)
        zT = sb(top, "zT", [17, NMB * 128], BF16)
        xT_main = sb(top, "xT_main", [128, 16, NMB * 128], BF16)
        mixed = sb(top, "mixed", [128, NMB, D], BF16)
        kout = sb(top, "kout", [128, 2, 256], F32)
        vout = sb(top, "vout", [128, 2, 256], F32)
        pb = [top.enter_context(nc.psum_tensor("pb%d" % i, [128, 512], F32)) for i in range(8)]
        pbb = [t[:].bitcast(BF16) for t in pb]
        ident_b = cstb[:, C_ID:C_ID + 128]
        ident_f = cstS[:, 0:128]
        NVC = lambda slot: cstS[:, 128 + slot:129 + slot]
        HMC = cstS[:, 162:163]
        s_ld = [P.dma_sem("s_ld%d" % i) for i in range(2)]
        s_c = P.dma_sem("s_c")
        s_cp = P.dma_sem("s_cp")
        s_w = [P.dma_sem("s_w%d" % i) for i in range(2)]
        s_o = P.dma_sem("s_o")
        s_mf = [P.dma_sem("s_mf%d" % i) for i in range(4)]
        s_mb = [P.dma_sem("s_mb%d" % i) for i in range(4)]
        s_mo = [P.dma_sem("s_mo%d" % i) for i in range(4)]

        P.dma("sp", lambda e: e.dma_start(out=cstS[:], in_=cstS_d[:, :]), s_c, writes=["cstf"])
        P.dma("pool", lambda e: e.dma_start(out=cstb[:], in_=cstB_d[:, :]), s_w[0], writes=["cstb"])
        if "bcast" not in _SKIP:
            P.dma("sp", lambda e: e.dma_start(out=esink[:], in_=bcast_rows(sinks, 16)), s_c, writes=["esink"])
        if "wa2" not in _SKIP:
            P.dma("pool", lambda e: e.dma_start(out=wa2a[0:16, :], in_=wa2[:, :]), s_cp, writes=["wa2a"])
            P.dma("pool", lambda e: e.dma_start(out=wa2a[16:17, :], in_=ba[:, :]), s_cp, writes=["wa2a"])
        P.op("pool", lambda e: e.memset(epst[:], EPS), writes=["epst"])
        P.op("pool", lambda e: e.memset(Sst[:], 0.0), writes=["S"])
        P.op("pool", lambda e: e.memset(Sbf[:], 0.0), writes=["Sbf"])
        P.op("pool", lambda e: e.memset(zT[:], 1.0), writes=["zT"])
        P.op("act", lambda e: e.activation(out=esink[:], in_=esink[:], func=AF.Exp), reads=["esink"], writes=["esink"])
        with ExitStack() as st:
            rba = sb(st, "rba", [33, 16], F32)
            fes = sb(st, "fes", [16, 383], F32)
            ohm = sb(st, "ohm_sb", [33, 383], F32)
            P.dma("sp", lambda e: e.dma_start(out=ohm[:], in_=ohm_d[:, :]), s_c, writes=["ohm"])
            P.op("pool", lambda e: e.memset(rba[:], 1.0), writes=["rba"])
            P.dma("sp", lambda e: e.dma_start(out=rba[0:32, :], in_=relb[:, :]), s_c, reads=[], writes=["rba"])
            if "f32mm" not in _SKIP:
                P.op("pe", lambda e: e.matmul(pb[0][0:16, 0:383], lhsT=rba[0:33, :], rhs=ohm[:, :],
                                             start=True, stop=True), reads=["rba", "ohm"], writes=[("ps", 0)])
                P.op("dve", lambda e: e.tensor_copy(out=fes[:], in_=pb[0][0:16, 0:383]), reads=[("ps", 0)], writes=["fes"])
            if "fscr" not in _SKIP:
                P.dma("sp", lambda e: e.dma_start(out=fscr[:, :, :], in_=fes[:].unsqueeze(1).broadcast_to([16, 128, 383])),
                      s_c, reads=["fes"], writes=["fscr"])
            P.flush()
            if _STOP == 0:
                return nc

        cnt = {"x": 0, "ev": 0}

        def xprep(slot, dst, dkey, par=0):
            i = cnt["x"] % 2
            cnt["x"] += 1
            P.dma("sp", lambda e: e.dma_start(out=xs[i][:], in_=xp[slot * 128:(slot + 1) * 128, :]), s_ld[i], writes=[("xs", i)])
            P.op("act", lambda e: e.activation(out=junk[:], in_=xs[i][:], func=AF.Square, accum_out=small[:, 2 * par:2 * par + 1]),
                 reads=[("xs", i)], writes=["junk", ("small", 2 * par)])
            P.op("act", lambda e: e.activation(out=small[:, 2 * par + 1:2 * par + 2], in_=small[:, 2 * par:2 * par + 1], func=AF.Ln, bias=epst[:, 0:1], scale=1.0 / D),
                 reads=[("small", 2 * par), "epst"], writes=[("small", 2 * par + 1)])
            P.op("act", lambda e: e.activation(out=rstd_all[:, slot:slot + 1], in_=small[:, 2 * par + 1:2 * par + 2], func=AF.Exp, scale=-0.5),
                 reads=[("small", 2 * par + 1)], writes=[("rstd", slot)])
            P.op("dve", lambda e: e.scalar_tensor_tensor(out=xb[par][:], in0=xs[i][:], scalar=rstd_all[:, slot:slot + 1], in1=npre_b[:],
                                                         op0=ALU.mult, op1=ALU.mult),
                 reads=[("xs", i), ("rstd", slot), "npre_b"], writes=[("xb", par)])
            for half in range(2):
                for j in range(8):
                    kc = half * 8 + j
                    P.op("pe", lambda e, kc=kc, j=j: e.transpose(out=pbb[2][:, j * 128:(j + 1) * 128],
                                                                in_=xb[par][:, kc * 128:(kc + 1) * 128], identity=ident_b),
                         reads=[("xb", par), "cstb"], writes=[("ps", 2)], sig=(j == 7))
                src = pbb[2][:, :].rearrange("p (a b) -> p a b", b=128)
                if half == 0:
                    P.op("act", lambda e, half=half, src=src: e.activation(out=dst[:, half * 8:(half + 1) * 8, :], in_=src, func=AF.Copy),
                         reads=[("ps", 2)], writes=[dkey])
                else:
                    P.op("dve", lambda e, half=half, src=src: e.tensor_copy(out=dst[:, half * 8:(half + 1) * 8, :], in_=src),
                         reads=[("ps", 2)], writes=[dkey])

        def wload(dst, dkey, segs, sem):
            off = 0
            for (c0, n) in segs:
                P.dma("pool", lambda e, c0=c0, n=n, off=off: e.dma_start(
                    out=dst[:, :, off:off + n], in_=w_in[:, c0:c0 + n].rearrange("(kc p) n -> p kc n", p=128)),
                    sem, writes=[dkey])
                off += n

        def mm16(out_ap, okey, lhs_fn, rhs_fn, rkeys):
            for kc in range(16):
                P.op("pe", lambda e, kc=kc: e.matmul(out_ap, lhsT=lhs_fn(kc), rhs=rhs_fn(kc), start=(kc == 0), stop=(kc == 15)),
                     reads=rkeys, writes=[okey], sig=(kc == 15))

        def evac(fn_act, fn_dve, reads, writes):
            cnt["ev"] += 1
            if cnt["ev"] % 2:
                P.op("act", fn_act, reads=reads, writes=writes)
            else:
                P.op("dve", fn_dve, reads=reads, writes=writes)

        def copy_evac(dst, src, reads, writes, scale=None):
            if scale is None:
                evac(lambda e: e.activation(out=dst, in_=src, func=AF.Copy), lambda e: e.tensor_copy(out=dst, in_=src), reads, writes)
            else:
                evac(lambda e: e.mul(dst, src, scale),
                     lambda e: e.tensor_scalar(out=dst, in0=src, scalar1=scale, scalar2=None, op0=ALU.mult), reads, writes)

        pcnt = {"b": 0}

        def nbank():
            pcnt["b"] ^= 1
            return pcnt["b"]

        def gla_block(T, heads, zT_ap, zkey, ktm, ktm_key, vtm, vtm_key, nvcol, C,
                      qT, kT, fkeys, sgg, sgg_key, mixed_dst, mixed_key, sample=False, stage="all", par=0):
            nch = 128 // C
            Uo, CIo, CMKo = (C_U64, C_CI64, C_CMK64) if C == 64 else (C_U8, C_CI8, C_CMK8)
            U = cstb[:, Uo:Uo + 128]
            CI = cstb[:, CIo:CIo + nch]
            CMK = cstb[:, CMKo:CMKo + nch * 128].rearrange("p (c t) -> p c t", t=128)
            h0 = heads[0]
            kpx, qpx, atm, el = T["kpx"][par], T["qpx"][par], T["atm"][par], T["el"][par]
            if stage in ("all", 1):
                P.op("pe", lambda e: e.matmul(pb[3][:, 0:256], lhsT=zT_ap, rhs=wa2a[0:17, h0 * 128:h0 * 128 + 256], start=True, stop=True),
                     reads=[zkey, "wa2a"], writes=[("ps", 3)])
                P.op("act", lambda e: e.activation(out=T["lf"][:], in_=pb[3][:, 0:256], func=AF.Exp, scale=-1.0),
                     reads=[("ps", 3)], writes=["lf"])
                P.op("act", lambda e: e.activation(out=T["lf"][:], in_=T["lf"][:], func=AF.Ln, bias=1.0), reads=["lf"], writes=["lf"])
                P.op("dve", lambda e: e.tensor_scalar(out=T["lfb"][:], in0=T["lf"][:], scalar1=nvcol, scalar2=None, op0=ALU.mult),
                     reads=["lf", "cstf"], writes=["lfb"])
                P.op("pe", lambda e: e.matmul(pb[3][:, 0:256], lhsT=U, rhs=T["lfb"][:], start=True, stop=True),
                     reads=["lfb", "cstb"], writes=[("ps", 3)])
                P.op("act", lambda e: e.activation(out=T["einv"][:], in_=pb[3][:, 0:256], func=AF.Exp, scale=-1.0),
                     reads=[("ps", 3)], writes=["einv"])
                P.op("dve", lambda e: e.tensor_tensor(out=T["kp"][:], in0=ktm, in1=T["einv"][:], op=ALU.mult),
                     reads=[ktm_key, "einv"], writes=["kp"])
                for hh in range(2):
                    P.op("pe", lambda e, hh=hh: e.matmul(pb[4][:, 256 + hh * 16:256 + hh * 16 + nch], lhsT=T["lfb"][:, hh * 128:(hh + 1) * 128],
                                                        rhs=CI, start=True, stop=True),
                         reads=["lfb", "cstb"], writes=[("ps", 4, "bl")], sig=(hh == 1))
                P.op("act", lambda e: e.activation(out=el[:], in_=pb[4][:, 256:288], func=AF.Exp), reads=[("ps", 4, "bl")], writes=[("el", par)])
                for hh in range(2):
                    P.op("pe", lambda e, hh=hh: e.matmul(pb[4][:, hh * 128:(hh + 1) * 128], lhsT=T["lfb"][:, hh * 128:(hh + 1) * 128],
                                                        rhs=U, start=True, stop=True),
                         reads=["lfb", "cstb"], writes=[("ps", 4, "bf")], sig=(hh == 1))
                P.op("act", lambda e: e.activation(out=T["efm"][:], in_=pb[4][:, 0:256], func=AF.Exp), reads=[("ps", 4, "bf")], writes=["efm"])
                P.op("act", lambda e: e.activation(out=T["eifm"][:], in_=pb[4][:, 0:256], func=AF.Exp, scale=-1.0),
                     reads=[("ps", 4, "bf")], writes=["eifm"])
                for hh in range(2):
                    P.op("dve", lambda e, hh=hh: e.scalar_tensor_tensor(out=T["qpT"][:, hh, :], in0=qT[hh], scalar=128.0 ** -0.5,
                                                                       in1=T["efm"][:, hh * 128:(hh + 1) * 128], op0=ALU.mult, op1=ALU.mult),
                         reads=list(fkeys) + ["efm"], writes=[("qpT", hh)])
                    P.op("pool", lambda e, hh=hh: e.tensor_tensor(out=T["kpT"][:, hh, :], in0=kT[hh], in1=T["eifm"][:, hh * 128:(hh + 1) * 128],
                                                                 op=ALU.mult),
                         reads=list(fkeys) + ["eifm"], writes=[("kpT", hh)])
                for hh in range(2):
                    P.op("pool", lambda e, hh=hh: e.tensor_tensor(
                        out=kpx[:, hh, 0:nch, :], in0=T["kp"][:, hh * 128:(hh + 1) * 128].unsqueeze(1).broadcast_to([128, nch, 128]),
                        in1=CI.unsqueeze(2).broadcast_to([128, nch, 128]), op=ALU.mult),
                        reads=["kp", "cstb"], writes=[("kpx", par, hh)])
                    abk = 5 if hh == 0 else 2
                    P.op("pe", lambda e, hh=hh, abk=abk: e.matmul(pb[abk][:, 0:128], lhsT=T["kpT"][:, hh, :], rhs=T["qpT"][:, hh, :],
                                                                 start=True, stop=True),
                         reads=[("kpT", hh), ("qpT", hh)], writes=[("ps", abk)])
                    P.op("dve", lambda e, hh=hh, abk=abk: e.tensor_tensor(out=atm[:, hh, :], in0=pb[abk][:, 0:128], in1=U, op=ALU.mult),
                         reads=[("ps", abk), "cstb"], writes=[("atm", par, hh)])
                    P.op("dve", lambda e, hh=hh: e.tensor_tensor(
                        out=qpx[:, hh, 0:nch, :], in0=T["qpT"][:, hh, :].unsqueeze(1).broadcast_to([128, nch, 128]), in1=CMK, op=ALU.mult),
                        reads=[("qpT", hh), "cstb"], writes=[("qpx", par, hh)])
            if stage not in ("all", 2):
                return
            for hh in range(2):
                h = heads[hh]
                obk = 6 if hh == 0 else 0
                tbk = 7 if hh == 0 else 1
                ob = pb[obk][:, 0:256]
                P.op("pe", lambda e, hh=hh, ob=ob: e.matmul(ob, lhsT=atm[:, hh, :], rhs=vtm[:, hh * 256:(hh + 1) * 256], start=True, stop=False),
                     reads=[("atm", par, hh), vtm_key], writes=[("ps", obk)], sig=False)

                def s_load(c, h=h, hh=hh):
                    i = (c + hh) % 4
                    s0f, s0b = T["s0f"][i], T["s0b"][i]
                    P.dma("sp", lambda e, c=c, h=h, s0f=s0f: e.dma_start(out=s0f[:], in_=sg[c, h, :, :]), s_mf[i], writes=[("s0f", i)])
                    P.dma("pool", lambda e, c=c, h=h, s0b=s0b: e.dma_start(out=s0b[:], in_=sg[c, h, :, :]), s_mb[i], writes=[("s0b", i)])
                for c in range(nch):
                    if sample:
                        if c == 0:
                            for c2 in range(3):
                                s_load(c2)
                        if c + 3 < nch:
                            s_load(c + 3)
                        i = (c + hh) % 4
                        s0f, s0b = T["s0f"][i], T["s0b"][i]
                        Sf, Sb, skf, skb = s0f[:], s0b[:], ("s0f", i), ("s0b", i)
                    else:
                        Sf, Sb, skf, skb = Sst[:, h, :], Sbf[:, h, :], ("S", h), ("Sbf", h)
                    P.op("pe", lambda e, hh=hh, c=c, Sb=Sb, ob=ob: e.matmul(ob, lhsT=qpx[:, hh, c, :], rhs=Sb, start=False, stop=(c == nch - 1)),
                         reads=[("qpx", par, hh), skb], writes=[("ps", obk)])
                    P.op("pe", lambda e, hh=hh, c=c, tbk=tbk: e.matmul(pb[tbk][:, 0:256], lhsT=kpx[:, hh, c, :], rhs=vtm[:, hh * 256:(hh + 1) * 256],
                                                                      start=True, stop=True),
                         reads=[("kpx", par, hh), vtm_key], writes=[("ps", tbk)])
                    ecol = el[:, hh * 16 + c:hh * 16 + c + 1]
                    P.op("act", lambda e, Sf=Sf, ecol=ecol: e.activation(out=T["se"][:], in_=Sf, func=AF.Copy, scale=ecol),
                         reads=[skf, ("el", par)], writes=["se"])
                    P.op("dve", lambda e, ecol=ecol, Sf=Sf, tbk=tbk: e.scalar_tensor_tensor(out=Sf, in0=pb[tbk][:, 0:256], scalar=ecol, in1=T["se"][:],
                                                                                          op0=ALU.mult, op1=ALU.add),
                         reads=[("ps", tbk), ("el", par), "se"], writes=[skf])
                    if sample:
                        P.dma("sp", lambda e, c=c, h=h, Sf=Sf: e.dma_start(out=gs_out[c, h, :, :], in_=Sf), s_mo[i], reads=[skf])
                    else:
                        P.op("act", lambda e, Sf=Sf, Sb=Sb: e.activation(out=Sb, in_=Sf, func=AF.Copy), reads=[skf], writes=[skb])
                ssq = small[:, 8 + hh:9 + hh]
                P.op("act", lambda e, ob=ob, ssq=ssq: e.activation(out=junk[:, 0:256], in_=ob, func=AF.Square, accum_out=ssq),
                     reads=[("ps", obk)], writes=["junk", ("small", 8 + hh)])
                P.op("act", lambda e, ssq=ssq: e.activation(out=ssq, in_=ssq, func=AF.Ln, bias=epst[:, 0:1], scale=1.0 / 256),
                     reads=[("small", 8 + hh), "epst"], writes=[("small", 8 + hh)])
                P.op("act", lambda e, ssq=ssq: e.activation(out=ssq, in_=ssq, func=AF.Exp, scale=-0.5),
                     reads=[("small", 8 + hh)], writes=[("small", 8 + hh)])
                P.op("dve", lambda e, hh=hh, ob=ob, ssq=ssq: e.scalar_tensor_tensor(
                    out=mixed_dst[:, hh * 256:(hh + 1) * 256], in0=ob, scalar=ssq, in1=sgg[:, hh * 256:(hh + 1) * 256], op0=ALU.mult, op1=ALU.mult),
                    reads=[("ps", obk), ("small", 8 + hh), sgg_key], writes=[mixed_key])

        def gla_tiles(st):
            sbq = lambda name, shape, dt: st.enter_context(nc.sbuf_tensor(name, shape, dt, align_bytes=64))
            T = {}
            T["lf"] = sbq("g_lf", [128, 256], F32)
            T["lfb"] = sbq("g_lfb", [128, 256], BF16)
            T["einv"] = sbq("g_einv", [128, 256], F32)
            T["kp"] = sbq("g_kp", [128, 256], BF16)
            el0 = sbq("g_el0", [128, 32], F32)
            kpx0 = sbq("g_kpx0", [128, 2, 16, 128], BF16)
            T["se"] = sbq("g_se", [128, 256], F32)
            T["efm"] = sbq("g_efm", [128, 256], F32)
            T["eifm"] = sbq("g_eifm", [128, 256], F32)
            T["qpT"] = sbq("g_qpT", [128, 2, 128], BF16)
            T["kpT"] = sbq("g_kpT", [128, 2, 128], BF16)
            atm0 = sbq("g_atm0", [128, 2, 128], BF16)
            qpx0 = sbq("g_qpx0", [128, 2, 16, 128], BF16)
            T["s0f"] = [sbq("g_s0f%d" % i, [128, 256], F32) for i in range(4)]
            T["s0b"] = [sbq("g_s0b%d" % i, [128, 256], BF16) for i in range(4)]
            T["el"] = [el0, sbq("g_el1", [128, 32], F32)]
            T["kpx"] = [kpx0, sbq("g_kpx1", [128, 2, 2, 128], BF16)]
            T["qpx"] = [qpx0, sbq("g_qpx1", [128, 2, 2, 128], BF16)]
            T["atm"] = [atm0, sbq("g_atm1", [128, 2, 128], BF16)]
            return T

        st12 = ExitStack()
        npre_b = sb(st12, "npre_b", [128, D], F32)
        xs = [sb(st12, "xs%d" % i, [128, D], F32) for i in range(2)]
        xb = [sb(st12, "xb%d" % i, [128, D], BF16) for i in range(2)]
        junk = sb(st12, "junk", [128, D], BF16)
        P.dma("sp", lambda e: e.dma_start(out=npre_b[:], in_=bcast_rows(npre, D)), s_c, writes=["npre_b"])
        with ExitStack() as st:
            wst = sb(st, "wst", [128, 16, 1552], BF16)
            xTb = [sb(st, "xTb%d" % i, [128, 16, 128], BF16) for i in range(2)]
            vtm = [sb(st, "p_vtm%d" % i, [128, 1024], BF16) for i in range(2)]
            lf = [sb(st, "p_lf%d" % i, [128, 512], F32) for i in range(2)]
            lfb = [sb(st, "p_lfb%d" % i, [128, 512], BF16) for i in range(2)]
            zTp = [sb(st, "p_zT%d" % i, [17, 128], BF16) for i in range(2)]
            ew = sb(st, "p_ew", [128, 512], F32)
            kpp = sb(st, "p_kpp", [128, 512], BF16)
            el = sb(st, "p_el", [128, 8], F32)
            US = cstb[:, C_US:C_US + 128]
            ONE = cstb[:, C_ONE:C_ONE + 2]
            for i in range(2):
                P.op("pool", lambda e, i=i: e.memset(zTp[i][:], 1.0), writes=[("zTp", i)])
            for (c0, n, off, key, sem) in ((Z_OFF, 16, 1536, ("wst", "z"), s_cp), (KG_OFF, 512, 0, ("wst", "k"), s_w[0]),
                                           (VG_OFF, 512, 512, ("wst", "v0"), s_w[1]), (VG_OFF + 512, 512, 1024, ("wst", "v1"), s_w[0])):
                P.dma("pool", lambda e, c0=c0, n=n, off=off: e.dma_start(
                    out=wst[:, :, off:off + n], in_=w_in[:, c0:c0 + n].rearrange("(kc p) n -> p kc n", p=128)), sem, writes=[key])
            NPS = 0 if 'only5' in _SKIP else NPRE

            def xt_of(slot):
                if slot == NPRE - 1:
                    return xT_main[:, :, 0:128], ("xT", 0)
                return xTb[slot % 2][:], ("xTb", slot % 2)

            def stAB(slot):
                par = slot % 2
                xt, xkey = xt_of(slot)
                xprep(slot, xt, xkey, par)
                mm16(pb[par][:, :], ("ps", par), lambda kc: xt[:, kc, :], lambda kc: wst[:, kc, 0:512], [xkey, ("wst", "k")])
                mm16(pb[4][0:16, 0:128], ("ps", 4, "z"), lambda kc: wst[:, kc, 1536:1552], lambda kc: xt[:, kc, :], [xkey, ("wst", "z")])
                copy_evac(zTp[par][0:16, :], pb[4][0:16, 0:128], [("ps", 4, "z")], [("zTp", par)])
                P.op("pe", lambda e: e.matmul(pb[3][:, :], lhsT=zTp[par][0:17, :], rhs=wa2a[0:17, :], start=True, stop=True),
                     reads=[("zTp", par), "wa2a"], writes=[("ps", 3)])

            def stC(slot):
                par = slot % 2
                for hf in range(2):
                    P.op("act", lambda e, hf=hf: e.activation(out=lf[par][:, hf * 256:(hf + 1) * 256], in_=pb[3][:, hf * 256:(hf + 1) * 256], func=AF.Exp, scale=-1.0),
                         reads=[("ps", 3)], writes=[("lf", par, hf)])
                P.op("act", lambda e: e.activation(out=lf[par][:], in_=lf[par][:], func=AF.Ln, bias=1.0), reads=[("lf", par)], writes=[("lf", par)])
                P.op("dve", lambda e: e.tensor_scalar(out=lfb[par][:], in0=lf[par][:], scalar1=NVC(slot), scalar2=None, op0=ALU.mult),
                     reads=[("lf", par)], writes=[("lfb", par)])

            def stD(slot):
                par = slot % 2
                xt, xkey = xt_of(slot)
                for pi in range(2):
                    mm16(pb[6][:, :], ("ps", 6), lambda kc: xt[:, kc, :], lambda kc, pi=pi: wst[:, kc, 512 + pi * 512:1024 + pi * 512], [xkey, ("wst", "v%d" % pi)])
                    copy_evac(vtm[par][:, pi * 512:(pi + 1) * 512], pb[6][:, :], [("ps", 6)], [("vtm", par, pi)])

            def stE(slot):
                par = slot % 2
                P.op("pe", lambda e: e.matmul(pb[3][:, :], lhsT=US, rhs=lfb[par][:], start=True, stop=True),
                     reads=[("lfb", par), "cstb"], writes=[("ps", 3)])
                for h in range(4):
                    P.op("pe", lambda e, h=h: e.matmul(pb[4][:, 200 + 2 * h:202 + 2 * h], lhsT=lfb[par][:, h * 128:(h + 1) * 128], rhs=ONE, start=True, stop=True),
                         reads=[("lfb", par), "cstb"], writes=[("ps", 4, "bl")], sig=(h == 3))
                for hf in range(2):
                    P.op("act", lambda e, hf=hf: e.activation(out=ew[:, hf * 256:(hf + 1) * 256], in_=pb[3][:, hf * 256:(hf + 1) * 256], func=AF.Exp),
                         reads=[("ps", 3)], writes=[("ew", hf)])
                P.op("act", lambda e: e.activation(out=el[:], in_=pb[4][:, 200:208], func=AF.Exp), reads=[("ps", 4, "bl")], writes=["el"])
                P.op("dve", lambda e: e.tensor_tensor(out=kpp[:], in0=pb[par][:, :], in1=ew[:], op=ALU.mult),
                     reads=[("ps", par), "ew"], writes=["kpp"])

            def stF(slot):
                par = slot % 2
                for half in range(2):
                    bnk = 5 if half == 0 else 7
                    for hh in range(2):
                        h = half * 2 + hh
                        P.op("pe", lambda e, h=h, hh=hh, bnk=bnk: e.matmul(pb[bnk][:, hh * 256:(hh + 1) * 256], lhsT=kpp[:, h * 128:(h + 1) * 128],
                                                                         rhs=vtm[par][:, h * 256:(h + 1) * 256], start=True, stop=True),
                             reads=["kpp", ("vtm", par)], writes=[("ps", bnk, hh)])
                    for hh in range(2):
                        h = half * 2 + hh
                        P.op("dve", lambda e, h=h, hh=hh, bnk=bnk: e.scalar_tensor_tensor(
                            out=Sst[:, h, :], in0=Sst[:, h, :], scalar=el[:, 2 * h:2 * h + 1], in1=pb[bnk][:, hh * 256:(hh + 1) * 256], op0=ALU.mult, op1=ALU.add),
                            reads=[("S", h), "el", ("ps", bnk, hh)], writes=[("S", h)])

            if NPS:
                stAB(0); stC(0); stD(0)
            for slot in range(NPS):
                if slot + 1 < NPS:
                    stAB(slot + 1); stC(slot + 1)
                if "pxE" not in _SKIP:
                    stE(slot)
                if slot + 1 < NPS:
                    stD(slot + 1)
                if "pxE" not in _SKIP and "pxF" not in _SKIP:
                    stF(slot)
            P.op("act", lambda e: e.activation(out=Sbf[:], in_=Sst[:], func=AF.Copy), reads=["S"], writes=["Sbf"])
            P.flush()
        if _STOP == 1:
            st12.close()
            return nc

        for mb in range(1, 1 if 'only5' in _SKIP else NMB):
            xprep(NPRE - 1 + mb, xT_main[:, :, mb * 128:(mb + 1) * 128], ("xT", mb), mb % 2)
        P.flush()
        st12.close()
        if _STOP == 2:
            return nc

        TOKG = [(128, 512), (640, 512), (1152, 128)]

        def fproj(wsl, wkey, lhs_fn, M, tok_groups, consumer):
            for (t0, n) in tok_groups:
                b = nbank()
                mm16(pb[b][0:M, 0:n], ("ps", b), lhs_fn, lambda kc, t0=t0, n=n: xT_main[:, kc, t0:t0 + n], ["xT", wkey])
                consumer(pb[b][0:M, 0:n], ("ps", b), t0, n)

        def tproj(wsl, wkey, c0, ncols, blocks, consumer):
            for mb in blocks:
                b = nbank()
                mm16(pb[b][:, 0:ncols], ("ps", b), lambda kc, mb=mb: xT_main[:, kc, mb * 128:(mb + 1) * 128],
                     lambda kc: wsl[:, kc, c0:c0 + ncols], [("xT", mb), wkey])
                consumer(pb[b][:, 0:ncols], ("ps", b), mb)

        cnt["ev"] = 100
        with ExitStack() as st:
            wsl = [sb(st, "wsl%d" % i, [128, 16, 512], BF16) for i in range(2)]
            QT = sb(st, "QT", [128, 4, NMB * 128], BF16)
            KT = sb(st, "KT", [128, NMB * 128], BF16)
            Va = sb(st, "Va", [128, NMB, 2, 80], BF16)
            sga = sb(st, "sga", [128, NMB, 512], BF16)
            KcT = sb(st, "KcT", [128, 16, 128], BF16)
            Vc = sb(st, "Vc", [128, 16, 2, 80], BF16)
            kcs = sb(st, "kcs", [128, 16, 128], BF16)
            stt = [sb(st, "stt%d" % i, [128, 512], F32) for i in range(2)]
            ptt = [sb(st, "ptt%d" % i, [128, 512], BF16) for i in range(4)]
            oTs = [sb(st, "oTs%d" % i, [65, 512], F32) for i in range(2)]
            zcol = sb(st, "zcol", [128, 1], F32)
            BTp = sb(st, "BTp", [128, 16, 128], F32)
            BTc = sb(st, "BTc", [128, 16, 128], F32)
            BTs = sb(st, "BTs", [128, 16, 16, 8], F32)
            P.op("pool", lambda e: e.memset(BTs[:], NEG), writes=["BTs"])
            if "p3bt" not in _SKIP:
                P.dma("sp", lambda e: e.dma_start(out=BTc[:], in_=bass.AP(fscr.tensor, 127, [[382, 128], [128 * 383, 16], [1, 128]])),
                      s_c, writes=["BTc"])
                P.dma("sp", lambda e: e.dma_start(out=BTp[:], in_=bass.AP(fscr.tensor, 255, [[382, 128], [128 * 383, 16], [1, 128]])),
                      s_c, writes=["BTp"])
                for j in range(16):
                    P.dma("sp", lambda e, j=j: e.dma_start(
                        out=BTs[8 * j:8 * j + 8, j, :, :],
                        in_=bass.AP(fscr.tensor, 127, [[382, 8], [128 * 383, 16], [1, 8]])), s_c, writes=["BTs"])
            P.op("pool", lambda e: e.memset(Va[:], 1.0), writes=["Va"])
            P.op("pool", lambda e: e.memset(Vc[:], 1.0), writes=["Vc"])
            P.op("pool", lambda e: e.memset(zcol[:], 0.0), writes=["zcol"])
            wi = 0
            for p in range(0 if 'only5' in _SKIP else 2):
                w, wk, sem = wsl[wi % 2], ("wsl", wi % 2), s_w[wi % 2]; wi += 1
                wload(w, wk, [(Q_OFF + (2 * p + kvl_) * 256 + g_ * 64, 64) for g_ in range(4) for kvl_ in range(2)], sem)
                for g in range(4):
                    if "p3q" in _SKIP:
                        continue
                    lhs = lambda kc, w=w, g=g: w[:, kc, g * 128:(g + 1) * 128]
                    fproj(w, wk, lhs, 128, TOKG,
                          lambda ps, pk, t0, n, g=g: copy_evac(QT[:, g, t0:t0 + n], ps, [pk], [("QT", g)], scale=0.125))
                w, wk, sem = wsl[wi % 2], ("wsl", wi % 2), s_w[wi % 2]; wi += 1
                wload(w, wk, [(K_OFF + p * 128, 128), (V_OFF + p * 128, 128)], sem)
                if "p3k" not in _SKIP:
                    fproj(w, wk, lambda kc, w=w: w[:, kc, 0:128], 128, [(0, 512), (512, 512), (1024, 256)],
                          lambda ps, pk, t0, n: copy_evac(KT[:, t0:t0 + n], ps, [pk], ["KT"]))

                def kv_cons(ps, pk, mb, p=p):
                    if "p3va" not in _SKIP:
                        for a_ in range(2):
                            copy_evac(Va[:, mb, a_, 0:64], ps[:, 128 + a_ * 64:128 + (a_ + 1) * 64], [pk], [("Va", mb, a_)])
                    if mb >= 8 and "p3ko" not in _SKIP:
                        P.op("dve", lambda e: e.tensor_copy(out=kout[:, mb - 8, p * 128:(p + 1) * 128], in_=ps[:, 0:128]), reads=[pk], writes=["kout"])
                        P.op("dve", lambda e: e.tensor_copy(out=vout[:, mb - 8, p * 128:(p + 1) * 128], in_=ps[:, 128:256]), reads=[pk], writes=["vout"])
                if "p3kv" not in _SKIP:
                    tproj(w, wk, 0, 256, range(NMB), kv_cons)
                w, wk, sem = wsl[wi % 2], ("wsl", wi % 2), s_w[wi % 2]; wi += 1
                wload(w, wk, [(GA_OFF + p * 512, 512)], sem)
                if "p3g" not in _SKIP:
                    tproj(w, wk, 0, 512, range(1, NMB),
                          lambda ps, pk, mb: P.op("act", lambda e: e.activation(out=sga[:, mb, :], in_=ps, func=AF.Silu), reads=[pk], writes=[("sga", mb)]))
                if "p3cache" in _SKIP:
                    continue
                P.dma("pool", lambda e, p=p: e.dma_start(out=kcs[:], in_=ck[:, :, p * 128:(p + 1) * 128].rearrange("j s c -> s j c")), s_cp, writes=["kcs"])
                for a_ in range(2):
                    P.dma("pool", lambda e, p=p, a_=a_: e.dma_start(
                        out=Vc[:, :, a_, 0:64], in_=cv[:, :, p * 128 + a_ * 64:p * 128 + (a_ + 1) * 64].rearrange("j s d -> s j d")),
                        s_cp, writes=["Vc"])
                for jj in range(2):
                    for j8 in range(8):
                        j = jj * 8 + j8
                        P.op("pe", lambda e, j=j, j8=j8: e.transpose(out=pbb[2][:, j8 * 128:(j8 + 1) * 128], in_=kcs[:, j, :], identity=ident_b),
                             reads=["kcs", "cstb"], writes=[("ps", 2)], sig=(j8 == 7))
                    copy_evac(KcT[:, jj * 8:(jj + 1) * 8, :], pbb[2][:, :].rearrange("p (a b) -> p a b", b=128), [("ps", 2)], ["KcT"])

                def unit_s1(kvl, kprev, kprev_key, kcur, qfn, nq, bprev, bcur, hmcol, vprev, vprev_key, vcur, ocol0, first, last, ui):
                    pr = slice(kvl * 64, kvl * 64 + 64)
                    for kb, (kt, kk, bias) in enumerate(((kprev, kprev_key, bprev), (kcur, "KT", bcur))):
                        sbk = 3 + kb
                        spsum = pb[sbk][:, 0:4 * nq].rearrange("p (g q) -> p g q", g=4)
                        for g in range(4):
                            P.op("pe", lambda e, g=g, kt=kt, spsum=spsum: e.matmul(spsum[:, g, :], lhsT=kt[pr, :], rhs=qfn(g)[pr, :], start=True, stop=True),
                                 reads=[kk, ("QT", g)], writes=[("ps", sbk)], sig=(g == 3))
                        stv = stt[kb][:, 0:4 * nq].rearrange("p (g q) -> p g q", g=4)
                        P.op("dve", lambda e, spsum=spsum, bias=bias, stv=stv, kb=kb: e.scalar_tensor_tensor(
                            out=stv, in0=spsum, scalar=(hmcol if kb == 0 else zcol[:, 0:1]), in1=bias, op0=ALU.add, op1=ALU.add),
                            reads=[("ps", sbk), "BTp", "BTc", "BTs", "cstf", "zcol"], writes=[("stt", kb)])
                        pi = (ui % 2) * 2 + kb
                        P.op("act", lambda e, pi=pi, kb=kb: e.activation(out=ptt[pi][:, 0:4 * nq], in_=stt[kb][:, 0:4 * nq], func=AF.Exp),
                             reads=[("stt", kb)], writes=[("ptt", pi)])

                def unit_s2(kvl, kprev, kprev_key, kcur, qfn, nq, bprev, bcur, hmcol, vprev, vprev_key, vcur, ocol0, first, last, ui):
                    for kb, (vv, vk) in enumerate(((vprev, vprev_key), (vcur, "Va"))):
                        pi = (ui % 2) * 2 + kb
                        oap = pb[5][0:65, ocol0:ocol0 + 4 * nq]
                        P.op("pe", lambda e, vv=vv, pi=pi, oap=oap, kb=kb: e.matmul(
                            oap, lhsT=vv, rhs=ptt[pi][:, 0:4 * nq], start=(kb == 0), stop=(kb == 1)),
                            reads=[vk, ("ptt", pi)], writes=[("ps", 5)])

                def fin_a(kvl, mb, fi):
                    ot = oTs[fi % 2]
                    if mb == 9:
                        P.op("act", lambda e: e.activation(out=ot[:, :].rearrange("p (g j t) -> p g j t", g=4, j=16),
                                                           in_=pb[5][0:65, :].rearrange("p (j g t) -> p g j t", j=16, g=4), func=AF.Copy),
                             reads=[("ps", 5)], writes=[("oTs", fi % 2)])
                    else:
                        P.op("act", lambda e: e.activation(out=ot[:], in_=pb[5][0:65, :], func=AF.Copy), reads=[("ps", 5)], writes=[("oTs", fi % 2)])

                def fin_b(kvl, mb, fi, p=p):
                    ot = oTs[fi % 2]
                    for g in range(4):
                        P.op("pe", lambda e, g=g: e.transpose(out=pb[6][:, g * 65:(g + 1) * 65], in_=ot[0:65, g * 128:(g + 1) * 128],
                                                             identity=ident_f[0:65, 0:65]),
                             reads=[("oTs", fi % 2), "cstf"], writes=[("ps", 6)], sig=(g == 3))
                    for g in range(4):
                        h = (2 * p + kvl) * 4 + g
                        rc = small[:, 16 + g:17 + g]
                        P.op("dve", lambda e, g=g, h=h, rc=rc: e.tensor_tensor(out=rc, in0=pb[6][:, g * 65 + 64:g * 65 + 65], in1=esink[:, h:h + 1], op=ALU.add),
                             reads=[("ps", 6), "esink"], writes=[("small", 16 + g)])
                        P.op("dve", lambda e, rc=rc: e.reciprocal(out=rc, in_=rc), reads=[("small", 16 + g)], writes=[("small", 16 + g)])
                        P.op("dve", lambda e, g=g, h=h, rc=rc: e.scalar_tensor_tensor(
                            out=mixed[:, mb, h * 64:(h + 1) * 64], in0=pb[6][:, g * 65:g * 65 + 64], scalar=rc,
                            in1=sga[:, mb, kvl * 256 + g * 64:kvl * 256 + (g + 1) * 64], op0=ALU.mult, op1=ALU.mult),
                            reads=[("ps", 6), ("small", 16 + g), ("sga", mb)], writes=[("mixed", mb)])

                def samp_s1(kvl, kv, ui):
                    pr = slice(kvl * 64, kvl * 64 + 64)
                    for kb in range(2):
                        sbk = 3 + kb
                        for j in range(16):
                            kt = KcT[:, j, :] if kb == 0 else KT[:, 9 * 128:10 * 128]
                            for g in range(4):
                                P.op("pe", lambda e, g=g, j=j, kt=kt, sbk=sbk: e.matmul(
                                    pb[sbk][:, j * 32 + g * 8:j * 32 + g * 8 + 8], lhsT=kt[pr, :],
                                    rhs=QT[pr, g, 9 * 128 + 8 * j:9 * 128 + 8 * j + 8], start=True, stop=True),
                                    reads=["KcT" if kb == 0 else "KT", ("QT", g)], writes=[("ps", sbk)], sig=(j == 15 and g == 3))
                        for j in range(16):
                            bias = BTp[:, kv * 4:kv * 4 + 4, 0:8] if kb == 0 else BTs[:, j, kv * 4:kv * 4 + 4, :]
                            P.op("dve", lambda e, j=j, bias=bias, sbk=sbk, kb=kb: e.scalar_tensor_tensor(
                                out=stt[kb][:, j * 32:(j + 1) * 32].rearrange("p (g q) -> p g q", g=4),
                                in0=pb[sbk][:, j * 32:(j + 1) * 32].rearrange("p (g q) -> p g q", g=4),
                                scalar=zcol[:, 0:1], in1=bias, op0=ALU.add, op1=ALU.add),
                                reads=[("ps", sbk), "BTp", "BTc", "BTs", "zcol"], writes=[("stt", kb)])
                        pi = (ui % 2) * 2 + kb
                        P.op("act", lambda e, pi=pi, kb=kb: e.activation(out=ptt[pi][:, :], in_=stt[kb][:, :], func=AF.Exp),
                             reads=[("stt", kb)], writes=[("ptt", pi)])

                def samp_s2(kvl, kv, ui):
                    for j in range(16):
                        for kb in range(2):
                            pi = (ui % 2) * 2 + kb
                            vv = Vc[:, j, kvl, 0:65] if kb == 0 else Va[:, 9, kvl, 0:65]
                            P.op("pe", lambda e, vv=vv, pi=pi, j=j, kb=kb: e.matmul(
                                pb[5][0:65, j * 32:(j + 1) * 32], lhsT=vv, rhs=ptt[pi][:, j * 32:(j + 1) * 32], start=(kb == 0), stop=(kb == 1)),
                                reads=["Vc" if kb == 0 else "Va", ("ptt", pi)], writes=[("ps", 5)], sig=(j == 15 and kb == 1))

                units = []
                ui = 0
                for kvl in range(2):
                    if "p3units" in _SKIP:
                        continue
                    kv = 2 * p + kvl
                    for mb in range(1, 9):
                        ua = (kvl, KT[:, (mb - 1) * 128:mb * 128], "KT", KT[:, mb * 128:(mb + 1) * 128],
                              (lambda g, mb=mb: QT[:, g, mb * 128:(mb + 1) * 128]), 128,
                              BTp[:, kv * 4:kv * 4 + 4, :], BTc[:, kv * 4:kv * 4 + 4, :],
                              (HMC if mb == 1 else zcol[:, 0:1]),
                              Va[:, mb - 1, kvl, 0:65], "Va", Va[:, mb, kvl, 0:65], 0, True, True, ui)
                        units.append((lambda ua=ua: unit_s1(*ua), lambda ua=ua: unit_s2(*ua), (kvl, mb)))
                        ui += 1
                    units.append((lambda kvl=kvl, kv=kv, ui=ui: samp_s1(kvl, kv, ui), lambda kvl=kvl, kv=kv, ui=ui: samp_s2(kvl, kv, ui), (kvl, 9)))
                    ui += 1
                if units:
                    units[0][0]()
                for i_, (f1, f2, fin) in enumerate(units):
                    if i_ + 1 < len(units):
                        units[i_ + 1][0]()
                    f2()
                    fin_a(fin[0], fin[1], i_)
                    if i_ >= 1:
                        fin_b(units[i_ - 1][2][0], units[i_ - 1][2][1], i_ - 1)
                if units:
                    fin_b(units[-1][2][0], units[-1][2][1], len(units) - 1)
            if "p3out" in _SKIP:
                P.flush()
                return nc
            P.dma("sp", lambda e: e.dma_start(out=kwin[:, :], in_=kout[:, 0, :]), s_o, reads=["kout"])
            P.dma("sp", lambda e: e.dma_start(out=vwin[:, :], in_=vout[:, 0, :]), s_o, reads=["vout"])
            P.dma("sp", lambda e: e.dma_start(out=ks_out[:, 0:120, :], in_=ck[:, 8:128, :]), s_o)
            P.dma("sp", lambda e: e.dma_start(out=vs_out[:, 0:120, :], in_=cv[:, 8:128, :]), s_o)
            for j in range(16):
                P.dma("sp", lambda e, j=j: e.dma_start(out=ks_out[j, 120:128, :], in_=kout[8 * j:8 * j + 8, 1, :]), s_o, reads=["kout"])
                P.dma("sp", lambda e, j=j: e.dma_start(out=vs_out[j, 120:128, :], in_=vout[8 * j:8 * j + 8, 1, :]), s_o, reads=["vout"])
            P.flush()
            if _STOP == 3:
                return nc

        cnt["ev"] = 174
        with ExitStack() as st:
            wsl = [sb(st, "wslg%d" % i, [128, 16, 512], BF16) for i in range(2)]
            qgT = sb(st, "qgT", [128, 2, NMB * 128], BF16)
            kgT = sb(st, "kgT", [128, 2, NMB * 128], BF16)
            kgt = sb(st, "kgt", [128, NMB, 256], BF16)
            vgt = sb(st, "vgt", [128, NMB, 512], BF16)
            sgg = sb(st, "sgg", [128, NMB, 512], BF16)
            sgtmp = sb(st, "sgtmp", [128, 512], F32)
            gn_b = sb(st, "gn_b", [128, 512], F32)
            junk = sb(st, "junk4", [128, 256], BF16)
            P.dma("sp", lambda e: e.dma_start(out=gn_b[:, 0:256], in_=bcast_rows(gnorm, 256)), s_c, writes=["gn_b"])
            P.dma("sp", lambda e: e.dma_start(out=gn_b[:, 256:512], in_=bcast_rows(gnorm, 256)), s_c, writes=["gn_b"])
            T = gla_tiles(st)
            wz = sb(st, "wz", [128, 16, 16], BF16)
            wload(wz, "wz", [(Z_OFF, 16)], s_w[0])
            fproj(wz, "wz", lambda kc: wz[:, kc, :], 16, TOKG,
                  lambda ps, pk, t0, n: copy_evac(zT[0:16, t0:t0 + n], ps, [pk], ["zT"]))
            wi = 0
            for hp in range(0 if 'only5' in _SKIP else 2):
                w, wk, sem = wsl[wi % 2], ("wslg", wi % 2), s_w[wi % 2]; wi += 1
                wload(w, wk, [(QG_OFF + hp * 256, 256), (KG_OFF + hp * 256, 256)], sem)
                for hh in range(2):
                    fproj(w, wk, lambda kc, w=w, hh=hh: w[:, kc, hh * 128:(hh + 1) * 128], 128, TOKG,
                          lambda ps, pk, t0, n, hh=hh: copy_evac(qgT[:, hh, t0:t0 + n], ps, [pk], ["qgT"]))
                    fproj(w, wk, lambda kc, w=w, hh=hh: w[:, kc, 256 + hh * 128:256 + (hh + 1) * 128], 128, TOKG,
                          lambda ps, pk, t0, n, hh=hh: copy_evac(kgT[:, hh, t0:t0 + n], ps, [pk], ["kgT"]))
                tproj(w, wk, 256, 256, range(1, NMB), lambda ps, pk, mb: copy_evac(kgt[:, mb, :], ps, [pk], [("kgt", mb)]))
                w, wk, sem = wsl[wi % 2], ("wslg", wi % 2), s_w[wi % 2]; wi += 1
                wload(w, wk, [(VG_OFF + hp * 512, 512)], sem)
                tproj(w, wk, 0, 512, range(1, NMB), lambda ps, pk, mb: copy_evac(vgt[:, mb, :], ps, [pk], [("vgt", mb)]))
                w, wk, sem = wsl[wi % 2], ("wslg", wi % 2), s_w[wi % 2]; wi += 1
                wload(w, wk, [(GG_OFF + hp * 512, 512)], sem)

                def gg_cons(ps, pk, mb):
                    P.op("act", lambda e: e.activation(out=sgtmp[:], in_=ps, func=AF.Silu), reads=[pk], writes=["sgtmp"])
                    P.op("dve", lambda e: e.tensor_tensor(out=sgg[:, mb, :], in0=sgtmp[:], in1=gn_b[:], op=ALU.mult),
                         reads=["sgtmp", "gn_b"], writes=[("sgg", mb)])
                tproj(w, wk, 0, 512, range(1, NMB), gg_cons)
                def gb(mb, stage, par):
                    samp = (mb == 9)
                    gla_block(T, (2 * hp, 2 * hp + 1), zT[0:17, mb * 128:(mb + 1) * 128], "zT", kgt[:, mb, :], ("kgt", mb),
                              vgt[:, mb, :], ("vgt", mb), NVC(NPRE - 1 + mb), 8 if samp else 64,
                              [qgT[:, hh, mb * 128:(mb + 1) * 128] for hh in range(2)],
                              [kgT[:, hh, mb * 128:(mb + 1) * 128] for hh in range(2)], ("qgT", "kgT"),
                              sgg[:, mb, :], ("sgg", mb),
                              mixed[:, mb, 1024 + hp * 512:1024 + (hp + 1) * 512], ("mixed", mb), sample=samp, stage=stage, par=par)
                if "gseq" in _SKIP:
                    for mb in range(1, 9):
                        gb(mb, 1, mb % 2)
                        gb(mb, 2, mb % 2)
                else:
                    gb(1, 1, 1)
                    for mb in range(1, 9):
                        if mb + 1 < 9:
                            gb(mb + 1, 1, (mb + 1) % 2)
                        gb(mb, 2, mb % 2)
                gb(9, "all", 0)
            for h in range(4):
                P.dma("sp", lambda e, h=h: e.dma_start(out=glast[h, :, :], in_=Sst[:, h, :]), s_o, reads=[("S", h)])
            P.flush()
            if _STOP == 4:
                return nc

        cnt["ev"] = 237
        with ExitStack() as st:
            wo = sb(st, "wo", [128, 16, D], BF16)
            npost_b = sb(st, "npost_b", [128, D], F32)
            mT = sb(st, "mT", [128, 16, 128], BF16)
            osb = sb(st, "osb", [128, D], F32)
            xs = [sb(st, "xs5_%d" % i, [128, D], F32) for i in range(2)]
            junk = sb(st, "junk5", [128, D], BF16)
            for i in range(4):
                P.dma("pool", lambda e, i=i: e.dma_start(out=wo[:, :, i * 512:(i + 1) * 512],
                                                       in_=w_out[:, i * 512:(i + 1) * 512].rearrange("(kc p) n -> p kc n", p=128)),
                      s_w[i % 2], writes=[("wo", i)])
            P.dma("sp", lambda e: e.dma_start(out=npost_b[:], in_=bcast_rows(npost, D)), s_c, writes=["npost_b"])
            for mb in range(1, NMB):
                slot = NPRE - 1 + mb
                i = cnt["x"] % 2
                cnt["x"] += 1
                P.dma("sp", lambda e, i=i, slot=slot: e.dma_start(out=xs[i][:], in_=xp[slot * 128:(slot + 1) * 128, :]), s_ld[i], writes=[("xs", i)])
                for half in range(2):
                    for j in range(8):
                        kc = half * 8 + j
                        P.op("pe", lambda e, kc=kc, j=j, mb=mb: e.transpose(out=pbb[2][:, j * 128:(j + 1) * 128],
                                                                         in_=mixed[:, mb, kc * 128:(kc + 1) * 128], identity=ident_b),
                             reads=[("mixed", mb), "cstb"], writes=[("ps", 2)], sig=(j == 7))
                    copy_evac(mT[:, half * 8:(half + 1) * 8, :], pbb[2][:, :].rearrange("p (a b) -> p a b", b=128), [("ps", 2)], ["mT"])
                for n in range(4):
                    b = 4 + n
                    mm16(pb[b][:, :], ("ps", b), lambda kc: mT[:, kc, :], lambda kc, n=n: wo[:, kc, n * 512:(n + 1) * 512], ["mT", ("wo", n)])
                    P.op("dve", lambda e, n=n, b=b: e.tensor_copy(out=osb[:, n * 512:(n + 1) * 512], in_=pb[b][:, :]), reads=[("ps", b)], writes=[("osb", n)])
                P.op("act", lambda e: e.activation(out=junk[:], in_=osb[:], func=AF.Square, accum_out=small[:, 30:31]),
                     reads=["osb"], writes=["junk", ("small", 30)])
                P.op("act", lambda e: e.activation(out=small[:, 31:32], in_=small[:, 30:31], func=AF.Ln, bias=epst[:, 0:1], scale=1.0 / D),
                     reads=[("small", 30), "epst"], writes=[("small", 31)])
                P.op("act", lambda e: e.activation(out=small[:, 32:33], in_=small[:, 31:32], func=AF.Exp, scale=-0.5),
                     reads=[("small", 31)], writes=[("small", 32)])
                P.op("dve", lambda e: e.scalar_tensor_tensor(out=osb[:], in0=osb[:], scalar=small[:, 32:33], in1=npost_b[:], op0=ALU.mult, op1=ALU.mult),
                     reads=["osb", ("small", 32), "npost_b"], writes=["osb"])
                P.op("dve" if "p5pool" in _SKIP else "pool", lambda e, i=i: e.tensor_tensor(out=osb[:], in0=osb[:], in1=xs[i][:], op=ALU.add), reads=["osb", ("xs", i)], writes=["osb"])
                if "p5out" in _SKIP:
                    continue
                if mb < 9:
                    P.dma("sp", lambda e, mb=mb: e.dma_start(out=y_main[(mb - 1) * 128:mb * 128, :], in_=osb[:]), s_o, reads=["osb"])
                else:
                    P.dma("sp", lambda e: e.dma_start(out=y_samp[:, :], in_=osb[:]), s_o, reads=["osb"])
            P.flush()
            if _STOP == 5:
                return nc
    return nc


_CACHE = {}


def kernel(x_prompt, x_sample, cache_k_win, cache_v_win, state_gla, meta_tokens, rel_bias,
           norm_pre, norm_post, w_in, w_a2, b_a, attn_sinks, gla_norm, w_out):
    f = lambda a: np.ascontiguousarray(np.asarray(a, dtype=np.float32))
    x_prompt, x_sample = f(x_prompt), f(x_sample)
    ckw, cvw, sgl = f(cache_k_win)[0], f(cache_v_win)[0], f(state_gla)[0]
    meta = f(meta_tokens)
    cbase = make_consts()
    ii = np.arange(128)
    cB = np.zeros((128, NCB), np.float32)
    cB[:, 0:C_OH] = cbase[:, 0:C_OH]
    cB[:, C_US:C_US + 128] = (ii[:, None] > ii[None, :])
    cB[:, C_ONE:C_ONE + 2] = 1.0
    ohm_h = np.ascontiguousarray(cbase[0:33, C_OH:C_OH + 383])
    if "nc" not in _CACHE:
        _CACHE["nc"] = build()
    nc = _CACHE["nc"]
    w_in0, w_out0 = f(w_in)[0], f(w_out)[0]
    in_maps = []
    for c in range(8):
        b, q = c // 4, c % 4
        xpad = np.zeros((33 * 128, D), np.float32)
        xpad[112:128] = meta
        xpad[128:] = x_prompt[b]
        nblk = 8 * q + 9
        xc = np.zeros((NSLOT * 128, D), np.float32)
        xc[(33 - nblk) * 128:33 * 128] = xpad[:nblk * 128]
        xc[33 * 128:] = x_sample[16 * c:16 * c + 16].reshape(128, D)
        valid = np.zeros((NSLOT * 128,), np.float32)
        vpad = np.ones((33 * 128,), np.float32)
        vpad[:112] = 0
        valid[(33 - nblk) * 128:33 * 128] = vpad[:nblk * 128]
        valid[33 * 128:] = 1
        cs = np.zeros((128, 163), np.float32)
        cs[:, 0:128] = cbase[:, C_ID:C_ID + 128]
        cs[:, 128:128 + NSLOT] = (-valid / 16.0).reshape(NSLOT, 128).T
        if q == 0:
            cs[:112, 162] = NEG
        in_maps.append({
            "xp": xc, "w_in": w_in0, "w_out": w_out0,
            "ck": np.ascontiguousarray(ckw[16 * c:16 * c + 16].reshape(16, 128, 256)),
            "cv": np.ascontiguousarray(cvw[16 * c:16 * c + 16].reshape(16, 128, 256)),
            "sg": np.ascontiguousarray(sgl[16 * c:16 * c + 16]),
            "cstS": cs, "cstB": cB, "ohm": ohm_h, "relb": f(rel_bias), "npre": f(norm_pre), "npost": f(norm_post),
            "wa2": f(w_a2)[0], "ba": f(b_a), "sinks": f(attn_sinks), "gnorm": f(gla_norm),
        })
    res = run_bass_kernel_spmd(nc, in_maps, core_ids=list(range(8)))
    R = res.results
    y_prompt = np.zeros((2, 4096, D), np.float32)
    y_sample = np.zeros((128, 8, D), np.float32)
    kwp = np.zeros((1, 2, 128, 4, 64), np.float32)
    vwp = np.zeros((1, 2, 128, 4, 64), np.float32)
    sgp = np.zeros((1, 2, 4, 128, 256), np.float32)
    kws = np.zeros((1, 128, 128, 4, 64), np.float32)
    vws = np.zeros((1, 128, 128, 4, 64), np.float32)
    sgs = np.zeros((1, 128, 4, 128, 256), np.float32)
    for c in range(8):
        b, q = c // 4, c % 4
        r = R[c]
        y_prompt[b, q * 1024:(q + 1) * 1024] = r["y_main"]
        y_sample[16 * c:16 * c + 16] = r["y_samp"].reshape(16, 8, D)
        kws[0, 16 * c:16 * c + 16] = r["ks_out"].reshape(16, 128, 4, 64)
        vws[0, 16 * c:16 * c + 16] = r["vs_out"].reshape(16, 128, 4, 64)
        sgs[0, 16 * c:16 * c + 16] = r["gs_out"]
        if q == 3:
            kwp[0, b] = r["kwin"].reshape(128, 4, 64)
            vwp[0, b] = r["vwin"].reshape(128, 4, 64)
            sgp[0, b] = r["glast"]
    return (y_prompt, y_sample, kwp, vwp, sgp, kws, vws, sgs)
```

```python
import math
from contextlib import ExitStack
import numpy as np
import concourse.bass as bass
import concourse.mybir as mybir
from concourse.bass_utils import run_bass_kernel_spmd

F32 = mybir.dt.float32
BF16 = mybir.dt.bfloat16
AF = mybir.ActivationFunctionType
ALU = mybir.AluOpType

D = 2048
INW = 5648
Q_OFF, K_OFF, V_OFF, GA_OFF, QG_OFF, KG_OFF, VG_OFF, GG_OFF, Z_OFF = 0, 1024, 1280, 1536, 2560, 3072, 3584, 4608, 5632
NSLOT = 34
NPRE = 25
NMB = 10
NEG = -30000.0
EPS = 1e-6

C_ID, C_U64, C_U8, C_CI64, C_CI8, C_CMK64, C_CMK8, C_OH, C_NV, C_HM = (
    0, 128, 256, 384, 386, 402, 658, 2706, 3089, 3123)
NCST = 3124
C_US, C_ONE, NCB = 2752, 2880, 2888


class Prog:
    ENG = ("pe", "act", "dve", "pool", "sp")

    def __init__(self, nc, stack):
        self.nc = nc
        self.stack = stack
        self.sems = {e: stack.enter_context(nc.semaphore("s_" + e)) for e in self.ENG}
        self.cnt = {e: 0 for e in self.ENG}
        self.dsems = []
        self._reset()

    def _reset(self):
        self.q = {e: [] for e in self.ENG}
        self.regs = {}
        self.children = {}

    def dma_sem(self, name):
        d = {"h": self.stack.enter_context(self.nc.semaphore(name)), "v": 0}
        self.dsems.append(d)
        return d

    def _reg(self, k):
        r = self.regs.get(k)
        if r is None:
            r = self.regs[k] = {"w": {}, "r": {}}
            for i in range(1, len(k)):
                self.children.setdefault(k[:i], set()).add(k)
        return r

    def _conf(self, k):
        out = []
        for i in range(1, len(k) + 1):
            r = self.regs.get(k[:i])
            if r is not None:
                out.append(r)
        for c in self.children.get(k, ()):
            out.append(self.regs[c])
        return out

    @staticmethod
    def _merge(dst, src):
        for sk, (h, v) in src.items():
            if sk not in dst or dst[sk][1] < v:
                dst[sk] = (h, v)

    def _deps(self, reads, writes):
        need = {}
        for k in reads:
            for c in self._conf(k):
                self._merge(need, c["w"])
        for k in writes:
            for c in self._conf(k):
                self._merge(need, c["w"])
                self._merge(need, c["r"])
        return need

    def _record(self, tok, reads, writes):
        sk, h, v = tok
        for k in reads:
            self._merge(self._reg(k)["r"], {sk: (h, v)})
        for k in writes:
            r = self._reg(k)
            for c in self._conf(k):
                if c is not r:
                    self._merge(c["w"], {sk: (h, v)})
            r["w"] = {sk: (h, v)}
            r["r"] = {}

    @staticmethod
    def _norm(k):
        k = tuple(k) if isinstance(k, (tuple, list)) else (k,)
        return k[:2] if k[0] == "ps" else k

    def op(self, eng, fn, reads=(), writes=(), sig=True):
        reads = [self._norm(k) for k in reads]
        writes = [self._norm(k) for k in writes]
        writes = writes + [k for k in reads if k[0] == "ps" and k not in writes]
        reads = [k for k in reads if k[0] != "ps"]
        need = self._deps(reads, writes)
        if sig:
            self.cnt[eng] += 1
            self._record((eng, self.sems[eng], self.cnt[eng]), reads, writes)
        self.q[eng].append((need, fn, sig, None))

    def dma(self, eng, fn, sem, reads=(), writes=()):
        reads = [self._norm(k) for k in reads]
        writes = [self._norm(k) for k in writes]
        need = self._deps(reads, writes)
        if sem["v"] > 0:
            self._merge(need, {id(sem): (sem["h"], sem["v"])})
        sem["v"] += 16
        self._record((id(sem), sem["h"], sem["v"]), reads, writes)
        self.q[eng].append((need, fn, False, sem))

    def flush(self):
        need = {id(d): (d["h"], d["v"]) for d in self.dsems if d["v"] > 0}
        for e in self.ENG:
            self.q[e].append((dict(need), lambda h: h.nop(), False, None))
        nc = self.nc
        with nc.Block() as block:
            handles = {"pe": block.tensor, "act": block.scalar, "dve": block.vector,
                       "pool": block.gpsimd, "sp": block.sync}
            for e in self.ENG:
                def body(h, lst=self.q[e], e=e):
                    waited = {}
                    for need_, fn, sig, dsem in lst:
                        for sk, (sh, v) in need_.items():
                            if waited.get(sk, 0) >= v:
                                continue
                            waited[sk] = v
                            h.wait_ge(sh, v)
                        ins = fn(h)
                        if sig:
                            ins.then_inc(self.sems[e], 1)
                        if dsem is not None:
                            ins.then_inc(dsem["h"], 16)
                handles[e](body)
        self._reset()


def t5_bucket_np(dist):
    d = np.maximum(dist, 0)
    ratio = np.maximum(d, 16).astype(np.float32) / np.float32(16)
    large = 16 + (np.log(ratio) / np.float32(math.log(128 / 16)) * 16).astype(np.int32)
    large = np.minimum(large, 31)
    return np.where(d < 16, d, large)


def make_consts():
    c = np.zeros((128, NCST), np.float32)
    i = np.arange(128)
    c[:, C_ID:C_ID + 128] = np.eye(128)
    for off, C in ((C_U64, 64), (C_U8, 8)):
        c[:, off:off + 128] = ((i[:, None] <= i[None, :]) & (i[:, None] // C == i[None, :] // C))
    c[:, C_CI64:C_CI64 + 2] = (i[:, None] // 64 == np.arange(2)[None, :])
    c[:, C_CI8:C_CI8 + 16] = (i[:, None] // 8 == np.arange(16)[None, :])
    c[:, C_CMK64:C_CMK64 + 256] = np.tile((np.arange(2)[:, None] == i[None, :] // 64).reshape(1, 256), (128, 1))
    c[:, C_CMK8:C_CMK8 + 2048] = np.tile((np.arange(16)[:, None] == i[None, :] // 8).reshape(1, 2048), (128, 1))
    m = np.arange(383)
    dist = m - 127
    valid = (dist >= 0) & (dist < 128)
    bk = t5_bucket_np(dist)
    oh = np.zeros((33, 383), np.float32)
    oh[bk[valid], m[valid]] = 1.0
    oh[32, ~valid] = NEG
    c[:33, C_OH:C_OH + 383] = oh
    return c


_STOP = None
_SKIP = set()


def build():
    nc = bass.Bass("TRN2", target_bir_lowering=False)
    dt_in = lambda n, s: nc.dram_tensor(n, s, F32, kind="ExternalInput").ap()
    dt_out = lambda n, s: nc.dram_tensor(n, s, F32, kind="ExternalOutput").ap()
    xp = dt_in("xp", [NSLOT * 128, D])
    w_in = dt_in("w_in", [D, INW])
    w_out = dt_in("w_out", [D, D])
    ck = dt_in("ck", [16, 128, 256])
    cv = dt_in("cv", [16, 128, 256])
    sg = dt_in("sg", [16, 4, 128, 256])
    cstS_d = dt_in("cstS", [128, 163])
    cstB_d = dt_in("cstB", [128, NCB])
    ohm_d = dt_in("ohm", [33, 383])
    relb = dt_in("relb", [32, 16])
    npre = dt_in("npre", [1, D])
    npost = dt_in("npost", [1, D])
    wa2 = dt_in("wa2", [16, 512])
    ba = dt_in("ba", [1, 512])
    sinks = dt_in("sinks", [1, 16])
    gnorm = dt_in("gnorm", [1, 256])
    y_main = dt_out("y_main", [8 * 128, D])
    y_samp = dt_out("y_samp", [128, D])
    kwin = dt_out("kwin", [128, 256])
    vwin = dt_out("vwin", [128, 256])
    glast = dt_out("glast", [4, 128, 256])
    ks_out = dt_out("ks_out", [16, 128, 256])
    vs_out = dt_out("vs_out", [16, 128, 256])
    gs_out = dt_out("gs_out", [16, 4, 128, 256])
    fscr = nc.dram_tensor("fscr", [16, 128, 383], F32, kind="Internal").ap()

    def bcast_rows(ap2d, n):
        return bass.AP(ap2d.tensor, 0, [[0, 128], [1, n]])

    with ExitStack() as top:
        P = Prog(nc, top)
        sb = lambda st, name, shape, dt: st.enter_context(nc.sbuf_tensor(name, shape, dt, align_bytes=64))
        cstS = sb(top, "cstS_sb", [128, 163], F32)
        cstb = sb(top, "cstb", [128, NCB], BF16)
        esink = sb(top, "esink", [128, 16], F32)
        epst = sb(top, "epst", [128, 1], F32)
        wa2a = sb(top, "wa2a", [17, 512], BF16)
        Sst = sb(top, "Sst", [128, 4, 256], F32)
        Sbf = sb(top, "Sbf", [128, 4, 256], BF16)
        rstd_all = sb(top, "rstd_all", [128, NSLOT], F32)
        small = sb(top, "small", [128, 64], F32)
        zT = sb(top, "zT", [17, NMB * 128], BF16)
        xT_main = sb(top, "xT_main", [128, 16, NMB * 128], BF16)
        mixed = sb(top, "mixed", [128, NMB, D], BF16)
        kout = sb(top, "kout", [128, 2, 256], F32)
        vout = sb(top, "vout", [128, 2, 256], F32)
        pb = [top.enter_context(nc.psum_tensor("pb%d" % i, [128, 512], F32)) for i in range(8)]
        pbb = [t[:].bitcast(BF16) for t in pb]
        ident_b = cstb[:, C_ID:C_ID + 128]
        ident_f = cstS[:, 0:128]
        NVC = lambda slot: cstS[:, 128 + slot:129 + slot]
        HMC = cstS[:, 162:163]
        s_ld = [P.dma_sem("s_ld%d" % i) for i in range(2)]
        s_c = P.dma_sem("s_c")
        s_cp = P.dma_sem("s_cp")
        s_w = [P.dma_sem("s_w%d" % i) for i in range(2)]
        s_o = P.dma_sem("s_o")
        s_mf = [P.dma_sem("s_mf%d" % i) for i in range(4)]
        s_mb = [P.dma_sem("s_mb%d" % i) for i in range(4)]
        s_mo = [P.dma_sem("s_mo%d" % i) for i in range(4)]

        P.dma("sp", lambda e: e.dma_start(out=cstS[:], in_=cstS_d[:, :]), s_c, writes=["cstf"])
        P.dma("pool", lambda e: e.dma_start(out=cstb[:], in_=cstB_d[:, :]), s_w[0], writes=["cstb"])
        if "bcast" not in _SKIP:
            P.dma("sp", lambda e: e.dma_start(out=esink[:], in_=bcast_rows(sinks, 16)), s_c, writes=["esink"])
        if "wa2" not in _SKIP:
            P.dma("pool", lambda e: e.dma_start(out=wa2a[0:16, :], in_=wa2[:, :]), s_cp, writes=["wa2a"])
            P.dma("pool", lambda e: e.dma_start(out=wa2a[16:17, :], in_=ba[:, :]), s_cp, writes=["wa2a"])
        P.op("pool", lambda e: e.memset(epst[:], EPS), writes=["epst"])
        P.op("pool", lambda e: e.memset(Sst[:], 0.0), writes=["S"])
        P.op("pool", lambda e: e.memset(Sbf[:], 0.0), writes=["Sbf"])
        P.op("pool", lambda e: e.memset(zT[:], 1.0), writes=["zT"])
        P.op("act", lambda e: e.activation(out=esink[:], in_=esink[:], func=AF.Exp), reads=["esink"], writes=["esink"])
        with ExitStack() as st:
            rba = sb(st, "rba", [33, 16], F32)
            fes = sb(st, "fes", [16, 383], F32)
            ohm = sb(st, "ohm_sb", [33, 383], F32)
            P.dma("sp", lambda e: e.dma_start(out=ohm[:], in_=ohm_d[:, :]), s_c, writes=["ohm"])
            P.op("pool", lambda e: e.memset(rba[:], 1.0), writes=["rba"])
            P.dma("sp", lambda e: e.dma_start(out=rba[0:32, :], in_=relb[:, :]), s_c, reads=[], writes=["rba"])
            if "f32mm" not in _SKIP:
                P.op("pe", lambda e: e.matmul(pb[0][0:16, 0:383], lhsT=rba[0:33, :], rhs=ohm[:, :],
                                             start=True, stop=True), reads=["rba", "ohm"], writes=[("ps", 0)])
                P.op("dve", lambda e: e.tensor_copy(out=fes[:], in_=pb[0][0:16, 0:383]), reads=[("ps", 0)], writes=["fes"])
            if "fscr" not in _SKIP:
                P.dma("sp", lambda e: e.dma_start(out=fscr[:, :, :], in_=fes[:].unsqueeze(1).broadcast_to([16, 128, 383])),
                      s_c, reads=["fes"], writes=["fscr"])
            P.flush()
            if _STOP == 0:
                return nc

        cnt = {"x": 0, "ev": 0}

        def xprep(slot, dst, dkey, par=0):
            i = cnt["x"] % 2
            cnt["x"] += 1
            P.dma("sp", lambda e: e.dma_start(out=xs[i][:], in_=xp[slot * 128:(slot + 1) * 128, :]), s_ld[i], writes=[("xs", i)])
            P.op("act", lambda e: e.activation(out=junk[:], in_=xs[i][:], func=AF.Square, accum_out=small[:, 2 * par:2 * par + 1]),
                 reads=[("xs", i)], writes=["junk", ("small", 2 * par)])
            P.op("act", lambda e: e.activation(out=small[:, 2 * par + 1:2 * par + 2], in_=small[:, 2 * par:2 * par + 1], func=AF.Ln, bias=epst[:, 0:1], scale=1.0 / D),
                 reads=[("small", 2 * par), "epst"], writes=[("small", 2 * par + 1)])
            P.op("act", lambda e: e.activation(out=rstd_all[:, slot:slot + 1], in_=small[:, 2 * par + 1:2 * par + 2], func=AF.Exp, scale=-0.5),
                 reads=[("small", 2 * par + 1)], writes=[("rstd", slot)])
            P.op("dve", lambda e: e.scalar_tensor_tensor(out=xb[par][:], in0=xs[i][:], scalar=rstd_all[:, slot:slot + 1], in1=npre_b[:],
                                                         op0=ALU.mult, op1=ALU.mult),
                 reads=[("xs", i), ("rstd", slot), "npre_b"], writes=[("xb", par)])
            for half in range(2):
                for j in range(8):
                    kc = half * 8 + j
                    P.op("pe", lambda e, kc=kc, j=j: e.transpose(out=pbb[2][:, j * 128:(j + 1) * 128],
                                                                in_=xb[par][:, kc * 128:(kc + 1) * 128], identity=ident_b),
                         reads=[("xb", par), "cstb"], writes=[("ps", 2)], sig=(j == 7))
                src = pbb[2][:, :].rearrange("p (a b) -> p a b", b=128)
                if half == 0:
                    P.op("act", lambda e, half=half, src=src: e.activation(out=dst[:, half * 8:(half + 1) * 8, :], in_=src, func=AF.Copy),
                         reads=[("ps", 2)], writes=[dkey])
                else:
                    P.op("dve", lambda e, half=half, src=src: e.tensor_copy(out=dst[:, half * 8:(half + 1) * 8, :], in_=src),
                         reads=[("ps", 2)], writes=[dkey])

        def wload(dst, dkey, segs, sem):
            off = 0
            for (c0, n) in segs:
                P.dma("pool", lambda e, c0=c0, n=n, off=off: e.dma_start(
                    out=dst[:, :, off:off + n], in_=w_in[:, c0:c0 + n].rearrange("(kc p) n -> p kc n", p=128)),
                    sem, writes=[dkey])
                off += n

        def mm16(out_ap, okey, lhs_fn, rhs_fn, rkeys):
            for kc in range(16):
                P.op("pe", lambda e, kc=kc: e.matmul(out_ap, lhsT=lhs_fn(kc), rhs=rhs_fn(kc), start=(kc == 0), stop=(kc == 15)),
                     reads=rkeys, writes=[okey], sig=(kc == 15))

        def evac(fn_act, fn_dve, reads, writes):
            cnt["ev"] += 1
            if cnt["ev"] % 2:
                P.op("act", fn_act, reads=reads, writes=writes)
            else:
                P.op("dve", fn_dve, reads=reads, writes=writes)

        def copy_evac(dst, src, reads, writes, scale=None):
            if scale is None:
                evac(lambda e: e.activation(out=dst, in_=src, func=AF.Copy), lambda e: e.tensor_copy(out=dst, in_=src), reads, writes)
            else:
                evac(lambda e: e.mul(dst, src, scale),
                     lambda e: e.tensor_scalar(out=dst, in0=src, scalar1=scale, scalar2=None, op0=ALU.mult), reads, writes)

        pcnt = {"b": 0}

        def nbank():
            pcnt["b"] ^= 1
            return pcnt["b"]

        def gla_block(T, heads, zT_ap, zkey, ktm, ktm_key, vtm, vtm_key, nvcol, C,
                      qT, kT, fkeys, sgg, sgg_key, mixed_dst, mixed_key, sample=False, stage="all", par=0):
            nch = 128 // C
            Uo, CIo, CMKo = (C_U64, C_CI64, C_CMK64) if C == 64 else (C_U8, C_CI8, C_CMK8)
            U = cstb[:, Uo:Uo + 128]
            CI = cstb[:, CIo:CIo + nch]
            CMK = cstb[:, CMKo:CMKo + nch * 128].rearrange("p (c t) -> p c t", t=128)
            h0 = heads[0]
            kpx, qpx, atm, el = T["kpx"][par], T["qpx"][par], T["atm"][par], T["el"][par]
            if stage in ("all", 1):
                P.op("pe", lambda e: e.matmul(pb[3][:, 0:256], lhsT=zT_ap, rhs=wa2a[0:17, h0 * 128:h0 * 128 + 256], start=True, stop=True),
                     reads=[zkey, "wa2a"], writes=[("ps", 3)])
                P.op("act", lambda e: e.activation(out=T["lf"][:], in_=pb[3][:, 0:256], func=AF.Exp, scale=-1.0),
                     reads=[("ps", 3)], writes=["lf"])
                P.op("act", lambda e: e.activation(out=T["lf"][:], in_=T["lf"][:], func=AF.Ln, bias=1.0), reads=["lf"], writes=["lf"])
                P.op("dve", lambda e: e.tensor_scalar(out=T["lfb"][:], in0=T["lf"][:], scalar1=nvcol, scalar2=None, op0=ALU.mult),
                     reads=["lf", "cstf"], writes=["lfb"])
                P.op("pe", lambda e: e.matmul(pb[3][:, 0:256], lhsT=U, rhs=T["lfb"][:], start=True, stop=True),
                     reads=["lfb", "cstb"], writes=[("ps", 3)])
                P.op("act", lambda e: e.activation(out=T["einv"][:], in_=pb[3][:, 0:256], func=AF.Exp, scale=-1.0),
                     reads=[("ps", 3)], writes=["einv"])
                P.op("dve", lambda e: e.tensor_tensor(out=T["kp"][:], in0=ktm, in1=T["einv"][:], op=ALU.mult),
                     reads=[ktm_key, "einv"], writes=["kp"])
                for hh in range(2):
                    P.op("pe", lambda e, hh=hh: e.matmul(pb[4][:, 256 + hh * 16:256 + hh * 16 + nch], lhsT=T["lfb"][:, hh * 128:(hh + 1) * 128],
                                                        rhs=CI, start=True, stop=True),
                         reads=["lfb", "cstb"], writes=[("ps", 4, "bl")], sig=(hh == 1))
                P.op("act", lambda e: e.activation(out=el[:], in_=pb[4][:, 256:288], func=AF.Exp), reads=[("ps", 4, "bl")], writes=[("el", par)])
                for hh in range(2):
                    P.op("pe", lambda e, hh=hh: e.matmul(pb[4][:, hh * 128:(hh + 1) * 128], lhsT=T["lfb"][:, hh * 128:(hh + 1) * 128],
                                                        rhs=U, start=True, stop=True),
                         reads=["lfb", "cstb"], writes=[("ps", 4, "bf")], sig=(hh == 1))
                P.op("act", lambda e: e.activation(out=T["efm"][:], in_=pb[4][:, 0:256], func=AF.Exp), reads=[("ps", 4, "bf")], writes=["efm"])
                P.op("act", lambda e: e.activation(out=T["eifm"][:], in_=pb[4][:, 0:256], func=AF.Exp, scale=-1.0),
                     reads=[("ps", 4, "bf")], writes=["eifm"])
                for hh in range(2):
                    P.op("dve", lambda e, hh=hh: e.scalar_tensor_tensor(out=T["qpT"][:, hh, :], in0=qT[hh], scalar=128.0 ** -0.5,
                                                                       in1=T["efm"][:, hh * 128:(hh + 1) * 128], op0=ALU.mult, op1=ALU.mult),
                         reads=list(fkeys) + ["efm"], writes=[("qpT", hh)])
                    P.op("pool", lambda e, hh=hh: e.tensor_tensor(out=T["kpT"][:, hh, :], in0=kT[hh], in1=T["eifm"][:, hh * 128:(hh + 1) * 128],
                                                                 op=ALU.mult),
                         reads=list(fkeys) + ["eifm"], writes=[("kpT", hh)])
                for hh in range(2):
                    P.op("pool", lambda e, hh=hh: e.tensor_tensor(
                        out=kpx[:, hh, 0:nch, :], in0=T["kp"][:, hh * 128:(hh + 1) * 128].unsqueeze(1).broadcast_to([128, nch, 128]),
                        in1=CI.unsqueeze(2).broadcast_to([128, nch, 128]), op=ALU.mult),
                        reads=["kp", "cstb"], writes=[("kpx", par, hh)])
                    abk = 5 if hh == 0 else 2
                    P.op("pe", lambda e, hh=hh, abk=abk: e.matmul(pb[abk][:, 0:128], lhsT=T["kpT"][:, hh, :], rhs=T["qpT"][:, hh, :],
                                                                 start=True, stop=True),
                         reads=[("kpT", hh), ("qpT", hh)], writes=[("ps", abk)])
                    P.op("dve", lambda e, hh=hh, abk=abk: e.tensor_tensor(out=atm[:, hh, :], in0=pb[abk][:, 0:128], in1=U, op=ALU.mult),
                         reads=[("ps", abk), "cstb"], writes=[("atm", par, hh)])
                    P.op("dve", lambda e, hh=hh: e.tensor_tensor(
                        out=qpx[:, hh, 0:nch, :], in0=T["qpT"][:, hh, :].unsqueeze(1).broadcast_to([128, nch, 128]), in1=CMK, op=ALU.mult),
                        reads=[("qpT", hh), "cstb"], writes=[("qpx", par, hh)])
            if stage not in ("all", 2):
                return
            for hh in range(2):
                h = heads[hh]
                obk = 6 if hh == 0 else 0
                tbk = 7 if hh == 0 else 1
                ob = pb[obk][:, 0:256]
                P.op("pe", lambda e, hh=hh, ob=ob: e.matmul(ob, lhsT=atm[:, hh, :], rhs=vtm[:, hh * 256:(hh + 1) * 256], start=True, stop=False),
                     reads=[("atm", par, hh), vtm_key], writes=[("ps", obk)], sig=False)

                def s_load(c, h=h, hh=hh):
                    i = (c + hh) % 4
                    s0f, s0b = T["s0f"][i], T["s0b"][i]
                    P.dma("sp", lambda e, c=c, h=h, s0f=s0f: e.dma_start(out=s0f[:], in_=sg[c, h, :, :]), s_mf[i], writes=[("s0f", i)])
                    P.dma("pool", lambda e, c=c, h=h, s0b=s0b: e.dma_start(out=s0b[:], in_=sg[c, h, :, :]), s_mb[i], writes=[("s0b", i)])
                for c in range(nch):
                    if sample:
                        if c == 0:
                            for c2 in range(3):
                                s_load(c2)
                        if c + 3 < nch:
                            s_load(c + 3)
                        i = (c + hh) % 4
                        s0f, s0b = T["s0f"][i], T["s0b"][i]
                        Sf, Sb, skf, skb = s0f[:], s0b[:], ("s0f", i), ("s0b", i)
                    else:
                        Sf, Sb, skf, skb = Sst[:, h, :], Sbf[:, h, :], ("S", h), ("Sbf", h)
                    P.op("pe", lambda e, hh=hh, c=c, Sb=Sb, ob=ob: e.matmul(ob, lhsT=qpx[:, hh, c, :], rhs=Sb, start=False, stop=(c == nch - 1)),
                         reads=[("qpx", par, hh), skb], writes=[("ps", obk)])
                    P.op("pe", lambda e, hh=hh, c=c, tbk=tbk: e.matmul(pb[tbk][:, 0:256], lhsT=kpx[:, hh, c, :], rhs=vtm[:, hh * 256:(hh + 1) * 256],
                                                                      start=True, stop=True),
                         reads=[("kpx", par, hh), vtm_key], writes=[("ps", tbk)])
                    ecol = el[:, hh * 16 + c:hh * 16 + c + 1]
                    P.op("act", lambda e, Sf=Sf, ecol=ecol: e.activation(out=T["se"][:], in_=Sf, func=AF.Copy, scale=ecol),
                         reads=[skf, ("el", par)], writes=["se"])
                    P.op("dve", lambda e, ecol=ecol, Sf=Sf, tbk=tbk: e.scalar_tensor_tensor(out=Sf, in0=pb[tbk][:, 0:256], scalar=ecol, in1=T["se"][:],
                                                                                          op0=ALU.mult, op1=ALU.add),
                         reads=[("ps", tbk), ("el", par), "se"], writes=[skf])
                    if sample:
                        P.dma("sp", lambda e, c=c, h=h, Sf=Sf: e.dma_start(out=gs_out[c, h, :, :], in_=Sf), s_mo[i], reads=[skf])
                    else:
                        P.op("act", lambda e, Sf=Sf, Sb=Sb: e.activation(out=Sb, in_=Sf, func=AF.Copy), reads=[skf], writes=[skb])
                ssq = small[:, 8 + hh:9 + hh]
                P.op("act", lambda e, ob=ob, ssq=ssq: e.activation(out=junk[:, 0:256], in_=ob, func=AF.Square, accum_out=ssq),
                     reads=[("ps", obk)], writes=["junk", ("small", 8 + hh)])
                P.op("act", lambda e, ssq=ssq: e.activation(out=ssq, in_=ssq, func=AF.Ln, bias=epst[:, 0:1], scale=1.0 / 256),
                     reads=[("small", 8 + hh), "epst"], writes=[("small", 8 + hh)])
                P.op("act", lambda e, ssq=ssq: e.activation(out=ssq, in_=ssq, func=AF.Exp, scale=-0.5),
                     reads=[("small", 8 + hh)], writes=[("small", 8 + hh)])
                P.op("dve", lambda e, hh=hh, ob=ob, ssq=ssq: e.scalar_tensor_tensor(
                    out=mixed_dst[:, hh * 256:(hh + 1) * 256], in0=ob, scalar=ssq, in1=sgg[:, hh * 256:(hh + 1) * 256], op0=ALU.mult, op1=ALU.mult),
                    reads=[("ps", obk), ("small", 8 + hh), sgg_key], writes=[mixed_key])

        def gla_tiles(st):
            sbq = lambda name, shape, dt: st.enter_context(nc.sbuf_tensor(name, shape, dt, align_bytes=64))
            T = {}
            T["lf"] = sbq("g_lf", [128, 256], F32)
            T["lfb"] = sbq("g_lfb", [128, 256], BF16)
            T["einv"] = sbq("g_einv", [128, 256], F32)
            T["kp"] = sbq("g_kp", [128, 256], BF16)
            el0 = sbq("g_el0", [128, 32], F32)
            kpx0 = sbq("g_kpx0", [128, 2, 16, 128], BF16)
            T["se"] = sbq("g_se", [128, 256], F32)
            T["efm"] = sbq("g_efm", [128, 256], F32)
            T["eifm"] = sbq("g_eifm", [128, 256], F32)
            T["qpT"] = sbq("g_qpT", [128, 2, 128], BF16)
            T["kpT"] = sbq("g_kpT", [128, 2, 128], BF16)
            atm0 = sbq("g_atm0", [128, 2, 128], BF16)
            qpx0 = sbq("g_qpx0", [128, 2, 16, 128], BF16)
            T["s0f"] = [sbq("g_s0f%d" % i, [128, 256], F32) for i in range(4)]
            T["s0b"] = [sbq("g_s0b%d" % i, [128, 256], BF16) for i in range(4)]
            T["el"] = [el0, sbq("g_el1", [128, 32], F32)]
            T["kpx"] = [kpx0, sbq("g_kpx1", [128, 2, 2, 128], BF16)]
            T["qpx"] = [qpx0, sbq("g_qpx1", [128, 2, 2, 128], BF16)]
            T["atm"] = [atm0, sbq("g_atm1", [128, 2, 128], BF16)]
            return T

        st12 = ExitStack()
        npre_b = sb(st12, "npre_b", [128, D], F32)
        xs = [sb(st12, "xs%d" % i, [128, D], F32) for i in range(2)]
        xb = [sb(st12, "xb%d" % i, [128, D], BF16) for i in range(2)]
        junk = sb(st12, "junk", [128, D], BF16)
        P.dma("sp", lambda e: e.dma_start(out=npre_b[:], in_=bcast_rows(npre, D)), s_c, writes=["npre_b"])
        with ExitStack() as st:
            wst = sb(st, "wst", [128, 16, 1552], BF16)
            xTb = [sb(st, "xTb%d" % i, [128, 16, 128], BF16) for i in range(2)]
            vtm = [sb(st, "p_vtm%d" % i, [128, 1024], BF16) for i in range(2)]
            lf = [sb(st, "p_lf%d" % i, [128, 512], F32) for i in range(2)]
            lfb = [sb(st, "p_lfb%d" % i, [128, 512], BF16) for i in range(2)]
            zTp = [sb(st, "p_zT%d" % i, [17, 128], BF16) for i in range(2)]
            ew = sb(st, "p_ew", [128, 512], F32)
            kpp = sb(st, "p_kpp", [128, 512], BF16)
            el = sb(st, "p_el", [128, 8], F32)
            US = cstb[:, C_US:C_US + 128]
            ONE = cstb[:, C_ONE:C_ONE + 2]
            for i in range(2):
                P.op("pool", lambda e, i=i: e.memset(zTp[i][:], 1.0), writes=[("zTp", i)])
            for (c0, n, off, key, sem) in ((Z_OFF, 16, 1536, ("wst", "z"), s_cp), (KG_OFF, 512, 0, ("wst", "k"), s_w[0]),
                                           (VG_OFF, 512, 512, ("wst", "v0"), s_w[1]), (VG_OFF + 512, 512, 1024, ("wst", "v1"), s_w[0])):
                P.dma("pool", lambda e, c0=c0, n=n, off=off: e.dma_start(
                    out=wst[:, :, off:off + n], in_=w_in[:, c0:c0 + n].rearrange("(kc p) n -> p kc n", p=128)), sem, writes=[key])
            NPS = 0 if 'only5' in _SKIP else NPRE

            def xt_of(slot):
                if slot == NPRE - 1:
                    return xT_main[:, :, 0:128], ("xT", 0)
                return xTb[slot % 2][:], ("xTb", slot % 2)

            def stAB(slot):
                par = slot % 2
                xt, xkey = xt_of(slot)
                xprep(slot, xt, xkey, par)
                mm16(pb[par][:, :], ("ps", par), lambda kc: xt[:, kc, :], lambda kc: wst[:, kc, 0:512], [xkey, ("wst", "k")])
                mm16(pb[4][0:16, 0:128], ("ps", 4, "z"), lambda kc: wst[:, kc, 1536:1552], lambda kc: xt[:, kc, :], [xkey, ("wst", "z")])
                copy_evac(zTp[par][0:16, :], pb[4][0:16, 0:128], [("ps", 4, "z")], [("zTp", par)])
                P.op("pe", lambda e: e.matmul(pb[3][:, :], lhsT=zTp[par][0:17, :], rhs=wa2a[0:17, :], start=True, stop=True),
                     reads=[("zTp", par), "wa2a"], writes=[("ps", 3)])

            def stC(slot):
                par = slot % 2
                for hf in range(2):
                    P.op("act", lambda e, hf=hf: e.activation(out=lf[par][:, hf * 256:(hf + 1) * 256], in_=pb[3][:, hf * 256:(hf + 1) * 256], func=AF.Exp, scale=-1.0),
                         reads=[("ps", 3)], writes=[("lf", par, hf)])
                P.op("act", lambda e: e.activation(out=lf[par][:], in_=lf[par][:], func=AF.Ln, bias=1.0), reads=[("lf", par)], writes=[("lf", par)])
                P.op("dve", lambda e: e.tensor_scalar(out=lfb[par][:], in0=lf[par][:], scalar1=NVC(slot), scalar2=None, op0=ALU.mult),
                     reads=[("lf", par)], writes=[("lfb", par)])

            def stD(slot):
                par = slot % 2
                xt, xkey = xt_of(slot)
                for pi in range(2):
                    mm16(pb[6][:, :], ("ps", 6), lambda kc: xt[:, kc, :], lambda kc, pi=pi: wst[:, kc, 512 + pi * 512:1024 + pi * 512], [xkey, ("wst", "v%d" % pi)])
                    copy_evac(vtm[par][:, pi * 512:(pi + 1) * 512], pb[6][:, :], [("ps", 6)], [("vtm", par, pi)])

            def stE(slot):
                par = slot % 2
                P.op("pe", lambda e: e.matmul(pb[3][:, :], lhsT=US, rhs=lfb[par][:], start=True, stop=True),
                     reads=[("lfb", par), "cstb"], writes=[("ps", 3)])
                for h in range(4):
                    P.op("pe", lambda e, h=h: e.matmul(pb[4][:, 200 + 2 * h:202 + 2 * h], lhsT=lfb[par][:, h * 128:(h + 1) * 128], rhs=ONE, start=True, stop=True),
                         reads=[("lfb", par), "cstb"], writes=[("ps", 4, "bl")], sig=(h == 3))
                for hf in range(2):
                    P.op("act", lambda e, hf=hf: e.activation(out=ew[:, hf * 256:(hf + 1) * 256], in_=pb[3][:, hf * 256:(hf + 1) * 256], func=AF.Exp),
                         reads=[("ps", 3)], writes=[("ew", hf)])
                P.op("act", lambda e: e.activation(out=el[:], in_=pb[4][:, 200:208], func=AF.Exp), reads=[("ps", 4, "bl")], writes=["el"])
                P.op("dve", lambda e: e.tensor_tensor(out=kpp[:], in0=pb[par][:, :], in1=ew[:], op=ALU.mult),
                     reads=[("ps", par), "ew"], writes=["kpp"])

            def stF(slot):
                par = slot % 2
                for half in range(2):
                    bnk = 5 if half == 0 else 7
                    for hh in range(2):
                        h = half * 2 + hh
                        P.op("pe", lambda e, h=h, hh=hh, bnk=bnk: e.matmul(pb[bnk][:, hh * 256:(hh + 1) * 256], lhsT=kpp[:, h * 128:(h + 1) * 128],
                                                                         rhs=vtm[par][:, h * 256:(h + 1) * 256], start=True, stop=True),
                             reads=["kpp", ("vtm", par)], writes=[("ps", bnk, hh)])
                    for hh in range(2):
                        h = half * 2 + hh
                        P.op("dve", lambda e, h=h, hh=hh, bnk=bnk: e.scalar_tensor_tensor(
                            out=Sst[:, h, :], in0=Sst[:, h, :], scalar=el[:, 2 * h:2 * h + 1], in1=pb[bnk][:, hh * 256:(hh + 1) * 256], op0=ALU.mult, op1=ALU.add),
                            reads=[("S", h), "el", ("ps", bnk, hh)], writes=[("S", h)])

            if NPS:
                stAB(0); stC(0); stD(0)
            for slot in range(NPS):
                if slot + 1 < NPS:
                    stAB(slot + 1); stC(slot + 1)
                if "pxE" not in _SKIP:
                    stE(slot)
                if slot + 1 < NPS:
                    stD(slot + 1)
                if "pxE" not in _SKIP and "pxF" not in _SKIP:
                    stF(slot)
            P.op("act", lambda e: e.activation(out=Sbf[:], in_=Sst[:], func=AF.Copy), reads=["S"], writes=["Sbf"])
            P.flush()
        if _STOP == 1:
            st12.close()
            return nc

        for mb in range(1, 1 if 'only5' in _SKIP else NMB):
            xprep(NPRE - 1 + mb, xT_main[:, :, mb * 128:(mb + 1) * 128], ("xT", mb), mb % 2)
        P.flush()
        st12.close()
        if _STOP == 2:
            return nc

        TOKG = [(128, 512), (640, 512), (1152, 128)]

        def fproj(wsl, wkey, lhs_fn, M, tok_groups, consumer):
            for (t0, n) in tok_groups:
                b = nbank()
                mm16(pb[b][0:M, 0:n], ("ps", b), lhs_fn, lambda kc, t0=t0, n=n: xT_main[:, kc, t0:t0 + n], ["xT", wkey])
                consumer(pb[b][0:M, 0:n], ("ps", b), t0, n)

        def tproj(wsl, wkey, c0, ncols, blocks, consumer):
            for mb in blocks:
                b = nbank()
                mm16(pb[b][:, 0:ncols], ("ps", b), lambda kc, mb=mb: xT_main[:, kc, mb * 128:(mb + 1) * 128],
                     lambda kc: wsl[:, kc, c0:c0 + ncols], [("xT", mb), wkey])
                consumer(pb[b][:, 0:ncols], ("ps", b), mb)

        cnt["ev"] = 100
        with ExitStack() as st:
            wsl = [sb(st, "wsl%d" % i, [128, 16, 512], BF16) for i in range(2)]
            QT = sb(st, "QT", [128, 4, NMB * 128], BF16)
            KT = sb(st, "KT", [128, NMB * 128], BF16)
            Va = sb(st, "Va", [128, NMB, 2, 80], BF16)
            sga = sb(st, "sga", [128, NMB, 512], BF16)
            KcT = sb(st, "KcT", [128, 16, 128], BF16)
            Vc = sb(st, "Vc", [128, 16, 2, 80], BF16)
            kcs = sb(st, "kcs", [128, 16, 128], BF16)
            stt = [sb(st, "stt%d" % i, [128, 512], F32) for i in range(2)]
            ptt = [sb(st, "ptt%d" % i, [128, 512], BF16) for i in range(4)]
            oTs = [sb(st, "oTs%d" % i, [65, 512], F32) for i in range(2)]
            zcol = sb(st, "zcol", [128, 1], F32)
            BTp = sb(st, "BTp", [128, 16, 128], F32)
            BTc = sb(st, "BTc", [128, 16, 128], F32)
            BTs = sb(st, "BTs", [128, 16, 16, 8], F32)
            P.op("pool", lambda e: e.memset(BTs[:], NEG), writes=["BTs"])
            if "p3bt" not in _SKIP:
                P.dma("sp", lambda e: e.dma_start(out=BTc[:], in_=bass.AP(fscr.tensor, 127, [[382, 128], [128 * 383, 16], [1, 128]])),
                      s_c, writes=["BTc"])
                P.dma("sp", lambda e: e.dma_start(out=BTp[:], in_=bass.AP(fscr.tensor, 255, [[382, 128], [128 * 383, 16], [1, 128]])),
                      s_c, writes=["BTp"])
                for j in range(16):
                    P.dma("sp", lambda e, j=j: e.dma_start(
                        out=BTs[8 * j:8 * j + 8, j, :, :],
                        in_=bass.AP(fscr.tensor, 127, [[382, 8], [128 * 383, 16], [1, 8]])), s_c, writes=["BTs"])
            P.op("pool", lambda e: e.memset(Va[:], 1.0), writes=["Va"])
            P.op("pool", lambda e: e.memset(Vc[:], 1.0), writes=["Vc"])
            P.op("pool", lambda e: e.memset(zcol[:], 0.0), writes=["zcol"])
            wi = 0
            for p in range(0 if 'only5' in _SKIP else 2):
                w, wk, sem = wsl[wi % 2], ("wsl", wi % 2), s_w[wi % 2]; wi += 1
                for g_ in range(4):
                    for kvl_ in range(2):
                        c0_ = Q_OFF + (2 * p + kvl_) * 256 + g_ * 64
                        off_ = g_ * 128 + kvl_ * 64
                        P.dma("pool", lambda e, c0_=c0_, off_=off_, w=w: e.dma_start(
                            out=w[:, :, off_:off_ + 64], in_=w_in[:, c0_:c0_ + 64].rearrange("(kc p) n -> p kc n", p=128)),
                            (sem if g_ % 2 == 0 else s_cp), writes=[wk + (g_,)])
                for g in range(4):
                    if "p3q" in _SKIP:
                        continue
                    lhs = lambda kc, w=w, g=g: w[:, kc, g * 128:(g + 1) * 128]
                    fproj(w, wk + (g,), lhs, 128, TOKG,
                          lambda ps, pk, t0, n, g=g: copy_evac(QT[:, g, t0:t0 + n], ps, [pk], [("QT", g)], scale=0.125))
                w, wk, sem = wsl[wi % 2], ("wsl", wi % 2), s_w[wi % 2]; wi += 1
                wload(w, wk, [(K_OFF + p * 128, 128), (V_OFF + p * 128, 128)], sem)
                if "p3k" not in _SKIP:
                    fproj(w, wk, lambda kc, w=w: w[:, kc, 0:128], 128, [(0, 512), (512, 512), (1024, 256)],
                          lambda ps, pk, t0, n: copy_evac(KT[:, t0:t0 + n], ps, [pk], ["KT"]))

                def kv_cons(ps, pk, mb, p=p):
                    if "p3va" not in _SKIP:
                        for a_ in range(2):
                            copy_evac(Va[:, mb, a_, 0:64], ps[:, 128 + a_ * 64:128 + (a_ + 1) * 64], [pk], [("Va", mb, a_)])
                    if mb >= 8 and "p3ko" not in _SKIP:
                        P.op("dve", lambda e: e.tensor_copy(out=kout[:, mb - 8, p * 128:(p + 1) * 128], in_=ps[:, 0:128]), reads=[pk], writes=["kout"])
                        P.op("dve", lambda e: e.tensor_copy(out=vout[:, mb - 8, p * 128:(p + 1) * 128], in_=ps[:, 128:256]), reads=[pk], writes=["vout"])
                if "p3kv" not in _SKIP:
                    tproj(w, wk, 0, 256, range(NMB), kv_cons)
                w, wk, sem = wsl[wi % 2], ("wsl", wi % 2), s_w[wi % 2]; wi += 1
                wload(w, wk, [(GA_OFF + p * 512, 512)], sem)
                if "p3g" not in _SKIP:
                    tproj(w, wk, 0, 512, range(1, NMB),
                          lambda ps, pk, mb: P.op("act", lambda e: e.activation(out=sga[:, mb, :], in_=ps, func=AF.Silu), reads=[pk], writes=[("sga", mb)]))
                if "p3cache" in _SKIP:
                    continue
                P.dma("pool", lambda e, p=p: e.dma_start(out=kcs[:], in_=ck[:, :, p * 128:(p + 1) * 128].rearrange("j s c -> s j c")), s_cp, writes=["kcs"])
                for a_ in range(2):
                    P.dma("pool", lambda e, p=p, a_=a_: e.dma_start(
                        out=Vc[:, :, a_, 0:64], in_=cv[:, :, p * 128 + a_ * 64:p * 128 + (a_ + 1) * 64].rearrange("j s d -> s j d")),
                        s_cp, writes=["Vc"])
                for jj in range(2):
                    for j8 in range(8):
                        j = jj * 8 + j8
                        P.op("pe", lambda e, j=j, j8=j8: e.transpose(out=pbb[2][:, j8 * 128:(j8 + 1) * 128], in_=kcs[:, j, :], identity=ident_b),
                             reads=["kcs", "cstb"], writes=[("ps", 2)], sig=(j8 == 7))
                    copy_evac(KcT[:, jj * 8:(jj + 1) * 8, :], pbb[2][:, :].rearrange("p (a b) -> p a b", b=128), [("ps", 2)], ["KcT"])

                def unit_s1(kvl, kprev, kprev_key, kcur, qfn, nq, bprev, bcur, hmcol, vprev, vprev_key, vcur, ocol0, first, last, ui):
                    pr = slice(kvl * 64, kvl * 64 + 64)
                    for kb, (kt, kk, bias) in enumerate(((kprev, kprev_key, bprev), (kcur, "KT", bcur))):
                        sbk = 3 + kb
                        spsum = pb[sbk][:, 0:4 * nq].rearrange("p (g q) -> p g q", g=4)
                        for g in range(4):
                            P.op("pe", lambda e, g=g, kt=kt, spsum=spsum: e.matmul(spsum[:, g, :], lhsT=kt[pr, :], rhs=qfn(g)[pr, :], start=True, stop=True),
                                 reads=[kk, ("QT", g)], writes=[("ps", sbk)], sig=(g == 3))
                        stv = stt[kb][:, 0:4 * nq].rearrange("p (g q) -> p g q", g=4)
                        P.op("dve", lambda e, spsum=spsum, bias=bias, stv=stv, kb=kb: e.scalar_tensor_tensor(
                            out=stv, in0=spsum, scalar=(hmcol if kb == 0 else zcol[:, 0:1]), in1=bias, op0=ALU.add, op1=ALU.add),
                            reads=[("ps", sbk), "BTp", "BTc", "BTs", "cstf", "zcol"], writes=[("stt", kb)])
                        pi = (ui % 2) * 2 + kb
                        P.op("act", lambda e, pi=pi, kb=kb: e.activation(out=ptt[pi][:, 0:4 * nq], in_=stt[kb][:, 0:4 * nq], func=AF.Exp),
                             reads=[("stt", kb)], writes=[("ptt", pi)])

                def unit_s2(kvl, kprev, kprev_key, kcur, qfn, nq, bprev, bcur, hmcol, vprev, vprev_key, vcur, ocol0, first, last, ui):
                    for kb, (vv, vk) in enumerate(((vprev, vprev_key), (vcur, "Va"))):
                        pi = (ui % 2) * 2 + kb
                        oap = pb[5][0:65, ocol0:ocol0 + 4 * nq]
                        P.op("pe", lambda e, vv=vv, pi=pi, oap=oap, kb=kb: e.matmul(
                            oap, lhsT=vv, rhs=ptt[pi][:, 0:4 * nq], start=(kb == 0), stop=(kb == 1)),
                            reads=[vk, ("ptt", pi)], writes=[("ps", 5)])

                def fin_a(kvl, mb, fi):
                    ot = oTs[fi % 2]
                    if mb == 9:
                        P.op("act", lambda e: e.activation(out=ot[:, :].rearrange("p (g j t) -> p g j t", g=4, j=16),
                                                           in_=pb[5][0:65, :].rearrange("p (j g t) -> p g j t", j=16, g=4), func=AF.Copy),
                             reads=[("ps", 5)], writes=[("oTs", fi % 2)])
                    else:
                        P.op("act", lambda e: e.activation(out=ot[:], in_=pb[5][0:65, :], func=AF.Copy), reads=[("ps", 5)], writes=[("oTs", fi % 2)])

                def fin_b(kvl, mb, fi, p=p):
                    ot = oTs[fi % 2]
                    for g in range(4):
                        P.op("pe", lambda e, g=g: e.transpose(out=pb[6][:, g * 65:(g + 1) * 65], in_=ot[0:65, g * 128:(g + 1) * 128],
                                                             identity=ident_f[0:65, 0:65]),
                             reads=[("oTs", fi % 2), "cstf"], writes=[("ps", 6)], sig=(g == 3))
                    for g in range(4):
                        h = (2 * p + kvl) * 4 + g
                        rc = small[:, 16 + g:17 + g]
                        P.op("dve", lambda e, g=g, h=h, rc=rc: e.tensor_tensor(out=rc, in0=pb[6][:, g * 65 + 64:g * 65 + 65], in1=esink[:, h:h + 1], op=ALU.add),
                             reads=[("ps", 6), "esink"], writes=[("small", 16 + g)])
                        P.op("dve", lambda e, rc=rc: e.reciprocal(out=rc, in_=rc), reads=[("small", 16 + g)], writes=[("small", 16 + g)])
                        P.op("dve", lambda e, g=g, h=h, rc=rc: e.scalar_tensor_tensor(
                            out=mixed[:, mb, h * 64:(h + 1) * 64], in0=pb[6][:, g * 65:g * 65 + 64], scalar=rc,
                            in1=sga[:, mb, kvl * 256 + g * 64:kvl * 256 + (g + 1) * 64], op0=ALU.mult, op1=ALU.mult),
                            reads=[("ps", 6), ("small", 16 + g), ("sga", mb)], writes=[("mixed", mb)])

                def samp_s1(kvl, kv, ui):
                    pr = slice(kvl * 64, kvl * 64 + 64)
                    for kb in range(2):
                        sbk = 3 + kb
                        for j in range(16):
                            kt = KcT[:, j, :] if kb == 0 else KT[:, 9 * 128:10 * 128]
                            for g in range(4):
                                P.op("pe", lambda e, g=g, j=j, kt=kt, sbk=sbk: e.matmul(
                                    pb[sbk][:, j * 32 + g * 8:j * 32 + g * 8 + 8], lhsT=kt[pr, :],
                                    rhs=QT[pr, g, 9 * 128 + 8 * j:9 * 128 + 8 * j + 8], start=True, stop=True),
                                    reads=["KcT" if kb == 0 else "KT", ("QT", g)], writes=[("ps", sbk)], sig=(j == 15 and g == 3))
                        for j in range(16):
                            bias = BTp[:, kv * 4:kv * 4 + 4, 0:8] if kb == 0 else BTs[:, j, kv * 4:kv * 4 + 4, :]
                            P.op("dve", lambda e, j=j, bias=bias, sbk=sbk, kb=kb: e.scalar_tensor_tensor(
                                out=stt[kb][:, j * 32:(j + 1) * 32].rearrange("p (g q) -> p g q", g=4),
                                in0=pb[sbk][:, j * 32:(j + 1) * 32].rearrange("p (g q) -> p g q", g=4),
                                scalar=zcol[:, 0:1], in1=bias, op0=ALU.add, op1=ALU.add),
                                reads=[("ps", sbk), "BTp", "BTc", "BTs", "zcol"], writes=[("stt", kb)])
                        pi = (ui % 2) * 2 + kb
                        P.op("act", lambda e, pi=pi, kb=kb: e.activation(out=ptt[pi][:, :], in_=stt[kb][:, :], func=AF.Exp),
                             reads=[("stt", kb)], writes=[("ptt", pi)])

                def samp_s2(kvl, kv, ui):
                    for j in range(16):
                        for kb in range(2):
                            pi = (ui % 2) * 2 + kb
                            vv = Vc[:, j, kvl, 0:65] if kb == 0 else Va[:, 9, kvl, 0:65]
                            P.op("pe", lambda e, vv=vv, pi=pi, j=j, kb=kb: e.matmul(
                                pb[5][0:65, j * 32:(j + 1) * 32], lhsT=vv, rhs=ptt[pi][:, j * 32:(j + 1) * 32], start=(kb == 0), stop=(kb == 1)),
                                reads=["Vc" if kb == 0 else "Va", ("ptt", pi)], writes=[("ps", 5)], sig=(j == 15 and kb == 1))

                units = []
                ui = 0
                for kvl in range(2):
                    if "p3units" in _SKIP:
                        continue
                    kv = 2 * p + kvl
                    for mb in range(1, 9):
                        ua = (kvl, KT[:, (mb - 1) * 128:mb * 128], "KT", KT[:, mb * 128:(mb + 1) * 128],
                              (lambda g, mb=mb: QT[:, g, mb * 128:(mb + 1) * 128]), 128,
                              BTp[:, kv * 4:kv * 4 + 4, :], BTc[:, kv * 4:kv * 4 + 4, :],
                              (HMC if mb == 1 else zcol[:, 0:1]),
                              Va[:, mb - 1, kvl, 0:65], "Va", Va[:, mb, kvl, 0:65], 0, True, True, ui)
                        units.append((lambda ua=ua: unit_s1(*ua), lambda ua=ua: unit_s2(*ua), (kvl, mb)))
                        ui += 1
                    units.append((lambda kvl=kvl, kv=kv, ui=ui: samp_s1(kvl, kv, ui), lambda kvl=kvl, kv=kv, ui=ui: samp_s2(kvl, kv, ui), (kvl, 9)))
                    ui += 1
                if units:
                    units[0][0]()
                for i_, (f1, f2, fin) in enumerate(units):
                    if i_ + 1 < len(units):
                        units[i_ + 1][0]()
                    f2()
                    fin_a(fin[0], fin[1], i_)
                    if i_ >= 1:
                        fin_b(units[i_ - 1][2][0], units[i_ - 1][2][1], i_ - 1)
                if units:
                    fin_b(units[-1][2][0], units[-1][2][1], len(units) - 1)
            if "p3out" in _SKIP:
                P.flush()
                return nc
            P.dma("sp", lambda e: e.dma_start(out=kwin[:, :], in_=kout[:, 0, :]), s_o, reads=["kout"])
            P.dma("sp", lambda e: e.dma_start(out=vwin[:, :], in_=vout[:, 0, :]), s_o, reads=["vout"])
            P.dma("sp", lambda e: e.dma_start(out=ks_out[:, 0:120, :], in_=ck[:, 8:128, :]), s_o)
            P.dma("sp", lambda e: e.dma_start(out=vs_out[:, 0:120, :], in_=cv[:, 8:128, :]), s_o)
            for j in range(16):
                P.dma("sp", lambda e, j=j: e.dma_start(out=ks_out[j, 120:128, :], in_=kout[8 * j:8 * j + 8, 1, :]), s_o, reads=["kout"])
                P.dma("sp", lambda e, j=j: e.dma_start(out=vs_out[j, 120:128, :], in_=vout[8 * j:8 * j + 8, 1, :]), s_o, reads=["vout"])
            P.flush()
            if _STOP == 3:
                return nc

        cnt["ev"] = 174
        with ExitStack() as st:
            wsl = [sb(st, "wslg%d" % i, [128, 16, 512], BF16) for i in range(2)]
            qgT = sb(st, "qgT", [128, 2, NMB * 128], BF16)
            kgT = sb(st, "kgT", [128, 2, NMB * 128], BF16)
            kgt = sb(st, "kgt", [128, NMB, 256], BF16)
            vgt = sb(st, "vgt", [128, NMB, 512], BF16)
            sgg = sb(st, "sgg", [128, NMB, 512], BF16)
            sgtmp = sb(st, "sgtmp", [128, 512], F32)
            gn_b = sb(st, "gn_b", [128, 512], F32)
            junk = sb(st, "junk4", [128, 256], BF16)
            P.dma("sp", lambda e: e.dma_start(out=gn_b[:, 0:256], in_=bcast_rows(gnorm, 256)), s_c, writes=["gn_b"])
            P.dma("sp", lambda e: e.dma_start(out=gn_b[:, 256:512], in_=bcast_rows(gnorm, 256)), s_c, writes=["gn_b"])
            T = gla_tiles(st)
            wz = sb(st, "wz", [128, 16, 16], BF16)
            wload(wz, "wz", [(Z_OFF, 16)], s_w[0])
            fproj(wz, "wz", lambda kc: wz[:, kc, :], 16, TOKG,
                  lambda ps, pk, t0, n: copy_evac(zT[0:16, t0:t0 + n], ps, [pk], ["zT"]))
            wi = 0
            for hp in range(0 if 'only5' in _SKIP else 2):
                w, wk, sem = wsl[wi % 2], ("wslg", wi % 2), s_w[wi % 2]; wi += 1
                wload(w, wk, [(QG_OFF + hp * 256, 256), (KG_OFF + hp * 256, 256)], sem)
                for hh in range(2):
                    fproj(w, wk, lambda kc, w=w, hh=hh: w[:, kc, hh * 128:(hh + 1) * 128], 128, TOKG,
                          lambda ps, pk, t0, n, hh=hh: copy_evac(qgT[:, hh, t0:t0 + n], ps, [pk], ["qgT"]))
                    fproj(w, wk, lambda kc, w=w, hh=hh: w[:, kc, 256 + hh * 128:256 + (hh + 1) * 128], 128, TOKG,
                          lambda ps, pk, t0, n, hh=hh: copy_evac(kgT[:, hh, t0:t0 + n], ps, [pk], ["kgT"]))
                tproj(w, wk, 256, 256, range(1, NMB), lambda ps, pk, mb: copy_evac(kgt[:, mb, :], ps, [pk], [("kgt", mb)]))
                w, wk, sem = wsl[wi % 2], ("wslg", wi % 2), s_w[wi % 2]; wi += 1
                wload(w, wk, [(VG_OFF + hp * 512, 512)], sem)
                tproj(w, wk, 0, 512, range(1, NMB), lambda ps, pk, mb: copy_evac(vgt[:, mb, :], ps, [pk], [("vgt", mb)]))
                w, wk, sem = wsl[wi % 2], ("wslg", wi % 2), s_w[wi % 2]; wi += 1
                wload(w, wk, [(GG_OFF + hp * 512, 512)], sem)

                def gg_cons(ps, pk, mb):
                    P.op("act", lambda e: e.activation(out=sgtmp[:], in_=ps, func=AF.Silu), reads=[pk], writes=["sgtmp"])
                    P.op("dve", lambda e: e.tensor_tensor(out=sgg[:, mb, :], in0=sgtmp[:], in1=gn_b[:], op=ALU.mult),
                         reads=["sgtmp", "gn_b"], writes=[("sgg", mb)])
                tproj(w, wk, 0, 512, range(1, NMB), gg_cons)
                def gb(mb, stage, par):
                    samp = (mb == 9)
                    gla_block(T, (2 * hp, 2 * hp + 1), zT[0:17, mb * 128:(mb + 1) * 128], "zT", kgt[:, mb, :], ("kgt", mb),
                              vgt[:, mb, :], ("vgt", mb), NVC(NPRE - 1 + mb), 8 if samp else 64,
                              [qgT[:, hh, mb * 128:(mb + 1) * 128] for hh in range(2)],
                              [kgT[:, hh, mb * 128:(mb + 1) * 128] for hh in range(2)], ("qgT", "kgT"),
                              sgg[:, mb, :], ("sgg", mb),
                              mixed[:, mb, 1024 + hp * 512:1024 + (hp + 1) * 512], ("mixed", mb), sample=samp, stage=stage, par=par)
                if "gseq" in _SKIP:
                    for mb in range(1, 9):
                        gb(mb, 1, mb % 2)
                        gb(mb, 2, mb % 2)
                else:
                    gb(1, 1, 1)
                    for mb in range(1, 9):
                        if mb + 1 < 9:
                            gb(mb + 1, 1, (mb + 1) % 2)
                        gb(mb, 2, mb % 2)
                gb(9, "all", 0)
            for h in range(4):
                P.dma("sp", lambda e, h=h: e.dma_start(out=glast[h, :, :], in_=Sst[:, h, :]), s_o, reads=[("S", h)])
            P.flush()
            if _STOP == 4:
                return nc

        cnt["ev"] = 237
        with ExitStack() as st:
            wo = sb(st, "wo", [128, 16, D], BF16)
            npost_b = sb(st, "npost_b", [128, D], F32)
            mT = sb(st, "mT", [128, 16, 128], BF16)
            osb = sb(st, "osb", [128, D], F32)
            xs = [sb(st, "xs5_%d" % i, [128, D], F32) for i in range(2)]
            junk = sb(st, "junk5", [128, D], BF16)
            for i in range(4):
                P.dma("pool", lambda e, i=i: e.dma_start(out=wo[:, :, i * 512:(i + 1) * 512],
                                                       in_=w_out[:, i * 512:(i + 1) * 512].rearrange("(kc p) n -> p kc n", p=128)),
                      s_w[i % 2], writes=[("wo", i)])
            P.dma("sp", lambda e: e.dma_start(out=npost_b[:], in_=bcast_rows(npost, D)), s_c, writes=["npost_b"])
            for mb in range(1, NMB):
                slot = NPRE - 1 + mb
                i = cnt["x"] % 2
                cnt["x"] += 1
                P.dma("sp", lambda e, i=i, slot=slot: e.dma_start(out=xs[i][:], in_=xp[slot * 128:(slot + 1) * 128, :]), s_ld[i], writes=[("xs", i)])
                for half in range(2):
                    for j in range(8):
                        kc = half * 8 + j
                        P.op("pe", lambda e, kc=kc, j=j, mb=mb: e.transpose(out=pbb[2][:, j * 128:(j + 1) * 128],
                                                                         in_=mixed[:, mb, kc * 128:(kc + 1) * 128], identity=ident_b),
                             reads=[("mixed", mb), "cstb"], writes=[("ps", 2)], sig=(j == 7))
                    copy_evac(mT[:, half * 8:(half + 1) * 8, :], pbb[2][:, :].rearrange("p (a b) -> p a b", b=128), [("ps", 2)], ["mT"])
                for n in range(4):
                    b = 4 + n
                    mm16(pb[b][:, :], ("ps", b), lambda kc: mT[:, kc, :], lambda kc, n=n: wo[:, kc, n * 512:(n + 1) * 512], ["mT", ("wo", n)])
                    P.op("dve", lambda e, n=n, b=b: e.tensor_copy(out=osb[:, n * 512:(n + 1) * 512], in_=pb[b][:, :]), reads=[("ps", b)], writes=[("osb", n)])
                P.op("act", lambda e: e.activation(out=junk[:], in_=osb[:], func=AF.Square, accum_out=small[:, 30:31]),
                     reads=["osb"], writes=["junk", ("small", 30)])
                P.op("act", lambda e: e.activation(out=small[:, 31:32], in_=small[:, 30:31], func=AF.Ln, bias=epst[:, 0:1], scale=1.0 / D),
                     reads=[("small", 30), "epst"], writes=[("small", 31)])
                P.op("act", lambda e: e.activation(out=small[:, 32:33], in_=small[:, 31:32], func=AF.Exp, scale=-0.5),
                     reads=[("small", 31)], writes=[("small", 32)])
                P.op("dve", lambda e: e.scalar_tensor_tensor(out=osb[:], in0=osb[:], scalar=small[:, 32:33], in1=npost_b[:], op0=ALU.mult, op1=ALU.mult),
                     reads=["osb", ("small", 32), "npost_b"], writes=["osb"])
                P.op("dve" if "p5pool" in _SKIP else "pool", lambda e, i=i: e.tensor_tensor(out=osb[:], in0=osb[:], in1=xs[i][:], op=ALU.add), reads=["osb", ("xs", i)], writes=["osb"])
                if "p5out" in _SKIP:
                    continue
                if mb < 9:
                    P.dma("sp", lambda e, mb=mb: e.dma_start(out=y_main[(mb - 1) * 128:mb * 128, :], in_=osb[:]), s_o, reads=["osb"])
                else:
                    P.dma("sp", lambda e: e.dma_start(out=y_samp[:, :], in_=osb[:]), s_o, reads=["osb"])
            P.flush()
            if _STOP == 5:
                return nc
    return nc


_CACHE = {}


def kernel(x_prompt, x_sample, cache_k_win, cache_v_win, state_gla, meta_tokens, rel_bias,
           norm_pre, norm_post, w_in, w_a2, b_a, attn_sinks, gla_norm, w_out):
    f = lambda a: np.ascontiguousarray(np.asarray(a, dtype=np.float32))
    x_prompt, x_sample = f(x_prompt), f(x_sample)
    ckw, cvw, sgl = f(cache_k_win)[0], f(cache_v_win)[0], f(state_gla)[0]
    meta = f(meta_tokens)
    cbase = make_consts()
    ii = np.arange(128)
    cB = np.zeros((128, NCB), np.float32)
    cB[:, 0:C_OH] = cbase[:, 0:C_OH]
    cB[:, C_US:C_US + 128] = (ii[:, None] > ii[None, :])
    cB[:, C_ONE:C_ONE + 2] = 1.0
    ohm_h = np.ascontiguousarray(cbase[0:33, C_OH:C_OH + 383])
    if "nc" not in _CACHE:
        _CACHE["nc"] = build()
    nc = _CACHE["nc"]
    w_in0, w_out0 = f(w_in)[0], f(w_out)[0]
    in_maps = []
    for c in range(8):
        b, q = c // 4, c % 4
        xpad = np.zeros((33 * 128, D), np.float32)
        xpad[112:128] = meta
        xpad[128:] = x_prompt[b]
        nblk = 8 * q + 9
        xc = np.zeros((NSLOT * 128, D), np.float32)
        xc[(33 - nblk) * 128:33 * 128] = xpad[:nblk * 128]
        xc[33 * 128:] = x_sample[16 * c:16 * c + 16].reshape(128, D)
        valid = np.zeros((NSLOT * 128,), np.float32)
        vpad = np.ones((33 * 128,), np.float32)
        vpad[:112] = 0
        valid[(33 - nblk) * 128:33 * 128] = vpad[:nblk * 128]
        valid[33 * 128:] = 1
        cs = np.zeros((128, 163), np.float32)
        cs[:, 0:128] = cbase[:, C_ID:C_ID + 128]
        cs[:, 128:128 + NSLOT] = (-valid / 16.0).reshape(NSLOT, 128).T
        if q == 0:
            cs[:112, 162] = NEG
        in_maps.append({
            "xp": xc, "w_in": w_in0, "w_out": w_out0,
            "ck": np.ascontiguousarray(ckw[16 * c:16 * c + 16].reshape(16, 128, 256)),
            "cv": np.ascontiguousarray(cvw[16 * c:16 * c + 16].reshape(16, 128, 256)),
            "sg": np.ascontiguousarray(sgl[16 * c:16 * c + 16]),
            "cstS": cs, "cstB": cB, "ohm": ohm_h, "relb": f(rel_bias), "npre": f(norm_pre), "npost": f(norm_post),
            "wa2": f(w_a2)[0], "ba": f(b_a), "sinks": f(attn_sinks), "gnorm": f(gla_norm),
        })
    res = run_bass_kernel_spmd(nc, in_maps, core_ids=list(range(8)))
    R = res.results
    y_prompt = np.zeros((2, 4096, D), np.float32)
    y_sample = np.zeros((128, 8, D), np.float32)
    kwp = np.zeros((1, 2, 128, 4, 64), np.float32)
    vwp = np.zeros((1, 2, 128, 4, 64), np.float32)
    sgp = np.zeros((1, 2, 4, 128, 256), np.float32)
    kws = np.zeros((1, 128, 128, 4, 64), np.float32)
    vws = np.zeros((1, 128, 128, 4, 64), np.float32)
    sgs = np.zeros((1, 128, 4, 128, 256), np.float32)
    for c in range(8):
        b, q = c // 4, c % 4
        r = R[c]
        y_prompt[b, q * 1024:(q + 1) * 1024] = r["y_main"]
        y_sample[16 * c:16 * c + 16] = r["y_samp"].reshape(16, 8, D)
        kws[0, 16 * c:16 * c + 16] = r["ks_out"].reshape(16, 128, 4, 64)
        vws[0, 16 * c:16 * c + 16] = r["vs_out"].reshape(16, 128, 4, 64)
        sgs[0, 16 * c:16 * c + 16] = r["gs_out"]
        if q == 3:
            kwp[0, b] = r["kwin"].reshape(128, 4, 64)
            vwp[0, b] = r["vwin"].reshape(128, 4, 64)
            sgp[0, b] = r["glast"]
    return (y_prompt, y_sample, kwp, vwp, sgp, kws, vws, sgs)
```

```python
import math
from contextlib import ExitStack
import numpy as np
import concourse.bass as bass
import concourse.mybir as mybir
from concourse.bass_utils import run_bass_kernel_spmd

F32 = mybir.dt.float32
BF16 = mybir.dt.bfloat16
AF = mybir.ActivationFunctionType
ALU = mybir.AluOpType

D = 2048
INW = 5648
Q_OFF, K_OFF, V_OFF, GA_OFF, QG_OFF, KG_OFF, VG_OFF, GG_OFF, Z_OFF = 0, 1024, 1280, 1536, 2560, 3072, 3584, 4608, 5632
NSLOT = 34
NPRE = 25
NMB = 10
NEG = -30000.0
EPS = 1e-6

C_ID, C_U64, C_U8, C_CI64, C_CI8, C_CMK64, C_CMK8, C_OH, C_NV, C_HM = (
    0, 128, 256, 384, 386, 402, 658, 2706, 3089, 3123)
NCST = 3124
C_US, C_ONE, NCB = 2752, 2880, 2888


class Prog:
    ENG = ("pe", "act", "dve", "pool", "sp")

    def __init__(self, nc, stack):
        self.nc = nc
        self.stack = stack
        self.sems = {e: stack.enter_context(nc.semaphore("s_" + e)) for e in self.ENG}
        self.cnt = {e: 0 for e in self.ENG}
        self.dsems = []
        self._reset()

    def _reset(self):
        self.q = {e: [] for e in self.ENG}
        self.regs = {}
        self.children = {}

    def dma_sem(self, name):
        d = {"h": self.stack.enter_context(self.nc.semaphore(name)), "v": 0}
        self.dsems.append(d)
        return d

    def _reg(self, k):
        r = self.regs.get(k)
        if r is None:
            r = self.regs[k] = {"w": {}, "r": {}}
            for i in range(1, len(k)):
                self.children.setdefault(k[:i], set()).add(k)
        return r

    def _conf(self, k):
        out = []
        for i in range(1, len(k) + 1):
            r = self.regs.get(k[:i])
            if r is not None:
                out.append(r)
        for c in self.children.get(k, ()):
            out.append(self.regs[c])
        return out

    @staticmethod
    def _merge(dst, src):
        for sk, (h, v) in src.items():
            if sk not in dst or dst[sk][1] < v:
                dst[sk] = (h, v)

    def _deps(self, reads, writes):
        need = {}
        for k in reads:
            for c in self._conf(k):
                self._merge(need, c["w"])
        for k in writes:
            for c in self._conf(k):
                self._merge(need, c["w"])
                self._merge(need, c["r"])
        return need

    def _record(self, tok, reads, writes):
        sk, h, v = tok
        for k in reads:
            self._merge(self._reg(k)["r"], {sk: (h, v)})
        for k in writes:
            r = self._reg(k)
            for c in self._conf(k):
                if c is not r:
                    self._merge(c["w"], {sk: (h, v)})
            r["w"] = {sk: (h, v)}
            r["r"] = {}

    @staticmethod
    def _norm(k):
        k = tuple(k) if isinstance(k, (tuple, list)) else (k,)
        return k[:2] if k[0] == "ps" else k

    def op(self, eng, fn, reads=(), writes=(), sig=True):
        reads = [self._norm(k) for k in reads]
        writes = [self._norm(k) for k in writes]
        writes = writes + [k for k in reads if k[0] == "ps" and k not in writes]
        reads = [k for k in reads if k[0] != "ps"]
        need = self._deps(reads, writes)
        if sig:
            self.cnt[eng] += 1
            self._record((eng, self.sems[eng], self.cnt[eng]), reads, writes)
        self.q[eng].append((need, fn, sig, None))

    def dma(self, eng, fn, sem, reads=(), writes=()):
        reads = [self._norm(k) for k in reads]
        writes = [self._norm(k) for k in writes]
        need = self._deps(reads, writes)
        if sem["v"] > 0:
            self._merge(need, {id(sem): (sem["h"], sem["v"])})
        sem["v"] += 16
        self._record((id(sem), sem["h"], sem["v"]), reads, writes)
        self.q[eng].append((need, fn, False, sem))

    def flush(self):
        need = {id(d): (d["h"], d["v"]) for d in self.dsems if d["v"] > 0}
        for e in self.ENG:
            self.q[e].append((dict(need), lambda h: h.nop(), False, None))
        nc = self.nc
        with nc.Block() as block:
            handles = {"pe": block.tensor, "act": block.scalar, "dve": block.vector,
                       "pool": block.gpsimd, "sp": block.sync}
            for e in self.ENG:
                def body(h, lst=self.q[e], e=e):
                    waited = {}
                    for need_, fn, sig, dsem in lst:
                        for sk, (sh, v) in need_.items():
                            if waited.get(sk, 0) >= v:
                                continue
                            waited[sk] = v
                            h.wait_ge(sh, v)
                        ins = fn(h)
                        if sig:
                            ins.then_inc(self.sems[e], 1)
                        if dsem is not None:
                            ins.then_inc(dsem["h"], 16)
                handles[e](body)
        self._reset()


def t5_bucket_np(dist):
    d = np.maximum(dist, 0)
    ratio = np.maximum(d, 16).astype(np.float32) / np.float32(16)
    large = 16 + (np.log(ratio) / np.float32(math.log(128 / 16)) * 16).astype(np.int32)
    large = np.minimum(large, 31)
    return np.where(d < 16, d, large)


def make_consts():
    c = np.zeros((128, NCST), np.float32)
    i = np.arange(128)
    c[:, C_ID:C_ID + 128] = np.eye(128)
    for off, C in ((C_U64, 64), (C_U8, 8)):
        c[:, off:off + 128] = ((i[:, None] <= i[None, :]) & (i[:, None] // C == i[None, :] // C))
    c[:, C_CI64:C_CI64 + 2] = (i[:, None] // 64 == np.arange(2)[None, :])
    c[:, C_CI8:C_CI8 + 16] = (i[:, None] // 8 == np.arange(16)[None, :])
    c[:, C_CMK64:C_CMK64 + 256] = np.tile((np.arange(2)[:, None] == i[None, :] // 64).reshape(1, 256), (128, 1))
    c[:, C_CMK8:C_CMK8 + 2048] = np.tile((np.arange(16)[:, None] == i[None, :] // 8).reshape(1, 2048), (128, 1))
    m = np.arange(383)
    dist = m - 127
    valid = (dist >= 0) & (dist < 128)
    bk = t5_bucket_np(dist)
    oh = np.zeros((33, 383), np.float32)
    oh[bk[valid], m[valid]] = 1.0
    oh[32, ~valid] = NEG
    c[:33, C_OH:C_OH + 383] = oh
    return c


_STOP = None
_SKIP = set()


def build():
    nc = bass.Bass("TRN2", target_bir_lowering=False)
    dt_in = lambda n, s: nc.dram_tensor(n, s, F32, kind="ExternalInput").ap()
    dt_out = lambda n, s: nc.dram_tensor(n, s, F32, kind="ExternalOutput").ap()
    xp = dt_in("xp", [NSLOT * 128, D])
    w_in = dt_in("w_in", [D, INW])
    w_out = dt_in("w_out", [D, D])
    ck = dt_in("ck", [16, 128, 256])
    cv = dt_in("cv", [16, 128, 256])
    sg = dt_in("sg", [16, 4, 128, 256])
    cstS_d = dt_in("cstS", [128, 163])
    cstB_d = dt_in("cstB", [128, NCB])
    ohm_d = dt_in("ohm", [33, 383])
    relb = dt_in("relb", [32, 16])
    npre = dt_in("npre", [1, D])
    npost = dt_in("npost", [1, D])
    wa2 = dt_in("wa2", [16, 512])
    ba = dt_in("ba", [1, 512])
    sinks = dt_in("sinks", [1, 16])
    gnorm = dt_in("gnorm", [1, 256])
    y_main = dt_out("y_main", [8 * 128, D])
    y_samp = dt_out("y_samp", [128, D])
    kwin = dt_out("kwin", [128, 256])
    vwin = dt_out("vwin", [128, 256])
    glast = dt_out("glast", [4, 128, 256])
    ks_out = dt_out("ks_out", [16, 128, 256])
    vs_out = dt_out("vs_out", [16, 128, 256])
    gs_out = dt_out("gs_out", [16, 4, 128, 256])
    fscr = nc.dram_tensor("fscr", [16, 128, 383], F32, kind="Internal").ap()

    def bcast_rows(ap2d, n):
        return bass.AP(ap2d.tensor, 0, [[0, 128], [1, n]])

    with ExitStack() as top:
        P = Prog(nc, top)
        sb = lambda st, name, shape, dt: st.enter_context(nc.sbuf_tensor(name, shape, dt, align_bytes=64))
        cstS = sb(top, "cstS_sb", [128, 163], F32)
        cstb = sb(top, "cstb", [128, NCB], BF16)
        esink = sb(top, "esink", [128, 16], F32)
        epst = sb(top, "epst", [128, 1], F32)
        wa2a = sb(top, "wa2a", [17, 512], BF16)
        Sst = sb(top, "Sst", [128, 4, 256], F32)
        Sbf = sb(top, "Sbf", [128, 4, 256], BF16)
        rstd_all = sb(top, "rstd_all", [128, NSLOT], F32)
        small = sb(top, "small", [128, 64], F32)
        zT = sb(top, "zT", [17, NMB * 128], BF16)
        xT_main = sb(top, "xT_main", [128, 16, NMB * 128], BF16)
        mixed = sb(top, "mixed", [128, NMB, D], BF16)
        kout = sb(top, "kout", [128, 2, 256], F32)
        vout = sb(top, "vout", [128, 2, 256], F32)
        pb = [top.enter_context(nc.psum_tensor("pb%d" % i, [128, 512], F32)) for i in range(8)]
        pbb = [t[:].bitcast(BF16) for t in pb]
        ident_b = cstb[:, C_ID:C_ID + 128]
        ident_f = cstS[:, 0:128]
        NVC = lambda slot: cstS[:, 128 + slot:129 + slot]
        HMC = cstS[:, 162:163]
        s_ld = [P.dma_sem("s_ld%d" % i) for i in range(2)]
        s_c = P.dma_sem("s_c")
        s_cp = P.dma_sem("s_cp")
        s_w = [P.dma_sem("s_w%d" % i) for i in range(2)]
        s_o = P.dma_sem("s_o")
        s_mf = [P.dma_sem("s_mf%d" % i) for i in range(4)]
        s_mb = [P.dma_sem("s_mb%d" % i) for i in range(4)]
        s_mo = [P.dma_sem("s_mo%d" % i) for i in range(4)]

        P.dma("sp", lambda e: e.dma_start(out=cstS[:], in_=cstS_d[:, :]), s_c, writes=["cstf"])
        P.dma("pool", lambda e: e.dma_start(out=cstb[:], in_=cstB_d[:, :]), s_w[0], writes=["cstb"])
        if "bcast" not in _SKIP:
            P.dma("sp", lambda e: e.dma_start(out=esink[:], in_=bcast_rows(sinks, 16)), s_c, writes=["esink"])
        if "wa2" not in _SKIP:
            P.dma("pool", lambda e: e.dma_start(out=wa2a[0:16, :], in_=wa2[:, :]), s_cp, writes=["wa2a"])
            P.dma("pool", lambda e: e.dma_start(out=wa2a[16:17, :], in_=ba[:, :]), s_cp, writes=["wa2a"])
        P.op("pool", lambda e: e.memset(epst[:], EPS), writes=["epst"])
        P.op("pool", lambda e: e.memset(Sst[:], 0.0), writes=["S"])
        P.op("pool", lambda e: e.memset(Sbf[:], 0.0), writes=["Sbf"])
        P.op("pool", lambda e: e.memset(zT[:], 1.0), writes=["zT"])
        P.op("act", lambda e: e.activation(out=esink[:], in_=esink[:], func=AF.Exp), reads=["esink"], writes=["esink"])
        with ExitStack() as st:
            rba = sb(st, "rba", [33, 16], F32)
            fes = sb(st, "fes", [16, 383], F32)
            ohm = sb(st, "ohm_sb", [33, 383], F32)
            P.dma("sp", lambda e: e.dma_start(out=ohm[:], in_=ohm_d[:, :]), s_c, writes=["ohm"])
            P.op("pool", lambda e: e.memset(rba[:], 1.0), writes=["rba"])
            P.dma("sp", lambda e: e.dma_start(out=rba[0:32, :], in_=relb[:, :]), s_c, reads=[], writes=["rba"])
            if "f32mm" not in _SKIP:
                P.op("pe", lambda e: e.matmul(pb[0][0:16, 0:383], lhsT=rba[0:33, :], rhs=ohm[:, :],
                                             start=True, stop=True), reads=["rba", "ohm"], writes=[("ps", 0)])
                P.op("dve", lambda e: e.tensor_copy(out=fes[:], in_=pb[0][0:16, 0:383]), reads=[("ps", 0)], writes=["fes"])
            if "fscr" not in _SKIP:
                P.dma("sp", lambda e: e.dma_start(out=fscr[:, :, :], in_=fes[:].unsqueeze(1).broadcast_to([16, 128, 383])),
                      s_c, reads=["fes"], writes=["fscr"])
            P.flush()
            if _STOP == 0:
                return nc

        cnt = {"x": 0, "ev": 0}

        def xprep(slot, dst, dkey, par=0):
            i = cnt["x"] % 2
            cnt["x"] += 1
            P.dma("sp", lambda e: e.dma_start(out=xs[i][:], in_=xp[slot * 128:(slot + 1) * 128, :]), s_ld[i], writes=[("xs", i)])
            P.op("act", lambda e: e.activation(out=junk[:], in_=xs[i][:], func=AF.Square, accum_out=small[:, 2 * par:2 * par + 1]),
                 reads=[("xs", i)], writes=["junk", ("small", 2 * par)])
            P.op("act", lambda e: e.activation(out=small[:, 2 * par + 1:2 * par + 2], in_=small[:, 2 * par:2 * par + 1], func=AF.Ln, bias=epst[:, 0:1], scale=1.0 / D),
                 reads=[("small", 2 * par), "epst"], writes=[("small", 2 * par + 1)])
            P.op("act", lambda e: e.activation(out=rstd_all[:, slot:slot + 1], in_=small[:, 2 * par + 1:2 * par + 2], func=AF.Exp, scale=-0.5),
                 reads=[("small", 2 * par + 1)], writes=[("rstd", slot)])
            P.op("dve", lambda e: e.scalar_tensor_tensor(out=xb[par][:], in0=xs[i][:], scalar=rstd_all[:, slot:slot + 1], in1=npre_b[:],
                                                         op0=ALU.mult, op1=ALU.mult),
                 reads=[("xs", i), ("rstd", slot), "npre_b"], writes=[("xb", par)])
            for half in range(2):
                for j in range(8):
                    kc = half * 8 + j
                    P.op("pe", lambda e, kc=kc, j=j: e.transpose(out=pbb[2][:, j * 128:(j + 1) * 128],
                                                                in_=xb[par][:, kc * 128:(kc + 1) * 128], identity=ident_b),
                         reads=[("xb", par), "cstb"], writes=[("ps", 2)], sig=(j == 7))
                src = pbb[2][:, :].rearrange("p (a b) -> p a b", b=128)
                if half == 0:
                    P.op("act", lambda e, half=half, src=src: e.activation(out=dst[:, half * 8:(half + 1) * 8, :], in_=src, func=AF.Copy),
                         reads=[("ps", 2)], writes=[dkey])
                else:
                    P.op("dve", lambda e, half=half, src=src: e.tensor_copy(out=dst[:, half * 8:(half + 1) * 8, :], in_=src),
                         reads=[("ps", 2)], writes=[dkey])

        def wload(dst, dkey, segs, sem):
            off = 0
            for (c0, n) in segs:
                P.dma("pool", lambda e, c0=c0, n=n, off=off: e.dma_start(
                    out=dst[:, :, off:off + n], in_=w_in[:, c0:c0 + n].rearrange("(kc p) n -> p kc n", p=128)),
                    sem, writes=[dkey])
                off += n

        def mm16(out_ap, okey, lhs_fn, rhs_fn, rkeys):
            for kc in range(16):
                P.op("pe", lambda e, kc=kc: e.matmul(out_ap, lhsT=lhs_fn(kc), rhs=rhs_fn(kc), start=(kc == 0), stop=(kc == 15)),
                     reads=rkeys, writes=[okey], sig=(kc == 15))

        def evac(fn_act, fn_dve, reads, writes):
            cnt["ev"] += 1
            if cnt["ev"] % 2:
                P.op("act", fn_act, reads=reads, writes=writes)
            else:
                P.op("dve", fn_dve, reads=reads, writes=writes)

        def copy_evac(dst, src, reads, writes, scale=None):
            if scale is None:
                evac(lambda e: e.activation(out=dst, in_=src, func=AF.Copy), lambda e: e.tensor_copy(out=dst, in_=src), reads, writes)
            else:
                evac(lambda e: e.mul(dst, src, scale),
                     lambda e: e.tensor_scalar(out=dst, in0=src, scalar1=scale, scalar2=None, op0=ALU.mult), reads, writes)

        pcnt = {"b": 0}

        def nbank():
            pcnt["b"] ^= 1
            return pcnt["b"]

        def gla_block(T, heads, zT_ap, zkey, ktm, ktm_key, vtm, vtm_key, nvcol, C,
                      qT, kT, fkeys, sgg, sgg_key, mixed_dst, mixed_key, sample=False, stage="all", par=0):
            nch = 128 // C
            Uo, CIo, CMKo = (C_U64, C_CI64, C_CMK64) if C == 64 else (C_U8, C_CI8, C_CMK8)
            U = cstb[:, Uo:Uo + 128]
            CI = cstb[:, CIo:CIo + nch]
            CMK = cstb[:, CMKo:CMKo + nch * 128].rearrange("p (c t) -> p c t", t=128)
            h0 = heads[0]
            kpx, qpx, atm, el = T["kpx"][par], T["qpx"][par], T["atm"][par], T["el"][par]
            if stage in ("all", 1):
                P.op("pe", lambda e: e.matmul(pb[3][:, 0:256], lhsT=zT_ap, rhs=wa2a[0:17, h0 * 128:h0 * 128 + 256], start=True, stop=True),
                     reads=[zkey, "wa2a"], writes=[("ps", 3)])
                P.op("act", lambda e: e.activation(out=T["lf"][:], in_=pb[3][:, 0:256], func=AF.Exp, scale=-1.0),
                     reads=[("ps", 3)], writes=["lf"])
                P.op("act", lambda e: e.activation(out=T["lf"][:], in_=T["lf"][:], func=AF.Ln, bias=1.0), reads=["lf"], writes=["lf"])
                P.op("dve", lambda e: e.tensor_scalar(out=T["lfb"][:], in0=T["lf"][:], scalar1=nvcol, scalar2=None, op0=ALU.mult),
                     reads=["lf", "cstf"], writes=["lfb"])
                P.op("pe", lambda e: e.matmul(pb[3][:, 0:256], lhsT=U, rhs=T["lfb"][:], start=True, stop=True),
                     reads=["lfb", "cstb"], writes=[("ps", 3)])
                P.op("act", lambda e: e.activation(out=T["einv"][:], in_=pb[3][:, 0:256], func=AF.Exp, scale=-1.0),
                     reads=[("ps", 3)], writes=["einv"])
                P.op("dve", lambda e: e.tensor_tensor(out=T["kp"][:], in0=ktm, in1=T["einv"][:], op=ALU.mult),
                     reads=[ktm_key, "einv"], writes=["kp"])
                for hh in range(2):
                    P.op("pe", lambda e, hh=hh: e.matmul(pb[4][:, 256 + hh * 16:256 + hh * 16 + nch], lhsT=T["lfb"][:, hh * 128:(hh + 1) * 128],
                                                        rhs=CI, start=True, stop=True),
                         reads=["lfb", "cstb"], writes=[("ps", 4, "bl")], sig=(hh == 1))
                P.op("act", lambda e: e.activation(out=el[:], in_=pb[4][:, 256:288], func=AF.Exp), reads=[("ps", 4, "bl")], writes=[("el", par)])
                for hh in range(2):
                    P.op("pe", lambda e, hh=hh: e.matmul(pb[4][:, hh * 128:(hh + 1) * 128], lhsT=T["lfb"][:, hh * 128:(hh + 1) * 128],
                                                        rhs=U, start=True, stop=True),
                         reads=["lfb", "cstb"], writes=[("ps", 4, "bf")], sig=(hh == 1))
                P.op("act", lambda e: e.activation(out=T["efm"][:], in_=pb[4][:, 0:256], func=AF.Exp), reads=[("ps", 4, "bf")], writes=["efm"])
                P.op("act", lambda e: e.activation(out=T["eifm"][:], in_=pb[4][:, 0:256], func=AF.Exp, scale=-1.0),
                     reads=[("ps", 4, "bf")], writes=["eifm"])
                for hh in range(2):
                    P.op("dve", lambda e, hh=hh: e.scalar_tensor_tensor(out=T["qpT"][:, hh, :], in0=qT[hh], scalar=128.0 ** -0.5,
                                                                       in1=T["efm"][:, hh * 128:(hh + 1) * 128], op0=ALU.mult, op1=ALU.mult),
                         reads=list(fkeys) + ["efm"], writes=[("qpT", hh)])
                    P.op("pool", lambda e, hh=hh: e.tensor_tensor(out=T["kpT"][:, hh, :], in0=kT[hh], in1=T["eifm"][:, hh * 128:(hh + 1) * 128],
                                                                 op=ALU.mult),
                         reads=list(fkeys) + ["eifm"], writes=[("kpT", hh)])
                for hh in range(2):
                    P.op("pool", lambda e, hh=hh: e.tensor_tensor(
                        out=kpx[:, hh, 0:nch, :], in0=T["kp"][:, hh * 128:(hh + 1) * 128].unsqueeze(1).broadcast_to([128, nch, 128]),
                        in1=CI.unsqueeze(2).broadcast_to([128, nch, 128]), op=ALU.mult),
                        reads=["kp", "cstb"], writes=[("kpx", par, hh)])
                    abk = 5 if hh == 0 else 2
                    P.op("pe", lambda e, hh=hh, abk=abk: e.matmul(pb[abk][:, 0:128], lhsT=T["kpT"][:, hh, :], rhs=T["qpT"][:, hh, :],
                                                                 start=True, stop=True),
                         reads=[("kpT", hh), ("qpT", hh)], writes=[("ps", abk)])
                    P.op("dve", lambda e, hh=hh, abk=abk: e.tensor_tensor(out=atm[:, hh, :], in0=pb[abk][:, 0:128], in1=U, op=ALU.mult),
                         reads=[("ps", abk), "cstb"], writes=[("atm", par, hh)])
                    P.op("dve", lambda e, hh=hh: e.tensor_tensor(
                        out=qpx[:, hh, 0:nch, :], in0=T["qpT"][:, hh, :].unsqueeze(1).broadcast_to([128, nch, 128]), in1=CMK, op=ALU.mult),
                        reads=[("qpT", hh), "cstb"], writes=[("qpx", par, hh)])
            if stage not in ("all", 2):
                return
            for hh in range(2):
                h = heads[hh]
                obk = 6 if hh == 0 else 0
                tbk = 7 if hh == 0 else 1
                ob = pb[obk][:, 0:256]
                P.op("pe", lambda e, hh=hh, ob=ob: e.matmul(ob, lhsT=atm[:, hh, :], rhs=vtm[:, hh * 256:(hh + 1) * 256], start=True, stop=False),
                     reads=[("atm", par, hh), vtm_key], writes=[("ps", obk)], sig=False)

                def s_load(c, h=h, hh=hh):
                    i = (c + hh) % 4
                    s0f, s0b = T["s0f"][i], T["s0b"][i]
                    P.dma("sp", lambda e, c=c, h=h, s0f=s0f: e.dma_start(out=s0f[:], in_=sg[c, h, :, :]), s_mf[i], writes=[("s0f", i)])
                    P.dma("pool", lambda e, c=c, h=h, s0b=s0b: e.dma_start(out=s0b[:], in_=sg[c, h, :, :]), s_mb[i], writes=[("s0b", i)])
                for c in range(nch):
                    if sample:
                        if c == 0:
                            for c2 in range(3):
                                s_load(c2)
                        if c + 3 < nch:
                            s_load(c + 3)
                        i = (c + hh) % 4
                        s0f, s0b = T["s0f"][i], T["s0b"][i]
                        Sf, Sb, skf, skb = s0f[:], s0b[:], ("s0f", i), ("s0b", i)
                    else:
                        Sf, Sb, skf, skb = Sst[:, h, :], Sbf[:, h, :], ("S", h), ("Sbf", h)
                    P.op("pe", lambda e, hh=hh, c=c, Sb=Sb, ob=ob: e.matmul(ob, lhsT=qpx[:, hh, c, :], rhs=Sb, start=False, stop=(c == nch - 1)),
                         reads=[("qpx", par, hh), skb], writes=[("ps", obk)])
                    P.op("pe", lambda e, hh=hh, c=c, tbk=tbk: e.matmul(pb[tbk][:, 0:256], lhsT=kpx[:, hh, c, :], rhs=vtm[:, hh * 256:(hh + 1) * 256],
                                                                      start=True, stop=True),
                         reads=[("kpx", par, hh), vtm_key], writes=[("ps", tbk)])
                    ecol = el[:, hh * 16 + c:hh * 16 + c + 1]
                    P.op("act", lambda e, Sf=Sf, ecol=ecol: e.activation(out=T["se"][:], in_=Sf, func=AF.Copy, scale=ecol),
                         reads=[skf, ("el", par)], writes=["se"])
                    P.op("dve", lambda e, ecol=ecol, Sf=Sf, tbk=tbk: e.scalar_tensor_tensor(out=Sf, in0=pb[tbk][:, 0:256], scalar=ecol, in1=T["se"][:],
                                                                                          op0=ALU.mult, op1=ALU.add),
                         reads=[("ps", tbk), ("el", par), "se"], writes=[skf])
                    if sample:
                        P.dma("sp", lambda e, c=c, h=h, Sf=Sf: e.dma_start(out=gs_out[c, h, :, :], in_=Sf), s_mo[i], reads=[skf])
                    else:
                        P.op("act", lambda e, Sf=Sf, Sb=Sb: e.activation(out=Sb, in_=Sf, func=AF.Copy), reads=[skf], writes=[skb])
                ssq = small[:, 8 + hh:9 + hh]
                P.op("act", lambda e, ob=ob, ssq=ssq: e.activation(out=junk[:, 0:256], in_=ob, func=AF.Square, accum_out=ssq),
                     reads=[("ps", obk)], writes=["junk", ("small", 8 + hh)])
                P.op("act", lambda e, ssq=ssq: e.activation(out=ssq, in_=ssq, func=AF.Ln, bias=epst[:, 0:1], scale=1.0 / 256),
                     reads=[("small", 8 + hh), "epst"], writes=[("small", 8 + hh)])
                P.op("act", lambda e, ssq=ssq: e.activation(out=ssq, in_=ssq, func=AF.Exp, scale=-0.5),
                     reads=[("small", 8 + hh)], writes=[("small", 8 + hh)])
                P.op("dve", lambda e, hh=hh, ob=ob, ssq=ssq: e.scalar_tensor_tensor(
                    out=mixed_dst[:, hh * 256:(hh + 1) * 256], in0=ob, scalar=ssq, in1=sgg[:, hh * 256:(hh + 1) * 256], op0=ALU.mult, op1=ALU.mult),
                    reads=[("ps", obk), ("small", 8 + hh), sgg_key], writes=[mixed_key])

        def gla_tiles(st):
            sbq = lambda name, shape, dt: st.enter_context(nc.sbuf_tensor(name, shape, dt, align_bytes=64))
            T = {}
            T["lf"] = sbq("g_lf", [128, 256], F32)
            T["lfb"] = sbq("g_lfb", [128, 256], BF16)
            T["einv"] = sbq("g_einv", [128, 256], F32)
            T["kp"] = sbq("g_kp", [128, 256], BF16)
            el0 = sbq("g_el0", [128, 32], F32)
            kpx0 = sbq("g_kpx0", [128, 2, 16, 128], BF16)
            T["se"] = sbq("g_se", [128, 256], F32)
            T["efm"] = sbq("g_efm", [128, 256], F32)
            T["eifm"] = sbq("g_eifm", [128, 256], F32)
            T["qpT"] = sbq("g_qpT", [128, 2, 128], BF16)
            T["kpT"] = sbq("g_kpT", [128, 2, 128], BF16)
            atm0 = sbq("g_atm0", [128, 2, 128], BF16)
            qpx0 = sbq("g_qpx0", [128, 2, 16, 128], BF16)
            T["s0f"] = [sbq("g_s0f%d" % i, [128, 256], F32) for i in range(4)]
            T["s0b"] = [sbq("g_s0b%d" % i, [128, 256], BF16) for i in range(4)]
            T["el"] = [el0, sbq("g_el1", [128, 32], F32)]
            T["kpx"] = [kpx0, sbq("g_kpx1", [128, 2, 2, 128], BF16)]
            T["qpx"] = [qpx0, sbq("g_qpx1", [128, 2, 2, 128], BF16)]
            T["atm"] = [atm0, sbq("g_atm1", [128, 2, 128], BF16)]
            return T

        st12 = ExitStack()
        npre_b = sb(st12, "npre_b", [128, D], F32)
        xs = [sb(st12, "xs%d" % i, [128, D], F32) for i in range(2)]
        xb = [sb(st12, "xb%d" % i, [128, D], BF16) for i in range(2)]
        junk = sb(st12, "junk", [128, D], BF16)
        P.dma("sp", lambda e: e.dma_start(out=npre_b[:], in_=bcast_rows(npre, D)), s_c, writes=["npre_b"])
        with ExitStack() as st:
            wst = sb(st, "wst", [128, 16, 1552], BF16)
            xTb = [sb(st, "xTb%d" % i, [128, 16, 128], BF16) for i in range(2)]
            vtm = [sb(st, "p_vtm%d" % i, [128, 1024], BF16) for i in range(2)]
            lf = [sb(st, "p_lf%d" % i, [128, 512], F32) for i in range(2)]
            lfb = [sb(st, "p_lfb%d" % i, [128, 512], BF16) for i in range(2)]
            zTp = [sb(st, "p_zT%d" % i, [17, 128], BF16) for i in range(2)]
            ew = sb(st, "p_ew", [128, 512], F32)
            kpp = sb(st, "p_kpp", [128, 512], BF16)
            el = sb(st, "p_el", [128, 8], F32)
            US = cstb[:, C_US:C_US + 128]
            ONE = cstb[:, C_ONE:C_ONE + 2]
            for i in range(2):
                P.op("pool", lambda e, i=i: e.memset(zTp[i][:], 1.0), writes=[("zTp", i)])
            for (c0, n, off, key, sem) in ((Z_OFF, 16, 1536, ("wst", "z"), s_cp), (KG_OFF, 512, 0, ("wst", "k"), s_w[0]),
                                           (VG_OFF, 512, 512, ("wst", "v0"), s_w[1]), (VG_OFF + 512, 512, 1024, ("wst", "v1"), s_w[0])):
                P.dma("pool", lambda e, c0=c0, n=n, off=off: e.dma_start(
                    out=wst[:, :, off:off + n], in_=w_in[:, c0:c0 + n].rearrange("(kc p) n -> p kc n", p=128)), sem, writes=[key])
            NPS = 0 if 'only5' in _SKIP else NPRE

            def xt_of(slot):
                if slot == NPRE - 1:
                    return xT_main[:, :, 0:128], ("xT", 0)
                return xTb[slot % 2][:], ("xTb", slot % 2)

            def stAB(slot):
                par = slot % 2
                xt, xkey = xt_of(slot)
                xprep(slot, xt, xkey, par)
                mm16(pb[par][:, :], ("ps", par), lambda kc: xt[:, kc, :], lambda kc: wst[:, kc, 0:512], [xkey, ("wst", "k")])
                mm16(pb[4][0:16, 0:128], ("ps", 4, "z"), lambda kc: wst[:, kc, 1536:1552], lambda kc: xt[:, kc, :], [xkey, ("wst", "z")])
                copy_evac(zTp[par][0:16, :], pb[4][0:16, 0:128], [("ps", 4, "z")], [("zTp", par)])
                P.op("pe", lambda e: e.matmul(pb[3][:, :], lhsT=zTp[par][0:17, :], rhs=wa2a[0:17, :], start=True, stop=True),
                     reads=[("zTp", par), "wa2a"], writes=[("ps", 3)])

            def stC(slot):
                par = slot % 2
                for hf in range(2):
                    P.op("act", lambda e, hf=hf: e.activation(out=lf[par][:, hf * 256:(hf + 1) * 256], in_=pb[3][:, hf * 256:(hf + 1) * 256], func=AF.Exp, scale=-1.0),
                         reads=[("ps", 3)], writes=[("lf", par, hf)])
                P.op("act", lambda e: e.activation(out=lf[par][:], in_=lf[par][:], func=AF.Ln, bias=1.0), reads=[("lf", par)], writes=[("lf", par)])
                P.op("dve", lambda e: e.tensor_scalar(out=lfb[par][:], in0=lf[par][:], scalar1=NVC(slot), scalar2=None, op0=ALU.mult),
                     reads=[("lf", par)], writes=[("lfb", par)])

            def stD(slot):
                par = slot % 2
                xt, xkey = xt_of(slot)
                for pi in range(2):
                    mm16(pb[6][:, :], ("ps", 6), lambda kc: xt[:, kc, :], lambda kc, pi=pi: wst[:, kc, 512 + pi * 512:1024 + pi * 512], [xkey, ("wst", "v%d" % pi)])
                    copy_evac(vtm[par][:, pi * 512:(pi + 1) * 512], pb[6][:, :], [("ps", 6)], [("vtm", par, pi)])

            def stE(slot):
                par = slot % 2
                P.op("pe", lambda e: e.matmul(pb[3][:, :], lhsT=US, rhs=lfb[par][:], start=True, stop=True),
                     reads=[("lfb", par), "cstb"], writes=[("ps", 3)])
                for h in range(4):
                    P.op("pe", lambda e, h=h: e.matmul(pb[4][:, 200 + 2 * h:202 + 2 * h], lhsT=lfb[par][:, h * 128:(h + 1) * 128], rhs=ONE, start=True, stop=True),
                         reads=[("lfb", par), "cstb"], writes=[("ps", 4, "bl")], sig=(h == 3))
                for hf in range(2):
                    P.op("act", lambda e, hf=hf: e.activation(out=ew[:, hf * 256:(hf + 1) * 256], in_=pb[3][:, hf * 256:(hf + 1) * 256], func=AF.Exp),
                         reads=[("ps", 3)], writes=[("ew", hf)])
                P.op("act", lambda e: e.activation(out=el[:], in_=pb[4][:, 200:208], func=AF.Exp), reads=[("ps", 4, "bl")], writes=["el"])
                P.op("dve", lambda e: e.tensor_tensor(out=kpp[:], in0=pb[par][:, :], in1=ew[:], op=ALU.mult),
                     reads=[("ps", par), "ew"], writes=["kpp"])

            def stF(slot):
                par = slot % 2
                for half in range(2):
                    bnk = 5 if half == 0 else 7
                    for hh in range(2):
                        h = half * 2 + hh
                        P.op("pe", lambda e, h=h, hh=hh, bnk=bnk: e.matmul(pb[bnk][:, hh * 256:(hh + 1) * 256], lhsT=kpp[:, h * 128:(h + 1) * 128],
                                                                         rhs=vtm[par][:, h * 256:(h + 1) * 256], start=True, stop=True),
                             reads=["kpp", ("vtm", par)], writes=[("ps", bnk, hh)])
                    for hh in range(2):
                        h = half * 2 + hh
                        P.op("dve", lambda e, h=h, hh=hh, bnk=bnk: e.scalar_tensor_tensor(
                            out=Sst[:, h, :], in0=Sst[:, h, :], scalar=el[:, 2 * h:2 * h + 1], in1=pb[bnk][:, hh * 256:(hh + 1) * 256], op0=ALU.mult, op1=ALU.add),
                            reads=[("S", h), "el", ("ps", bnk, hh)], writes=[("S", h)])

            if NPS:
                stAB(0); stC(0); stD(0)
            for slot in range(NPS):
                if slot + 1 < NPS:
                    stAB(slot + 1); stC(slot + 1)
                if "pxE" not in _SKIP:
                    stE(slot)
                if slot + 1 < NPS:
                    stD(slot + 1)
                if "pxE" not in _SKIP and "pxF" not in _SKIP:
                    stF(slot)
            P.op("act", lambda e: e.activation(out=Sbf[:], in_=Sst[:], func=AF.Copy), reads=["S"], writes=["Sbf"])
            P.flush()
        if _STOP == 1:
            st12.close()
            return nc

        for mb in range(1, 1 if 'only5' in _SKIP else NMB):
            xprep(NPRE - 1 + mb, xT_main[:, :, mb * 128:(mb + 1) * 128], ("xT", mb), mb % 2)
        P.flush()
        st12.close()
        if _STOP == 2:
            return nc

        TOKG = [(128, 512), (640, 512), (1152, 128)]

        def fproj(wsl, wkey, lhs_fn, M, tok_groups, consumer):
            for (t0, n) in tok_groups:
                b = nbank()
                mm16(pb[b][0:M, 0:n], ("ps", b), lhs_fn, lambda kc, t0=t0, n=n: xT_main[:, kc, t0:t0 + n], ["xT", wkey])
                consumer(pb[b][0:M, 0:n], ("ps", b), t0, n)

        def tproj(wsl, wkey, c0, ncols, blocks, consumer):
            for mb in blocks:
                b = nbank()
                mm16(pb[b][:, 0:ncols], ("ps", b), lambda kc, mb=mb: xT_main[:, kc, mb * 128:(mb + 1) * 128],
                     lambda kc: wsl[:, kc, c0:c0 + ncols], [("xT", mb), wkey])
                consumer(pb[b][:, 0:ncols], ("ps", b), mb)

        cnt["ev"] = 100
        with ExitStack() as st:
            wsl = [sb(st, "wsl%d" % i, [128, 16, 512], BF16) for i in range(2)]
            QT = sb(st, "QT", [128, 4, NMB * 128], BF16)
            KT = sb(st, "KT", [128, NMB * 128], BF16)
            Va = sb(st, "Va", [128, NMB, 2, 80], BF16)
            sga = sb(st, "sga", [128, NMB, 512], BF16)
            KcT = sb(st, "KcT", [128, 16, 128], BF16)
            Vc = sb(st, "Vc", [128, 16, 2, 80], BF16)
            kcs = sb(st, "kcs", [128, 16, 128], BF16)
            stt = [sb(st, "stt%d" % i, [128, 512], F32) for i in range(2)]
            ptt = [sb(st, "ptt%d" % i, [128, 512], BF16) for i in range(4)]
            oTs = [sb(st, "oTs%d" % i, [65, 512], F32) for i in range(2)]
            zcol = sb(st, "zcol", [128, 1], F32)
            BTp = sb(st, "BTp", [128, 16, 128], F32)
            BTc = sb(st, "BTc", [128, 16, 128], F32)
            BTs = sb(st, "BTs", [128, 16, 16, 8], F32)
            P.op("pool", lambda e: e.memset(BTs[:], NEG), writes=["BTs"])
            if "p3bt" not in _SKIP:
                P.dma("sp", lambda e: e.dma_start(out=BTc[:], in_=bass.AP(fscr.tensor, 127, [[382, 128], [128 * 383, 16], [1, 128]])),
                      s_c, writes=["BTc"])
                P.dma("sp", lambda e: e.dma_start(out=BTp[:], in_=bass.AP(fscr.tensor, 255, [[382, 128], [128 * 383, 16], [1, 128]])),
                      s_c, writes=["BTp"])
                for j in range(16):
                    P.dma("sp", lambda e, j=j: e.dma_start(
                        out=BTs[8 * j:8 * j + 8, j, :, :],
                        in_=bass.AP(fscr.tensor, 127, [[382, 8], [128 * 383, 16], [1, 8]])), s_c, writes=["BTs"])
            P.op("pool", lambda e: e.memset(Va[:], 1.0), writes=["Va"])
            P.op("pool", lambda e: e.memset(Vc[:], 1.0), writes=["Vc"])
            P.op("pool", lambda e: e.memset(zcol[:], 0.0), writes=["zcol"])
            wi = 0
            for p in range(0 if 'only5' in _SKIP else 2):
                w, wk, sem = wsl[wi % 2], ("wsl", wi % 2), s_w[wi % 2]; wi += 1
                for g_ in range(4):
                    for kvl_ in range(2):
                        c0_ = Q_OFF + (2 * p + kvl_) * 256 + g_ * 64
                        off_ = g_ * 128 + kvl_ * 64
                        P.dma("pool", lambda e, c0_=c0_, off_=off_, w=w: e.dma_start(
                            out=w[:, :, off_:off_ + 64], in_=w_in[:, c0_:c0_ + 64].rearrange("(kc p) n -> p kc n", p=128)),
                            (sem if g_ % 2 == 0 else s_cp), writes=[wk + (g_,)])
                for g in range(4):
                    if "p3q" in _SKIP:
                        continue
                    lhs = lambda kc, w=w, g=g: w[:, kc, g * 128:(g + 1) * 128]
                    fproj(w, wk + (g,), lhs, 128, TOKG,
                          lambda ps, pk, t0, n, g=g: copy_evac(QT[:, g, t0:t0 + n], ps, [pk], [("QT", g)], scale=0.125))
                w, wk, sem = wsl[wi % 2], ("wsl", wi % 2), s_w[wi % 2]; wi += 1
                wload(w, wk, [(K_OFF + p * 128, 128), (V_OFF + p * 128, 128)], sem)
                if "p3k" not in _SKIP:
                    fproj(w, wk, lambda kc, w=w: w[:, kc, 0:128], 128, [(0, 512), (512, 512), (1024, 256)],
                          lambda ps, pk, t0, n: copy_evac(KT[:, t0:t0 + n], ps, [pk], ["KT"]))

                def kv_cons(ps, pk, mb, p=p):
                    if "p3va" not in _SKIP:
                        for a_ in range(2):
                            copy_evac(Va[:, mb, a_, 0:64], ps[:, 128 + a_ * 64:128 + (a_ + 1) * 64], [pk], [("Va", mb, a_)])
                    if mb >= 8 and "p3ko" not in _SKIP:
                        P.op("dve", lambda e: e.tensor_copy(out=kout[:, mb - 8, p * 128:(p + 1) * 128], in_=ps[:, 0:128]), reads=[pk], writes=["kout"])
                        P.op("dve", lambda e: e.tensor_copy(out=vout[:, mb - 8, p * 128:(p + 1) * 128], in_=ps[:, 128:256]), reads=[pk], writes=["vout"])
                if "p3kv" not in _SKIP:
                    tproj(w, wk, 0, 256, range(NMB), kv_cons)
                w, wk, sem = wsl[wi % 2], ("wsl", wi % 2), s_w[wi % 2]; wi += 1
                wload(w, wk, [(GA_OFF + p * 512, 512)], sem)
                if "p3g" not in _SKIP:
                    tproj(w, wk, 0, 512, range(1, NMB),
                          lambda ps, pk, mb: P.op("act", lambda e: e.activation(out=sga[:, mb, :], in_=ps, func=AF.Silu), reads=[pk], writes=[("sga", mb)]))
                if "p3cache" in _SKIP:
                    continue
                P.dma("pool", lambda e, p=p: e.dma_start(out=kcs[:], in_=ck[:, :, p * 128:(p + 1) * 128].rearrange("j s c -> s j c")), s_cp, writes=["kcs"])
                for a_ in range(2):
                    P.dma("pool", lambda e, p=p, a_=a_: e.dma_start(
                        out=Vc[:, :, a_, 0:64], in_=cv[:, :, p * 128 + a_ * 64:p * 128 + (a_ + 1) * 64].rearrange("j s d -> s j d")),
                        s_cp, writes=["Vc"])
                for jj in range(2):
                    for j8 in range(8):
                        j = jj * 8 + j8
                        P.op("pe", lambda e, j=j, j8=j8: e.transpose(out=pbb[2][:, j8 * 128:(j8 + 1) * 128], in_=kcs[:, j, :], identity=ident_b),
                             reads=["kcs", "cstb"], writes=[("ps", 2)], sig=(j8 == 7))
                    copy_evac(KcT[:, jj * 8:(jj + 1) * 8, :], pbb[2][:, :].rearrange("p (a b) -> p a b", b=128), [("ps", 2)], ["KcT"])

                def unit_s1(kvl, kprev, kprev_key, kcur, qfn, nq, bprev, bcur, hmcol, vprev, vprev_key, vcur, ocol0, first, last, ui):
                    pr = slice(kvl * 64, kvl * 64 + 64)
                    for kb, (kt, kk, bias) in enumerate(((kprev, kprev_key, bprev), (kcur, "KT", bcur))):
                        sbk = 3 + kb
                        spsum = pb[sbk][:, 0:4 * nq].rearrange("p (g q) -> p g q", g=4)
                        for g in range(4):
                            P.op("pe", lambda e, g=g, kt=kt, spsum=spsum: e.matmul(spsum[:, g, :], lhsT=kt[pr, :], rhs=qfn(g)[pr, :], start=True, stop=True),
                                 reads=[kk, ("QT", g)], writes=[("ps", sbk)], sig=(g == 3))
                        stv = stt[kb][:, 0:4 * nq].rearrange("p (g q) -> p g q", g=4)
                        P.op("dve", lambda e, spsum=spsum, bias=bias, stv=stv, kb=kb: e.scalar_tensor_tensor(
                            out=stv, in0=spsum, scalar=(hmcol if kb == 0 else zcol[:, 0:1]), in1=bias, op0=ALU.add, op1=ALU.add),
                            reads=[("ps", sbk), "BTp", "BTc", "BTs", "cstf", "zcol"], writes=[("stt", kb)])
                        pi = (ui % 2) * 2 + kb
                        P.op("act", lambda e, pi=pi, kb=kb: e.activation(out=ptt[pi][:, 0:4 * nq], in_=stt[kb][:, 0:4 * nq], func=AF.Exp),
                             reads=[("stt", kb)], writes=[("ptt", pi)])

                def unit_s2(kvl, kprev, kprev_key, kcur, qfn, nq, bprev, bcur, hmcol, vprev, vprev_key, vcur, ocol0, first, last, ui):
                    for kb, (vv, vk) in enumerate(((vprev, vprev_key), (vcur, "Va"))):
                        pi = (ui % 2) * 2 + kb
                        oap = pb[5][0:65, ocol0:ocol0 + 4 * nq]
                        P.op("pe", lambda e, vv=vv, pi=pi, oap=oap, kb=kb: e.matmul(
                            oap, lhsT=vv, rhs=ptt[pi][:, 0:4 * nq], start=(kb == 0), stop=(kb == 1)),
                            reads=[vk, ("ptt", pi)], writes=[("ps", 5)])

                def fin_a(kvl, mb, fi):
                    ot = oTs[fi % 2]
                    if mb == 9:
                        P.op("act", lambda e: e.activation(out=ot[:, :].rearrange("p (g j t) -> p g j t", g=4, j=16),
                                                           in_=pb[5][0:65, :].rearrange("p (j g t) -> p g j t", j=16, g=4), func=AF.Copy),
                             reads=[("ps", 5)], writes=[("oTs", fi % 2)])
                    else:
                        P.op("act", lambda e: e.activation(out=ot[:], in_=pb[5][0:65, :], func=AF.Copy), reads=[("ps", 5)], writes=[("oTs", fi % 2)])

                def fin_b(kvl, mb, fi, p=p):
                    ot = oTs[fi % 2]
                    for g in range(4):
                        P.op("pe", lambda e, g=g: e.transpose(out=pb[6][:, g * 65:(g + 1) * 65], in_=ot[0:65, g * 128:(g + 1) * 128],
                                                             identity=ident_f[0:65, 0:65]),
                             reads=[("oTs", fi % 2), "cstf"], writes=[("ps", 6)], sig=(g == 3))
                    for g in range(4):
                        h = (2 * p + kvl) * 4 + g
                        rc = small[:, 16 + g:17 + g]
                        P.op("dve", lambda e, g=g, h=h, rc=rc: e.tensor_tensor(out=rc, in0=pb[6][:, g * 65 + 64:g * 65 + 65], in1=esink[:, h:h + 1], op=ALU.add),
                             reads=[("ps", 6), "esink"], writes=[("small", 16 + g)])
                        P.op("dve", lambda e, rc=rc: e.reciprocal(out=rc, in_=rc), reads=[("small", 16 + g)], writes=[("small", 16 + g)])
                        P.op("dve", lambda e, g=g, h=h, rc=rc: e.scalar_tensor_tensor(
                            out=mixed[:, mb, h * 64:(h + 1) * 64], in0=pb[6][:, g * 65:g * 65 + 64], scalar=rc,
                            in1=sga[:, mb, kvl * 256 + g * 64:kvl * 256 + (g + 1) * 64], op0=ALU.mult, op1=ALU.mult),
                            reads=[("ps", 6), ("small", 16 + g), ("sga", mb)], writes=[("mixed", mb)])

                def samp_s1(kvl, kv, ui):
                    pr = slice(kvl * 64, kvl * 64 + 64)
                    for kb in range(2):
                        sbk = 3 + kb
                        for j in range(16):
                            kt = KcT[:, j, :] if kb == 0 else KT[:, 9 * 128:10 * 128]
                            for g in range(4):
                                P.op("pe", lambda e, g=g, j=j, kt=kt, sbk=sbk: e.matmul(
                                    pb[sbk][:, j * 32 + g * 8:j * 32 + g * 8 + 8], lhsT=kt[pr, :],
                                    rhs=QT[pr, g, 9 * 128 + 8 * j:9 * 128 + 8 * j + 8], start=True, stop=True),
                                    reads=["KcT" if kb == 0 else "KT", ("QT", g)], writes=[("ps", sbk)], sig=(j == 15 and g == 3))
                        for j in range(16):
                            bias = BTp[:, kv * 4:kv * 4 + 4, 0:8] if kb == 0 else BTs[:, j, kv * 4:kv * 4 + 4, :]
                            P.op("dve", lambda e, j=j, bias=bias, sbk=sbk, kb=kb: e.scalar_tensor_tensor(
                                out=stt[kb][:, j * 32:(j + 1) * 32].rearrange("p (g q) -> p g q", g=4),
                                in0=pb[sbk][:, j * 32:(j + 1) * 32].rearrange("p (g q) -> p g q", g=4),
                                scalar=zcol[:, 0:1], in1=bias, op0=ALU.add, op1=ALU.add),
                                reads=[("ps", sbk), "BTp", "BTc", "BTs", "zcol"], writes=[("stt", kb)])
                        pi = (ui % 2) * 2 + kb
                        P.op("act", lambda e, pi=pi, kb=kb: e.activation(out=ptt[pi][:, :], in_=stt[kb][:, :], func=AF.Exp),
                             reads=[("stt", kb)], writes=[("ptt", pi)])

                def samp_s2(kvl, kv, ui):
                    for j in range(16):
                        for kb in range(2):
                            pi = (ui % 2) * 2 + kb
                            vv = Vc[:, j, kvl, 0:65] if kb == 0 else Va[:, 9, kvl, 0:65]
                            P.op("pe", lambda e, vv=vv, pi=pi, j=j, kb=kb: e.matmul(
                                pb[5][0:65, j * 32:(j + 1) * 32], lhsT=vv, rhs=ptt[pi][:, j * 32:(j + 1) * 32], start=(kb == 0), stop=(kb == 1)),
                                reads=["Vc" if kb == 0 else "Va", ("ptt", pi)], writes=[("ps", 5)], sig=(j == 15 and kb == 1))

                units = []
                ui = 0
                for kvl in range(2):
                    if "p3units" in _SKIP:
                        continue
                    kv = 2 * p + kvl
                    for mb in range(1, 9):
                        ua = (kvl, KT[:, (mb - 1) * 128:mb * 128], "KT", KT[:, mb * 128:(mb + 1) * 128],
                              (lambda g, mb=mb: QT[:, g, mb * 128:(mb + 1) * 128]), 128,
                              BTp[:, kv * 4:kv * 4 + 4, :], BTc[:, kv * 4:kv * 4 + 4, :],
                              (HMC if mb == 1 else zcol[:, 0:1]),
                              Va[:, mb - 1, kvl, 0:65], "Va", Va[:, mb, kvl, 0:65], 0, True, True, ui)
                        units.append((lambda ua=ua: unit_s1(*ua), lambda ua=ua: unit_s2(*ua), (kvl, mb)))
                        ui += 1
                    units.append((lambda kvl=kvl, kv=kv, ui=ui: samp_s1(kvl, kv, ui), lambda kvl=kvl, kv=kv, ui=ui: samp_s2(kvl, kv, ui), (kvl, 9)))
                    ui += 1
                if units:
                    units[0][0]()
                for i_, (f1, f2, fin) in enumerate(units):
                    if i_ + 1 < len(units):
                        units[i_ + 1][0]()
                    f2()
                    fin_a(fin[0], fin[1], i_)
                    if i_ >= 1:
                        fin_b(units[i_ - 1][2][0], units[i_ - 1][2][1], i_ - 1)
                if units:
                    fin_b(units[-1][2][0], units[-1][2][1], len(units) - 1)
            if "p3out" in _SKIP:
                P.flush()
                return nc
            P.dma("sp", lambda e: e.dma_start(out=kwin[:, :], in_=kout[:, 0, :]), s_o, reads=["kout"])
            P.dma("sp", lambda e: e.dma_start(out=vwin[:, :], in_=vout[:, 0, :]), s_o, reads=["vout"])
            P.dma("sp", lambda e: e.dma_start(out=ks_out[:, 0:120, :], in_=ck[:, 8:128, :]), s_o)
            P.dma("sp", lambda e: e.dma_start(out=vs_out[:, 0:120, :], in_=cv[:, 8:128, :]), s_o)
            for j in range(16):
                P.dma("sp", lambda e, j=j: e.dma_start(out=ks_out[j, 120:128, :], in_=kout[8 * j:8 * j + 8, 1, :]), s_o, reads=["kout"])
                P.dma("sp", lambda e, j=j: e.dma_start(out=vs_out[j, 120:128, :], in_=vout[8 * j:8 * j + 8, 1, :]), s_o, reads=["vout"])
            P.flush()
            if _STOP == 3:
                return nc

        cnt["ev"] = 174
        with ExitStack() as st:
            wsl = [sb(st, "wslg%d" % i, [128, 16, 512], BF16) for i in range(2)]
            qgT = sb(st, "qgT", [128, 2, NMB * 128], BF16)
            kgT = sb(st, "kgT", [128, 2, NMB * 128], BF16)
            kgt = sb(st, "kgt", [128, NMB, 256], BF16)
            vgt = sb(st, "vgt", [128, NMB, 512], BF16)
            sgg = sb(st, "sgg", [128, NMB, 512], BF16)
            sgtmp = sb(st, "sgtmp", [128, 512], F32)
            gn_b = sb(st, "gn_b", [128, 512], F32)
            junk = sb(st, "junk4", [128, 256], BF16)
            P.dma("sp", lambda e: e.dma_start(out=gn_b[:, 0:256], in_=bcast_rows(gnorm, 256)), s_c, writes=["gn_b"])
            P.dma("sp", lambda e: e.dma_start(out=gn_b[:, 256:512], in_=bcast_rows(gnorm, 256)), s_c, writes=["gn_b"])
            T = gla_tiles(st)
            wz = sb(st, "wz", [128, 16, 16], BF16)
            wload(wz, "wz", [(Z_OFF, 16)], s_w[0])
            fproj(wz, "wz", lambda kc: wz[:, kc, :], 16, TOKG,
                  lambda ps, pk, t0, n: copy_evac(zT[0:16, t0:t0 + n], ps, [pk], ["zT"]))
            wi = 0
            for hp in range(0 if 'only5' in _SKIP else 2):
                w, wk, sem = wsl[wi % 2], ("wslg", wi % 2), s_w[wi % 2]; wi += 1
                for (c0_, off_, key_, sem_) in ((QG_OFF + hp * 256, 0, wk + ("q",), sem), (KG_OFF + hp * 256, 256, wk + ("k",), s_cp)):
                    P.dma("pool", lambda e, c0_=c0_, off_=off_, w=w: e.dma_start(
                        out=w[:, :, off_:off_ + 256], in_=w_in[:, c0_:c0_ + 256].rearrange("(kc p) n -> p kc n", p=128)), sem_, writes=[key_])
                for hh in range(2):
                    fproj(w, wk + ("q",), lambda kc, w=w, hh=hh: w[:, kc, hh * 128:(hh + 1) * 128], 128, TOKG,
                          lambda ps, pk, t0, n, hh=hh: copy_evac(qgT[:, hh, t0:t0 + n], ps, [pk], ["qgT"]))
                for hh in range(2):
                    fproj(w, wk + ("k",), lambda kc, w=w, hh=hh: w[:, kc, 256 + hh * 128:256 + (hh + 1) * 128], 128, TOKG,
                          lambda ps, pk, t0, n, hh=hh: copy_evac(kgT[:, hh, t0:t0 + n], ps, [pk], ["kgT"]))
                tproj(w, wk + ("k",), 256, 256, range(1, NMB), lambda ps, pk, mb: copy_evac(kgt[:, mb, :], ps, [pk], [("kgt", mb)]))
                w, wk, sem = wsl[wi % 2], ("wslg", wi % 2), s_w[wi % 2]; wi += 1
                wload(w, wk, [(VG_OFF + hp * 512, 512)], sem)
                tproj(w, wk, 0, 512, range(1, NMB), lambda ps, pk, mb: copy_evac(vgt[:, mb, :], ps, [pk], [("vgt", mb)]))
                w, wk, sem = wsl[wi % 2], ("wslg", wi % 2), s_w[wi % 2]; wi += 1
                wload(w, wk, [(GG_OFF + hp * 512, 512)], sem)

                def gg_cons(ps, pk, mb):
                    P.op("act", lambda e: e.activation(out=sgtmp[:], in_=ps, func=AF.Silu), reads=[pk], writes=["sgtmp"])
                    P.op("dve", lambda e: e.tensor_tensor(out=sgg[:, mb, :], in0=sgtmp[:], in1=gn_b[:], op=ALU.mult),
                         reads=["sgtmp", "gn_b"], writes=[("sgg", mb)])
                tproj(w, wk, 0, 512, range(1, NMB), gg_cons)
                def gb(mb, stage, par):
                    samp = (mb == 9)
                    gla_block(T, (2 * hp, 2 * hp + 1), zT[0:17, mb * 128:(mb + 1) * 128], "zT", kgt[:, mb, :], ("kgt", mb),
                              vgt[:, mb, :], ("vgt", mb), NVC(NPRE - 1 + mb), 8 if samp else 64,
                              [qgT[:, hh, mb * 128:(mb + 1) * 128] for hh in range(2)],
                              [kgT[:, hh, mb * 128:(mb + 1) * 128] for hh in range(2)], ("qgT", "kgT"),
                              sgg[:, mb, :], ("sgg", mb),
                              mixed[:, mb, 1024 + hp * 512:1024 + (hp + 1) * 512], ("mixed", mb), sample=samp, stage=stage, par=par)
                if "gseq" in _SKIP:
                    for mb in range(1, 9):
                        gb(mb, 1, mb % 2)
                        gb(mb, 2, mb % 2)
                else:
                    gb(1, 1, 1)
                    for mb in range(1, 9):
                        if mb + 1 < 9:
                            gb(mb + 1, 1, (mb + 1) % 2)
                        gb(mb, 2, mb % 2)
                gb(9, "all", 0)
            for h in range(4):
                P.dma("sp", lambda e, h=h: e.dma_start(out=glast[h, :, :], in_=Sst[:, h, :]), s_o, reads=[("S", h)])
            P.flush()
            if _STOP == 4:
                return nc

        cnt["ev"] = 237
        with ExitStack() as st:
            wo = sb(st, "wo", [128, 16, D], BF16)
            npost_b = sb(st, "npost_b", [128, D], F32)
            mT = sb(st, "mT", [128, 16, 128], BF16)
            osb = sb(st, "osb", [128, D], F32)
            xs = [sb(st, "xs5_%d" % i, [128, D], F32) for i in range(2)]
            junk = sb(st, "junk5", [128, D], BF16)
            for i in range(4):
                P.dma("pool", lambda e, i=i: e.dma_start(out=wo[:, :, i * 512:(i + 1) * 512],
                                                       in_=w_out[:, i * 512:(i + 1) * 512].rearrange("(kc p) n -> p kc n", p=128)),
                      s_w[i % 2], writes=[("wo", i)])
            P.dma("sp", lambda e: e.dma_start(out=npost_b[:], in_=bcast_rows(npost, D)), s_c, writes=["npost_b"])
            for mb in range(1, NMB):
                slot = NPRE - 1 + mb
                i = cnt["x"] % 2
                cnt["x"] += 1
                P.dma("sp", lambda e, i=i, slot=slot: e.dma_start(out=xs[i][:], in_=xp[slot * 128:(slot + 1) * 128, :]), s_ld[i], writes=[("xs", i)])
                for half in range(2):
                    for j in range(8):
                        kc = half * 8 + j
                        P.op("pe", lambda e, kc=kc, j=j, mb=mb: e.transpose(out=pbb[2][:, j * 128:(j + 1) * 128],
                                                                         in_=mixed[:, mb, kc * 128:(kc + 1) * 128], identity=ident_b),
                             reads=[("mixed", mb), "cstb"], writes=[("ps", 2)], sig=(j == 7))
                    copy_evac(mT[:, half * 8:(half + 1) * 8, :], pbb[2][:, :].rearrange("p (a b) -> p a b", b=128), [("ps", 2)], ["mT"])
                for n in range(4):
                    b = 4 + n
                    mm16(pb[b][:, :], ("ps", b), lambda kc: mT[:, kc, :], lambda kc, n=n: wo[:, kc, n * 512:(n + 1) * 512], ["mT", ("wo", n)])
                    P.op("dve", lambda e, n=n, b=b: e.tensor_copy(out=osb[:, n * 512:(n + 1) * 512], in_=pb[b][:, :]), reads=[("ps", b)], writes=[("osb", n)])
                P.op("act", lambda e: e.activation(out=junk[:], in_=osb[:], func=AF.Square, accum_out=small[:, 30:31]),
                     reads=["osb"], writes=["junk", ("small", 30)])
                P.op("act", lambda e: e.activation(out=small[:, 31:32], in_=small[:, 30:31], func=AF.Ln, bias=epst[:, 0:1], scale=1.0 / D),
                     reads=[("small", 30), "epst"], writes=[("small", 31)])
                P.op("act", lambda e: e.activation(out=small[:, 32:33], in_=small[:, 31:32], func=AF.Exp, scale=-0.5),
                     reads=[("small", 31)], writes=[("small", 32)])
                P.op("dve", lambda e: e.scalar_tensor_tensor(out=osb[:], in0=osb[:], scalar=small[:, 32:33], in1=npost_b[:], op0=ALU.mult, op1=ALU.mult),
                     reads=["osb", ("small", 32), "npost_b"], writes=["osb"])
                P.op("dve" if "p5pool" in _SKIP else "pool", lambda e, i=i: e.tensor_tensor(out=osb[:], in0=osb[:], in1=xs[i][:], op=ALU.add), reads=["osb", ("xs", i)], writes=["osb"])
                if "p5out" in _SKIP:
                    continue
                if mb < 9:
                    P.dma("sp", lambda e, mb=mb: e.dma_start(out=y_main[(mb - 1) * 128:mb * 128, :], in_=osb[:]), s_o, reads=["osb"])
                else:
                    P.dma("sp", lambda e: e.dma_start(out=y_samp[:, :], in_=osb[:]), s_o, reads=["osb"])
            P.flush()
            if _STOP == 5:
                return nc
    return nc


_CACHE = {}


def kernel(x_prompt, x_sample, cache_k_win, cache_v_win, state_gla, meta_tokens, rel_bias,
           norm_pre, norm_post, w_in, w_a2, b_a, attn_sinks, gla_norm, w_out):
    f = lambda a: np.ascontiguousarray(np.asarray(a, dtype=np.float32))
    x_prompt, x_sample = f(x_prompt), f(x_sample)
    ckw, cvw, sgl = f(cache_k_win)[0], f(cache_v_win)[0], f(state_gla)[0]
    meta = f(meta_tokens)
    cbase = make_consts()
    ii = np.arange(128)
    cB = np.zeros((128, NCB), np.float32)
    cB[:, 0:C_OH] = cbase[:, 0:C_OH]
    cB[:, C_US:C_US + 128] = (ii[:, None] > ii[None, :])
    cB[:, C_ONE:C_ONE + 2] = 1.0
    ohm_h = np.ascontiguousarray(cbase[0:33, C_OH:C_OH + 383])
    if "nc" not in _CACHE:
        _CACHE["nc"] = build()
    nc = _CACHE["nc"]
    w_in0, w_out0 = f(w_in)[0], f(w_out)[0]
    in_maps = []
    for c in range(8):
        b, q = c // 4, c % 4
        xpad = np.zeros((33 * 128, D), np.float32)
        xpad[112:128] = meta
        xpad[128:] = x_prompt[b]
        nblk = 8 * q + 9
        xc = np.zeros((NSLOT * 128, D), np.float32)
        xc[(33 - nblk) * 128:33 * 128] = xpad[:nblk * 128]
        xc[33 * 128:] = x_sample[16 * c:16 * c + 16].reshape(128, D)
        valid = np.zeros((NSLOT * 128,), np.float32)
        vpad = np.ones((33 * 128,), np.float32)
        vpad[:112] = 0
        valid[(33 - nblk) * 128:33 * 128] = vpad[:nblk * 128]
        valid[33 * 128:] = 1
        cs = np.zeros((128, 163), np.float32)
        cs[:, 0:128] = cbase[:, C_ID:C_ID + 128]
        cs[:, 128:128 + NSLOT] = (-valid / 16.0).reshape(NSLOT, 128).T
        if q == 0:
            cs[:112, 162] = NEG
        in_maps.append({
            "xp": xc, "w_in": w_in0, "w_out": w_out0,
            "ck": np.ascontiguousarray(ckw[16 * c:16 * c + 16].reshape(16, 128, 256)),
            "cv": np.ascontiguousarray(cvw[16 * c:16 * c + 16].reshape(16, 128, 256)),
            "sg": np.ascontiguousarray(sgl[16 * c:16 * c + 16]),
            "cstS": cs, "cstB": cB, "ohm": ohm_h, "relb": f(rel_bias), "npre": f(norm_pre), "npost": f(norm_post),
            "wa2": f(w_a2)[0], "ba": f(b_a), "sinks": f(attn_sinks), "gnorm": f(gla_norm),
        })
    res = run_bass_kernel_spmd(nc, in_maps, core_ids=list(range(8)))
    R = res.results
    y_prompt = np.zeros((2, 4096, D), np.float32)
    y_sample = np.zeros((128, 8, D), np.float32)
    kwp = np.zeros((1, 2, 128, 4, 64), np.float32)
    vwp = np.zeros((1, 2, 128, 4, 64), np.float32)
    sgp = np.zeros((1, 2, 4, 128, 256), np.float32)
    kws = np.zeros((1, 128, 128, 4, 64), np.float32)
    vws = np.zeros((1, 128, 128, 4, 64), np.float32)
    sgs = np.zeros((1, 128, 4, 128, 256), np.float32)
    for c in range(8):
        b, q = c // 4, c % 4
        r = R[c]
        y_prompt[b, q * 1024:(q + 1) * 1024] = r["y_main"]
        y_sample[16 * c:16 * c + 16] = r["y_samp"].reshape(16, 8, D)
        kws[0, 16 * c:16 * c + 16] = r["ks_out"].reshape(16, 128, 4, 64)
        vws[0, 16 * c:16 * c + 16] = r["vs_out"].reshape(16, 128, 4, 64)
        sgs[0, 16 * c:16 * c + 16] = r["gs_out"]
        if q == 3:
            kwp[0, b] = r["kwin"].reshape(128, 4, 64)
            vwp[0, b] = r["vwin"].reshape(128, 4, 64)
            sgp[0, b] = r["glast"]
    return (y_prompt, y_sample, kwp, vwp, sgp, kws, vws, sgs)
```
